# Optimizing a Trainium2 kernel written in Bass

```python
import math
import jax, jax.numpy as jnp
from jax import lax
import numpy as np

D_MODEL = 2048
BATCH = 2
SEQ = 4096
DEPTH = 4
DEC_BATCH = 8
DEC_SEQ = 2048
PAST_LEN = 128

N_META = 16
S5_WIDTH = 512
S5_GROUP = 16
S5_GROUPS = S5_WIDTH // S5_GROUP
S5_STATE = 64
GLA_HEADS = 4
GLA_DK = 64
GLA_DV = 128
GLA_KEY = GLA_HEADS * GLA_DK
GLA_WIDTH = GLA_HEADS * GLA_DV
GLA_RANK = 16
GLA_GATE_NORM = 16.0
GLA_CHUNK = 64
LRU_WIDTH = 1024
LRU_BLOCKS = 8
LRU_BLOCK = LRU_WIDTH // LRU_BLOCKS
CONV_WIDTH = 4
LRU_C = 8.0
N_BRANCH = 3
EPS = 1e-6
N_IN = 2 * S5_WIDTH + 2 * GLA_KEY + 2 * GLA_WIDTH + 2 * GLA_RANK + 2 * LRU_WIDTH + N_BRANCH * D_MODEL

kernel_name = 'hybrid_s5_gla_rglru_parallel_encoder'

F32 = jnp.float32


def rmsnorm(x, g):
    xf = x.astype(F32)
    return xf * lax.rsqrt(jnp.mean(xf * xf, axis=-1, keepdims=True) + EPS) * g.astype(F32)


def linear_scan(a, b, axis, reverse=False):
    if reverse:
        a = jnp.flip(a, axis)
        b = jnp.flip(b, axis)
    def combine(l, r):
        return r[0] * l[0], r[0] * l[1] + r[1]
    _, h = lax.associative_scan(combine, (a, b), axis=axis)
    if reverse:
        h = jnp.flip(h, axis)
    return h


def complex_linear_scan(a_re, a_im, b_re, b_im, axis, reverse=False):
    elems = (a_re, a_im, b_re, b_im)
    if reverse:
        elems = tuple(jnp.flip(e, axis) for e in elems)
    def combine(l, r):
        alr, ali, blr, bli = l
        arr, ari, brr, bri = r
        return (arr * alr - ari * ali, arr * ali + ari * alr,
                arr * blr - ari * bli + brr, arr * bli + ari * blr + bri)
    _, _, h_re, h_im = lax.associative_scan(combine, elems, axis=axis)
    if reverse:
        h_re = jnp.flip(h_re, axis)
        h_im = jnp.flip(h_im, axis)
    return h_re, h_im


def split_cols(c):
    sizes = (S5_WIDTH, S5_WIDTH, GLA_KEY, GLA_KEY, GLA_WIDTH, GLA_WIDTH, 2 * GLA_RANK,
             LRU_WIDTH, LRU_WIDTH, N_BRANCH * D_MODEL)
    outs = []
    start = 0
    for s in sizes:
        outs.append(c[..., start:start + s])
        start += s
    return outs


def s5_mixer(u, lam_re, lam_im, log_step, b_re, b_im, c_re, c_im, d_skip, w_glu, b_glu):
    bsz, length, _ = u.shape
    ug = u.reshape(bsz, length, S5_GROUPS, S5_GROUP)
    b_re = b_re.astype(F32)
    b_im = b_im.astype(F32)
    y = d_skip.astype(F32) * u
    for d in range(2):
        lr = lam_re[d].astype(F32)
        li = lam_im[d].astype(F32)
        dt = jnp.exp(log_step[d].astype(F32))[:, None]
        mag = jnp.exp(lr * dt)
        abr = mag * jnp.cos(li * dt)
        abi = mag * jnp.sin(li * dt)
        den = lr * lr + li * li
        fr = ((abr - 1.0) * lr + abi * li) / den
        fi = (abi * lr - (abr - 1.0) * li) / den
        bbr = fr[..., None] * b_re - fi[..., None] * b_im
        bbi = fr[..., None] * b_im + fi[..., None] * b_re
        bur = jnp.einsum('gnc,blgc->blgn', bbr, ug)
        bui = jnp.einsum('gnc,blgc->blgn', bbi, ug)
        s_re, s_im = complex_linear_scan(jnp.broadcast_to(abr, bur.shape), jnp.broadcast_to(abi, bur.shape),
                                         bur, bui, axis=1, reverse=(d == 1))
        yg = (jnp.einsum('gcn,blgn->blgc', c_re[d].astype(F32), s_re)
              - jnp.einsum('gcn,blgn->blgc', c_im[d].astype(F32), s_im))
        y = y + yg.reshape(bsz, length, S5_WIDTH)
    z = jax.nn.gelu(y)
    return z * jax.nn.sigmoid(z @ w_glu + b_glu)


def gla_chunked(q, k, v, g):
    bsz, t_len, heads, dk = q.shape
    dv = v.shape[-1]
    n_chunks = t_len // GLA_CHUNK
    rs = lambda t: t.reshape(bsz, n_chunks, GLA_CHUNK, heads, t.shape[-1]).astype(F32)
    q, k, v, g = rs(q), rs(k), rs(v), rs(g)
    b = jnp.cumsum(g, axis=2)
    qe = q * jnp.exp(b)
    ke = k * jnp.exp(-b)
    mask = jnp.tril(jnp.ones((GLA_CHUNK, GLA_CHUNK), dtype=bool))
    att = jnp.where(mask, jnp.einsum('bnihk,bnjhk->bnhij', qe, ke), 0.0)
    o = jnp.einsum('bnhij,bnjhv->bnihv', att, v)
    b_last = b[:, :, -1:]
    ds = jnp.einsum('bnjhk,bnjhv->bnhkv', k * jnp.exp(b_last - b), v)
    decay = jnp.broadcast_to(jnp.exp(b_last[:, :, 0])[..., None], ds.shape)
    s = linear_scan(decay, ds, axis=1)
    s_prev = jnp.concatenate([jnp.zeros_like(s[:, :1]), s[:, :-1]], axis=1)
    o = o + jnp.einsum('bnihk,bnhkv->bnihv', qe, s_prev)
    return o.reshape(bsz, t_len, heads, dv)


def gla_mixer(q, k, v, glr, w_gate_up, b_gate, norm_g):
    bsz, length, _ = q.shape
    q = q.reshape(bsz, length, GLA_HEADS, GLA_DK) * (GLA_DK ** -0.5)
    k = k.reshape(bsz, length, GLA_HEADS, GLA_DK)
    v = v.reshape(bsz, length, GLA_HEADS, GLA_DV)
    pad = (-N_META) % GLA_CHUNK
    padt = lambda t: jnp.pad(t, ((0, 0), (pad, 0), (0, 0), (0, 0)))
    o = 0.0
    for d in range(2):
        g = jax.nn.log_sigmoid(glr[..., d * GLA_RANK:(d + 1) * GLA_RANK] @ w_gate_up[d] + b_gate[d]) / GLA_GATE_NORM
        g = g.reshape(bsz, length, GLA_HEADS, GLA_DK)
        qp, kp, vp, gp = padt(q), padt(k), padt(v), padt(g)
        if d == 1:
            qp, kp, vp, gp = (jnp.flip(t, 1) for t in (qp, kp, vp, gp))
        od = gla_chunked(qp, kp, vp, gp)
        if d == 1:
            od = jnp.flip(od, 1)
        o = o + od[:, pad:]
    o = o * lax.rsqrt(jnp.mean(o * o, axis=-1, keepdims=True) + EPS)
    o = o * norm_g.astype(F32).reshape(GLA_HEADS, GLA_DV)
    return o.reshape(bsz, length, GLA_WIDTH)


def rglru_mixer(x, conv_w, conv_b, w_a, b_a, w_x, b_x, lam):
    bsz, length, _ = x.shape
    left = CONV_WIDTH // 2
    xp = jnp.pad(x, ((0, 0), (left, CONV_WIDTH - 1 - left), (0, 0)))
    xc = conv_b.astype(F32)
    for j in range(CONV_WIDTH):
        xc = xc + xp[:, j:j + length] * conv_w[j]
    xb = xc.reshape(bsz, length, LRU_BLOCKS, LRU_BLOCK)
    h = 0.0
    for d in range(2):
        r = jax.nn.sigmoid(jnp.einsum('blnc,ncd->blnd', xb, w_a[d]).reshape(bsz, length, LRU_WIDTH) + b_a[d])
        i = jax.nn.sigmoid(jnp.einsum('blnc,ncd->blnd', xb, w_x[d]).reshape(bsz, length, LRU_WIDTH) + b_x[d])
        log_a = -LRU_C * r * jax.nn.softplus(-lam[d].astype(F32))
        a = jnp.exp(log_a)
        bt = jnp.sqrt(-jnp.expm1(2.0 * log_a)) * (i * xc)
        h = h + linear_scan(a, bt, axis=1, reverse=(d == 1))
    return h


def mixer_layer(z, norm_g, w_in, s5_lam_re, s5_lam_im, s5_log_step, s5_b_re, s5_b_im, s5_c_re, s5_c_im,
                s5_d, s5_w_glu, s5_b_glu, gla_w_gate_up, gla_b_gate, gla_norm_g, conv_w, conv_b,
                lru_w_a, lru_b_a, lru_w_x, lru_b_x, lru_lam, w_out_a, w_out_b, w_out_c, w_o):
    bsz, length, _ = z.shape
    h = rmsnorm(z, norm_g)
    cols = h @ w_in
    u_a, gate_a, q_b, k_b, v_b, gate_b, glr_b, x_c, gate_c, merge = split_cols(cols)
    y_a = s5_mixer(u_a, s5_lam_re, s5_lam_im, s5_log_step, s5_b_re, s5_b_im, s5_c_re, s5_c_im,
                   s5_d, s5_w_glu, s5_b_glu) * jax.nn.silu(gate_a)
    y_b = gla_mixer(q_b, k_b, v_b, glr_b, gla_w_gate_up, gla_b_gate, gla_norm_g) * jax.nn.silu(gate_b)
    y_c = rglru_mixer(x_c, conv_w, conv_b, lru_w_a, lru_b_a, lru_w_x, lru_b_x, lru_lam) * jax.nn.silu(gate_c)
    gates = jax.nn.sigmoid(merge).reshape(bsz, length, N_BRANCH, D_MODEL)
    m = (gates[:, :, 0] * (y_a @ w_out_a) + gates[:, :, 1] * (y_b @ w_out_b)
         + gates[:, :, 2] * (y_c @ w_out_c))
    return z + m @ w_o


def encoder(x, meta_tokens, layer_params, final_norm_g):
    bsz = x.shape[0]
    meta = jnp.broadcast_to(meta_tokens.astype(F32)[None], (bsz, N_META, D_MODEL))
    z = jnp.concatenate([meta, x.astype(F32)], axis=1)
    for l in range(DEPTH):
        z = mixer_layer(z, *[p[l] for p in layer_params])
    return rmsnorm(z, final_norm_g)[:, N_META:].astype(x.dtype)


def setup_inputs(seed: int = 0) -> dict:
    key = jax.random.key(seed)
    ks = jax.random.split(key, 40)
    nrm = lambda k, shape, scale: jax.random.normal(k, shape, F32) * scale
    lam_im_base = math.pi * jnp.arange(S5_STATE, dtype=F32)
    u_lru = jax.random.uniform(ks[24], (DEPTH, 2, LRU_WIDTH), F32, minval=0.9, maxval=0.999)
    a_base = u_lru ** (1.0 / LRU_C)
    return {
        'x_prompt': nrm(ks[0], (BATCH, SEQ, D_MODEL), 1.0),
        'x_sample': nrm(ks[1], (DEC_BATCH, DEC_SEQ, D_MODEL), 1.0),
        'meta_tokens': nrm(ks[2], (N_META, D_MODEL), 1.0),
        'norm_g': 1.0 + nrm(ks[3], (DEPTH, D_MODEL), 0.01),
        'w_in': nrm(ks[4], (DEPTH, D_MODEL, N_IN), D_MODEL ** -0.5),
        's5_lam_re': -0.5 + nrm(ks[5], (DEPTH, 2, S5_GROUPS, S5_STATE), 0.01),
        's5_lam_im': lam_im_base + nrm(ks[6], (DEPTH, 2, S5_GROUPS, S5_STATE), 0.01),
        's5_log_step': jax.random.uniform(ks[7], (DEPTH, 2, S5_GROUPS), F32,
                                          minval=math.log(1e-3), maxval=math.log(1e-1)),
        's5_b_re': nrm(ks[8], (DEPTH, S5_GROUPS, S5_STATE, S5_GROUP), (2 * S5_GROUP) ** -0.5),
        's5_b_im': nrm(ks[9], (DEPTH, S5_GROUPS, S5_STATE, S5_GROUP), (2 * S5_GROUP) ** -0.5),
        's5_c_re': nrm(ks[10], (DEPTH, 2, S5_GROUPS, S5_GROUP, S5_STATE), (2 * S5_STATE) ** -0.5),
        's5_c_im': nrm(ks[11], (DEPTH, 2, S5_GROUPS, S5_GROUP, S5_STATE), (2 * S5_STATE) ** -0.5),
        's5_d': nrm(ks[12], (DEPTH, S5_WIDTH), 1.0),
        's5_w_glu': nrm(ks[13], (DEPTH, S5_WIDTH, S5_WIDTH), S5_WIDTH ** -0.5),
        's5_b_glu': nrm(ks[14], (DEPTH, S5_WIDTH), 0.01),
        'gla_w_gate_up': nrm(ks[15], (DEPTH, 2, GLA_RANK, GLA_KEY), GLA_RANK ** -0.5),
        'gla_b_gate': nrm(ks[16], (DEPTH, 2, GLA_KEY), 0.01),
        'gla_norm_g': 1.0 + nrm(ks[17], (DEPTH, GLA_WIDTH), 0.01),
        'conv_w': nrm(ks[18], (DEPTH, CONV_WIDTH, LRU_WIDTH), CONV_WIDTH ** -0.5),
        'conv_b': nrm(ks[19], (DEPTH, LRU_WIDTH), 0.01),
        'lru_w_a': nrm(ks[20], (DEPTH, 2, LRU_BLOCKS, LRU_BLOCK, LRU_BLOCK), LRU_BLOCK ** -0.5),
        'lru_b_a': nrm(ks[21], (DEPTH, 2, LRU_WIDTH), 0.01),
        'lru_w_x': nrm(ks[22], (DEPTH, 2, LRU_BLOCKS, LRU_BLOCK, LRU_BLOCK), LRU_BLOCK ** -0.5),
        'lru_b_x': nrm(ks[23], (DEPTH, 2, LRU_WIDTH), 0.01),
        'lru_lam': jnp.log(a_base) - jnp.log1p(-a_base),
        'w_out_a': nrm(ks[25], (DEPTH, S5_WIDTH, D_MODEL), S5_WIDTH ** -0.5),
        'w_out_b': nrm(ks[26], (DEPTH, GLA_WIDTH, D_MODEL), GLA_WIDTH ** -0.5),
        'w_out_c': nrm(ks[27], (DEPTH, LRU_WIDTH, D_MODEL), LRU_WIDTH ** -0.5),
        'w_o': nrm(ks[28], (DEPTH, D_MODEL, D_MODEL), D_MODEL ** -0.5),
        'final_norm_g': 1.0 + nrm(ks[29], (D_MODEL,), 0.01),
    }


def reference(x_prompt, x_sample, meta_tokens, norm_g, w_in, s5_lam_re, s5_lam_im, s5_log_step,
              s5_b_re, s5_b_im, s5_c_re, s5_c_im, s5_d, s5_w_glu, s5_b_glu, gla_w_gate_up, gla_b_gate,
              gla_norm_g, conv_w, conv_b, lru_w_a, lru_b_a, lru_w_x, lru_b_x, lru_lam,
              w_out_a, w_out_b, w_out_c, w_o, final_norm_g):
    layer_params = (norm_g, w_in, s5_lam_re, s5_lam_im, s5_log_step, s5_b_re, s5_b_im, s5_c_re, s5_c_im,
                    s5_d, s5_w_glu, s5_b_glu, gla_w_gate_up, gla_b_gate, gla_norm_g, conv_w, conv_b,
                    lru_w_a, lru_b_a, lru_w_x, lru_b_x, lru_lam, w_out_a, w_out_b, w_out_c, w_o)
    y_prompt = encoder(x_prompt, meta_tokens, layer_params, final_norm_g)
    y_sample = encoder(x_sample, meta_tokens, layer_params, final_norm_g)
    return (y_prompt, y_sample)
```

```python
import math
from contextlib import ExitStack

import numpy as np
import concourse.bass as bass
import concourse.mybir as mybir
from concourse.bass_utils import run_bass_kernel_spmd

F32 = mybir.dt.float32
BF16 = mybir.dt.bfloat16
AF = mybir.ActivationFunctionType
ALU = mybir.AluOpType

D = 2048
KC = 16
N_IN = 10784
N_META = 16
EPS = 1e-6
C_U, C_GA, C_Q, C_K, C_V, C_GB, C_GLR, C_X, C_GC, C_M = 0, 512, 1024, 1280, 1536, 2048, 2560, 2592, 3616, 4640
TWO_PI = 2.0 * math.pi
NDQ = 12


class Sched:
    def __init__(self, nc, es):
        self.nc = nc
        self.engs = ["pe", "act", "dve", "pool", "sp"]
        self.ops = {e: [] for e in self.engs}
        self.semh = {}
        for e in ["pe", "act", "dve", "pool"]:
            self.semh[e] = es.enter_context(nc.semaphore("s_" + e))
        for q in ["sp", "pool", "act"]:
            for i in range(NDQ):
                self.semh[f"d_{q}{i}"] = es.enter_context(nc.semaphore(f"d_{q}{i}"))
        self.latest = {k: 0 for k in self.semh}
        self.dcnt = {"sp": 0, "pool": 0, "act": 0}
        self.seen = {e: {} for e in self.engs}
        self.bufs = {}
        self.nins = 0

    def _deps(self, reads, writes):
        toks = {}

        def add(k, v):
            if toks.get(k, 0) < v:
                toks[k] = v

        for key in reads:
            b = self.bufs.get(key)
            if b and b[0]:
                add(*b[0])
        for key in writes:
            b = self.bufs.get(key)
            if b:
                if b[0]:
                    add(*b[0])
                for k, v in b[1].items():
                    add(k, v)
        return toks

    def _wait(self, eng, toks, skip=None):
        for k, v in toks.items():
            if k == skip:
                continue
            if self.seen[eng].get(k, 0) >= v:
                continue
            self.seen[eng][k] = v
            sem = self.semh[k]
            self.ops[eng].append(lambda e, sem=sem, v=v: e.wait_ge(sem, v))
            self.nins += 1

    def _record(self, tok, reads, writes):
        for key in writes:
            self.bufs[key] = [tok, {}]
        k, v = tok
        for key in reads:
            if key in writes:
                continue
            b = self.bufs.setdefault(key, [None, {}])
            if b[1].get(k, 0) < v:
                b[1][k] = v

    def op(self, eng, fn, reads=(), writes=()):
        toks = self._deps(reads, writes)
        self._wait(eng, toks, skip=("pe" if eng == "pe" else None))
        self.latest[eng] += 1
        v = self.latest[eng]
        sem = self.semh[eng]
        self.ops[eng].append(lambda e, fn=fn, sem=sem: fn(e).then_inc(sem, 1))
        self.nins += 1
        self._record((eng, v), reads, writes)

    def dma(self, q, out, in_, reads=(), writes=()):
        toks = self._deps(reads, writes)
        i = self.dcnt[q]
        self.dcnt[q] += 1
        k = f"d_{q}{i % NDQ}"
        val = (i // NDQ + 1) * 16
        if val > 16:
            toks[k] = max(toks.get(k, 0), val - 16)
        self._wait(q, toks)
        sem = self.semh[k]
        self.ops[q].append(lambda e, out=out, in_=in_, sem=sem: e.dma_start(out=out, in_=in_).then_inc(sem, 16))
        self.nins += 1
        self.latest[k] = val
        self._record((k, val), reads, writes)

    def barrier(self):
        toks = {k: v for k, v in self.latest.items() if v > 0}
        for e in self.engs:
            self._wait(e, dict(toks))

    def replay(self, block):
        ops = self.ops

        @block.tensor
        def _(e):
            for f in ops["pe"]:
                f(e)

        @block.scalar
        def _(e):
            for f in ops["act"]:
                f(e)

        @block.vector
        def _(e):
            for f in ops["dve"]:
                f(e)

        @block.gpsimd
        def _(e):
            for f in ops["pool"]:
                f(e)

        @block.sync
        def _(e):
            for f in ops["sp"]:
                f(e)


class Prog:
    def __init__(self, Ls, depth, NB, SBK, debug=False):
        self.Ls, self.L = Ls, depth
        self.HALF = Ls + 128
        self.NT = 2 * self.HALF
        self.NB, self.SBK = NB, SBK
        assert self.NT % NB == 0 and NB % SBK == 0 and SBK <= 512 and self.NT % SBK == 0
        self.NCH = self.NT // 128
        self.GATE_PER_STEP = 2
        self.S5_PROD_ENG = "dve"
        self.MSUB = []
        o = 0
        while o < NB:
            w_ = min(512, NB - o)
            self.MSUB.append((o, w_))
            o += w_
        self.debug = debug
        self.nc = bass.Bass("TRN2", target_bir_lowering=False)
        self.es = ExitStack()
        self.build()

    def din(self, name, shape, dt=F32):
        return self.nc.dram_tensor(name, list(shape), dt, kind="ExternalInput").ap()

    def dscr(self, name, shape, dt):
        kind = "ExternalOutput" if self.debug else "Internal"
        return self.nc.dram_tensor(name, list(shape), dt, kind=kind).ap()

    def sb(self, st, name, shape, dt):
        self._uid = getattr(self, "_uid", 0) + 1
        return st.enter_context(self.nc.sbuf_tensor(f"sb{self._uid}_{name}", list(shape), dt))

    def mm(self, out, lhsT, rhs, start, stop, r, w):
        self.S.op("pe", lambda e: e.matmul(out, lhsT, rhs, start=start, stop=stop), r, w)

    def tr(self, out, in_, ident, r, w):
        self.S.op("pe", lambda e: e.transpose(out, in_, ident), r, w)

    def act(self, out, in_, func, r, w, bias=0.0, scale=1.0):
        self.S.op("act", lambda e: e.activation(out=out, in_=in_, func=func, bias=bias, scale=scale), r, w)

    def tt(self, eng, out, in0, in1, op, r, w):
        self.S.op(eng, lambda e: e.tensor_tensor(out=out, in0=in0, in1=in1, op=op), r, w)

    def ts(self, eng, out, in0, s1, s2, op0, op1, r, w):
        if op1 is None:
            self.S.op(eng, lambda e: e.tensor_scalar(out=out, in0=in0, scalar1=s1, scalar2=None, op0=op0), r, w)
        else:
            self.S.op(eng, lambda e: e.tensor_scalar(out=out, in0=in0, scalar1=s1, scalar2=s2, op0=op0, op1=op1), r, w)

    def stt(self, eng, out, in0, scalar, in1, op0, op1, r, w):
        self.S.op(eng, lambda e: e.scalar_tensor_tensor(out=out, in0=in0, scalar=scalar, in1=in1, op0=op0, op1=op1), r, w)

    def scan(self, out, d0, d1, init, r, w):
        self.S.op("dve", lambda e: e.tensor_tensor_scan(out=out, data0=d0, data1=d1, initial=init, op0=ALU.mult, op1=ALU.add), r, w)

    def cp(self, eng, out, in_, r, w):
        if eng == "act":
            self.S.op("act", lambda e: e.activation(out=out, in_=in_, func=AF.Copy), r, w)
        else:
            self.S.op(eng, lambda e: e.tensor_copy(out=out, in_=in_), r, w)

    def dbg(self, name, ap, shape, dt, r):
        if not self.debug:
            return
        t = self.nc.dram_tensor("dbg_" + name, list(shape), dt, kind="ExternalOutput").ap()
        self.S.dma("sp", t, ap, r, ["dbg_" + name])

    def recip(self, eng, out, in_, r, w):
        self.S.op(eng, lambda e: e.reciprocal(out=out, in_=in_), r, w)

    def sin_of(self, out, x, shift, tf, ti, r, w):
        k = ["_sincos"]
        if shift != 0.0:
            self.ts("dve", tf, x, shift, None, ALU.add, None, r + k, k)
            xs = tf
        else:
            xs = x
        self.ts("dve", out, xs, 1.0 / TWO_PI, None, ALU.mult, None, r + k, w)
        self.cp("dve", ti, out, w, k)
        self.cp("dve", out, ti, k, w)
        self.stt("dve", out, out, -TWO_PI, xs, ALU.mult, ALU.add, r + k + list(w), w)
        self.ts("dve", out, out, -3.141592, 3.141592, ALU.max, ALU.min, w, w)
        self.act(out, out, AF.Sin, w, w)

    def memset(self, eng, ap, val, w):
        self.S.op(eng, lambda e: e.memset(ap, val), (), w)

    def build(self):
        nc, es, L, NT = self.nc, self.es, self.L, self.NT
        I = self.I = {}
        I["xT"] = self.din("xT", [D, NT])
        I["carry"] = self.din("carry", [128, 1])
        I["rmask"] = self.din("rmask", [128, NT])
        I["kmask"] = self.din("kmask", [128, NT + 1])
        I["consts"] = self.din("consts", [128, 6, 128])
        I["w_in"] = self.din("w_in", [L, D, N_IN])
        I["w_out_a"] = self.din("w_out_a", [L, 512, D])
        I["w_out_b"] = self.din("w_out_b", [L, 512, D])
        I["w_out_c"] = self.din("w_out_c", [L, 1024, D])
        I["w_o"] = self.din("w_o", [L, D, D])
        I["w_glu"] = self.din("w_glu", [L, 512, 512])
        I["lru_w"] = self.din("lru_w", [L, 8, 4, 128, 128])
        I["normg"] = self.din("normg", [128, L + 1, 16])
        I["lrup"] = self.din("lrup", [128, L, 8, 11])
        I["wgp"] = self.din("wgp", [L, 2, 32, 256])
        I["glap"] = self.din("glap", [128, L, 8])
        I["s5s"] = self.din("s5s", [128, L, 3, 2, 16])
        I["s5r"] = self.din("s5r", [L, 3, 2, 2048])
        I["bexp"] = self.din("bexp", [L, 2, 128, 16, 128])
        I["cexp"] = self.din("cexp", [L, 2, 2, 128, 16, 128])
        I["s5p"] = self.din("s5p", [128, L, 8])
        self.yT = nc.dram_tensor("yT", [D, NT], F32, kind="ExternalOutput").ap()

        X = self.X = {}
        X["zT"] = self.dscr("zT", [D, NT], F32)
        X["hS"] = self.dscr("hS", [D, NT], BF16)
        X["uS"] = self.dscr("uS", [512, NT], BF16)
        X["qS"] = self.dscr("qS", [256, NT], BF16)
        X["kS"] = self.dscr("kS", [256, NT], BF16)
        X["vS"] = self.dscr("vS", [NT, 512], BF16)
        X["gS"] = self.dscr("gS", [32, NT], BF16)
        X["xS"] = self.dscr("xS", [1024, NT], BF16)
        X["yfS"] = self.dscr("yfS", [512, NT], F32)
        X["yaS"] = self.dscr("yaS", [512, NT], BF16)
        X["ybS"] = self.dscr("ybS", [512, NT], BF16)
        X["ycS"] = self.dscr("ycS", [1024, NT], BF16)
        X["sgS"] = self.dscr("sgS", [2048, NT], BF16)
        X["mgS"] = self.dscr("mgS", [3 * 2048, NT], BF16)

        self.S = Sched(nc, es)
        top = es
        self.ps = [top.enter_context(nc.psum_tensor(f"ps{i}", [128, 512], F32)) for i in range(7)]
        self.cst = self.sb(top, "cst", [128, 6, 128], F32)
        self.identb = self.sb(top, "identb", [128, 128], BF16)
        self.ones = self.sb(top, "ones", [128, 128], F32)
        self.normg = self.sb(top, "normg", [128, L + 1, 16], F32)
        self.carry = self.sb(top, "carry", [128, 1], F32)
        S = self.S
        S.dma("sp", self.cst[:], I["consts"], (), ["cst"])
        S.dma("pool", self.identb[:], I["consts"][:, 0, :], (), ["identb"])
        S.dma("sp", self.normg[:], I["normg"], (), ["normg"])
        S.dma("sp", self.carry[:], I["carry"], (), ["carry"])
        self.memset("dve", self.ones[:], 1.0, ["ones"])

        for l in range(L):
            zin = I["xT"] if l == 0 else X["zT"]
            self.phase1(l, zin)
            S.barrier()
            self.phase_lru(l)
            S.barrier()
            self.phase_gla(l)
            S.barrier()
            self.phase_s5(l)
            S.barrier()
            self.phase3(l, zin, last=(l == L - 1))
            S.barrier()
        self.phase4()
        S.barrier()
        with nc.Block() as block:
            S.replay(block)

    def wring_init(self, st, nbuf=4, cols=256):
        self.wr = [self.sb(st, f"wr{i}", [128, 16, cols], BF16) for i in range(nbuf)]
        self.wri = 0

    def wload(self, src, kc, ncols):
        i = self.wri % len(self.wr)
        self.wri += 1
        t = self.wr[i]
        self.S.dma("pool", t[:, 0:kc, 0:ncols], src.rearrange("(kc p) n -> p kc n", p=128), (), [("wr", i)])
        return t, ("wr", i)

    def rms_rstd(self, zt, zkey, sq, rstd, n, pbank, dim):
        S = self.S
        self.act(sq[:, :, 0:n], zt[:, :, 0:n], AF.Square, [zkey], ["sq"])
        for kc in range(KC):
            self.mm(self.ps[pbank][:, 0:n], self.ones[:], sq[:, kc, 0:n], kc == 0, kc == KC - 1,
                    ["ones", "sq"], [("ps", pbank)])
        self.act(rstd[:, 0:n], self.ps[pbank][:, 0:n], AF.Sqrt, [("ps", pbank)], ["rstd"], bias=EPS, scale=1.0 / dim)
        self.recip("dve", rstd[:, 0:n], rstd[:, 0:n], ["rstd"], ["rstd"])

    def phase1(self, l, zin):
        S, I, X, NB, SBK, NT = self.S, self.I, self.X, self.NB, self.SBK, self.NT
        nsub = NB // SBK
        w_in = I["w_in"]
        with ExitStack() as st:
            hT2 = [self.sb(st, f"hT{i}", [128, KC, NB], BF16) for i in range(2)]
            zt = [self.sb(st, f"zt{i}", [128, KC, SBK], F32) for i in range(2)]
            sq = self.sb(st, "sq", [128, KC, SBK], F32)
            rstd = self.sb(st, "rstd", [128, SBK], F32)
            og = [self.sb(st, f"og{i}", [128, NB], BF16) for i in range(3)]
            ov = [self.sb(st, f"ov{i}", [128, 512], BF16) for i in range(2)]
            wv = self.sb(st, "wv", [128, KC, 512], BF16)
            self.wring_init(st)
            S.dma("pool", wv[:], w_in[l, :, C_V:C_V + 512].rearrange("(kc p) n -> p kc n", p=128), (), ["wv"])
            ogi = 0
            ovi = 0
            pb = 0
            def norm_block(b):
                t0 = b * NB
                hT = hT2[b % 2]
                for s in range(nsub):
                    z = zt[s % 2]
                    zk = ("zt", s % 2)
                    c0 = t0 + s * SBK
                    S.dma("sp", z[:], zin[:, c0:c0 + SBK].rearrange("(kc p) t -> p kc t", p=128),
                          [("zT", b)] if zin is X["zT"] else (), [zk])
                    self.rms_rstd(z, zk, sq, rstd, SBK, 6, D)
                    for kc in range(KC):
                        self.stt("dve", hT[:, kc, s * SBK:(s + 1) * SBK], z[:, kc, :], self.normg[:, l, kc:kc + 1],
                                 rstd[:], ALU.mult, ALU.mult, [zk, "rstd", "normg"], [("hT", b % 2, s)])
                hk_ = [("hT", b % 2, s) for s in range(nsub)]
                S.dma("sp", X["hS"][:, t0:t0 + NB].rearrange("(kc p) t -> p kc t", p=128), hT[:], hk_, [("hS", b)])

            nblk = NT // NB
            norm_block(0)
            for b in range(nblk):
                t0 = b * NB
                hT = hT2[b % 2]
                hkeys = [("hT", b % 2, s) for s in range(nsub)]
                gcount = 0
                groups = [(C_U, 512, "uS", 0), (C_Q, 256, "qS", 0), (C_K, 256, "kS", 0),
                          (C_X, 256, "xS", 0), (C_X + 256, 256, "xS", 256), (C_X + 512, 256, "xS", 512),
                          (C_X + 768, 256, "xS", 768), (C_GLR, 32, "gS", 0)]
                groups = [(C_U, 256, "uS", 0), (C_U + 256, 256, "uS", 256)] + groups[1:]
                for (c0, ncols, dst, r0) in groups:
                    gcount += 1
                    if gcount == 4 and b + 1 < nblk:
                        norm_block(b + 1)
                    wt, wk = self.wload(w_in[l, :, c0:c0 + ncols], KC, ncols)
                    for m0 in range(0, ncols, 128):
                        mw = min(128, ncols - m0)
                        o = og[ogi % 3]
                        ok = ("og", ogi % 3)
                        ogi += 1
                        for (so, sw) in self.MSUB:
                            bank = pb % 6
                            pb += 1
                            for kc in range(KC):
                                self.mm(self.ps[bank][0:mw, 0:sw], wt[:, kc, m0:m0 + mw], hT[:, kc, so:so + sw],
                                        kc == 0, kc == KC - 1, [wk] + hkeys, [("ps", bank)])
                            eng = "act" if (pb % 2) else "dve"
                            self.cp(eng, o[0:mw, so:so + sw], self.ps[bank][0:mw, 0:sw], [("ps", bank)], [ok])
                        S.dma("sp", X[dst][r0 + m0:r0 + m0 + mw, t0:t0 + NB], o[0:mw, :], [ok], [(dst, b)])
                tt0 = 0
                while tt0 < NB:
                    tw = min(128, NB - tt0)
                    bank = pb % 6
                    pb += 1
                    for kc in range(KC):
                        self.mm(self.ps[bank][0:tw, 0:512], hT[:, kc, tt0:tt0 + tw], wv[:, kc, :], kc == 0, kc == KC - 1,
                                ["wv"] + hkeys, [("ps", bank)])
                    o = ov[ovi % 2]
                    ok = ("ov", ovi % 2)
                    ovi += 1
                    self.cp("act" if (pb % 2) else "dve", o[0:tw, :], self.ps[bank][0:tw, 0:512], [("ps", bank)], [ok])
                    S.dma("sp", X["vS"][t0 + tt0:t0 + tt0 + tw, :], o[0:tw, :], [ok], [("vS", b)])
                    tt0 += tw

    def phase_lru(self, l):
        S, I, X, NT, HALF, SBK = self.S, self.I, self.X, self.NT, self.HALF, self.SBK
        nch = NT // SBK
        with ExitStack() as st:
            xr = self.sb(st, "l_xr", [128, NT + 4], BF16)
            acc = self.sb(st, "l_acc", [128, NT], F32)
            xcb = self.sb(st, "l_xcb", [128, NT], BF16)
            rmask = self.sb(st, "l_rm", [128, NT], F32)
            a_t = self.sb(st, "l_a", [128, NT], F32)
            bt = self.sb(st, "l_bt", [128, NT], F32)
            hf = self.sb(st, "l_hf", [128, NT], F32)
            hb = self.sb(st, "l_hb", [128, NT], F32)
            rfull = self.sb(st, "l_rf", [128, NT], F32)
            ifull = self.sb(st, "l_if", [128, NT], F32)
            w4 = [self.sb(st, f"l_w{i}", [128, 4, 128], BF16) for i in range(2)]
            prm = self.sb(st, "l_prm", [128, 8, 11], F32)
            sc = self.sb(st, "l_sc", [128, 8, 4], F32)
            ini = self.sb(st, "l_ini", [128, 2], F32)
            nlg = self.n_lru_gate_blocks()
            if nlg > 0:
                ps7 = st.enter_context(self.nc.psum_tensor(f"ps7l_{l}", [128, 512], F32))
                ggen = self.gate_gen(l, st, [(ps7, ("ps", 7)), (self.ps[6], ("ps", 6))], self.gate_blocks()[:nlg])
            else:
                ggen = iter(())
            g_per_hook = (nlg * 32 + 15) // 16
            hook_every = max(1, nch // max(1, g_per_hook))
            hooks_left = [g_per_hook]
            S.dma("sp", rmask[:], I["rmask"], (), ["l_rm"])
            S.dma("sp", prm[:], I["lrup"][:, l], (), ["l_prm"])
            self.act(sc[:, :, 0:2], prm[:, :, 9:11], AF.Exp, ["l_prm"], ["l_sc"], scale=-1.0)
            self.act(sc[:, :, 0:2], sc[:, :, 0:2], AF.Ln, ["l_sc"], ["l_sc"], bias=1.0)
            self.ts("dve", sc[:, :, 2:4], sc[:, :, 0:2], -16.0, None, ALU.mult, None, ["l_sc"], ["l_sc2"])
            self.ts("dve", sc[:, :, 0:2], sc[:, :, 0:2], -8.0, None, ALU.mult, None, ["l_sc", "l_sc2"], ["l_sc"])
            self.memset("dve", xr[:, 0:2], 0.0, ["l_xr"])
            self.memset("dve", xr[:, NT + 2:NT + 4], 0.0, ["l_xr"])
            pb = 0
            ri = 0
            for n in range(8):
                w = w4[n % 2]
                wk = ("l_w", n % 2)
                S.dma("pool", w[:], I["lru_w"][l, n].rearrange("f c d -> c f d"), (), [wk])
                S.dma("sp", xr[:, 2:2 + NT], X["xS"][n * 128:(n + 1) * 128, :], ["xS_all"], ["l_xr"])
                self.ts("dve", acc[:], xr[:, 0:NT], prm[:, n, 0:1], prm[:, n, 4:5], ALU.mult, ALU.add, ["l_xr", "l_prm"], ["l_acc"])
                for j in range(1, 4):
                    self.stt("dve", acc[:], xr[:, j:j + NT], prm[:, n, j:j + 1], acc[:], ALU.mult, ALU.add,
                             ["l_xr", "l_prm", "l_acc"], ["l_acc"])
                self.tt("dve", acc[:], acc[:], rmask[:], ALU.mult, ["l_acc", "l_rm"], ["l_acc"])
                self.cp("act", xcb[:], acc[:], ["l_acc"], ["l_xcb"])
                for d in range(2):
                    for c in range(nch):
                        cs = slice(c * SBK, (c + 1) * SBK)
                        b0 = pb % 6
                        b1 = (pb + 1) % 6
                        pb += 2
                        self.mm(self.ps[b0][:, 0:SBK], w[:, 2 * d, :], xcb[:, cs], True, True, [wk, "l_xcb"], [("ps", b0)])
                        self.mm(self.ps[b1][:, 0:SBK], w[:, 2 * d + 1, :], xcb[:, cs], True, True, [wk, "l_xcb"], [("ps", b1)])
                        self.act(rfull[:, cs], self.ps[b0][:, 0:SBK], AF.Sigmoid, [("ps", b0), "l_prm"], [("l_rf", c)], bias=prm[:, n, 5 + d:6 + d])
                        self.act(ifull[:, cs], self.ps[b1][:, 0:SBK], AF.Sigmoid, [("ps", b1), "l_prm"], [("l_if", c)], bias=prm[:, n, 7 + d:8 + d])
                        if (c + 1) % hook_every == 0 and hooks_left[0] > 0:
                            hooks_left[0] -= 1
                            next(ggen, None)
                    rk = [("l_rf", c) for c in range(nch)]
                    ik = [("l_if", c) for c in range(nch)]
                    self.act(a_t[:], rfull[:], AF.Exp, rk + ["l_sc"], ["l_a"], scale=sc[:, n, d:d + 1])
                    self.act(rfull[:], rfull[:], AF.Exp, rk + ["l_sc2"], ["l_rf"] + rk, scale=sc[:, n, 2 + d:3 + d])
                    self.act(rfull[:], rfull[:], AF.Sqrt, ["l_rf"], ["l_rf"], bias=1.0, scale=-1.0)
                    self.tt("dve", bt[:], ifull[:], acc[:], ALU.mult, ik + ["l_acc"], ["l_bt"])
                    self.tt("dve", bt[:], bt[:], rfull[:], ALU.mult, ["l_rf", "l_bt"], ["l_bt"])
                    for c in range(nch):
                        self.S.bufs[("l_rf", c)] = self.S.bufs["l_rf"]
                    hooks_left[0] = g_per_hook
                    akeys = ["l_a", "l_bt"]
                    H = HALF
                    if d == 0:
                        self.scan(hf[:, 0:H], a_t[:, 0:H], bt[:, 0:H], 0.0, akeys, ["l_hf"])
                        self.tt("dve", ini[:, 0:1], hf[:, H - 1:H], self.carry[:], ALU.mult, ["l_hf", "carry"], ["l_ini0"])
                        self.scan(hf[:, H:NT], a_t[:, H:NT], bt[:, H:NT], ini[:, 0:1], akeys + ["l_ini0", "l_hf"], ["l_hf"])
                    else:
                        self.scan(hb[:, H:NT][:, ::-1], a_t[:, H:NT][:, ::-1], bt[:, H:NT][:, ::-1], 0.0, akeys, ["l_hb"])
                        self.tt("dve", ini[:, 1:2], hb[:, H:H + 1], self.carry[:], ALU.mult, ["l_hb", "carry"], ["l_ini1"])
                        self.scan(hb[:, 0:H][:, ::-1], a_t[:, 0:H][:, ::-1], bt[:, 0:H][:, ::-1], ini[:, 1:2],
                                  akeys + ["l_ini1", "l_hb"], ["l_hb"])
                self.tt("dve", xcb[:], hf[:], hb[:], ALU.add, ["l_hf", "l_hb"], ["l_xcb"])
                S.dma("sp", X["ycS"][n * 128:(n + 1) * 128, :], xcb[:], ["l_xcb"], ["ycS_all"])
            for _ in ggen:
                pass

    def phase_gla(self, l):
        S, I, X, NT, HALF, SBK, NCH = self.S, self.I, self.X, self.NT, self.HALF, self.SBK, self.NCH
        nch = NT // SBK
        HC = NCH // 2
        cst = self.cst
        with ExitStack() as st:
            self.psb = st.enter_context(self.nc.psum_tensor(f"psb_{l}", [128, 1024], BF16))
            vt = self.sb(st, "g_v", [128, NCH, 512], BF16)
            glr = self.sb(st, "g_glr", [32, NT], BF16)
            wg = self.sb(st, "g_wg", [32, 2, 256], BF16)
            gp = self.sb(st, "g_gp", [128, 8], F32)
            nbg = self.sb(st, "g_nbg", [128, 4], F32)
            km = self.sb(st, "g_km", [128, NT + 1], F32)
            q = self.sb(st, "g_q", [128, NT], BF16)
            k = self.sb(st, "g_k", [128, NT], BF16)
            bb = self.sb(st, "g_b", [128, NT], F32)
            ex = self.sb(st, "g_ex", [128, NT], F32)
            qe = [self.sb(st, f"g_qe{d}", [128, NT], BF16) for d in range(2)]
            ke = [self.sb(st, f"g_ke{d}", [128, NT], BF16) for d in range(2)]
            kd = self.sb(st, "g_kd", [128, NT], BF16)
            nbl = self.sb(st, "g_nbl", [128, NCH], F32)
            edec = self.sb(st, "g_edec", [128, NCH], F32)
            sbf = [self.sb(st, f"g_sbf{d}", [128, NCH, 128], BF16) for d in range(2)]
            scur = [self.sb(st, f"g_sc{i}", [128, 128], F32) for i in range(2)]
            kdt = [self.sb(st, f"g_kdt{i}", [128, 128], BF16) for i in range(2)]
            attm = [self.sb(st, f"g_att{i}", [128, 128], BF16) for i in range(4)]
            osb = [self.sb(st, f"g_o{i}", [128, 128], F32) for i in range(2)]
            osq = [self.sb(st, f"g_osq{i}", [128, 128], F32) for i in range(2)]
            ors = [self.sb(st, f"g_ors{i}", [128, 128], F32) for i in range(2)]
            yo = [self.sb(st, "g_yo0", [128, NT], BF16)] * 2
            dsall = self.sb(st, "g_ds", [128, NCH, 128], F32)
            psbs = [slice(0, 128), slice(512, 640)]
            S.dma("sp", vt[:], X["vS"].rearrange("(c p) v -> p c v", p=128), ["vS_all"], ["g_v"])
            S.dma("sp", glr[:], X["gS"], ["gS_all"], ["g_glr"])
            S.dma("pool", wg[:], I["wgp"][l].rearrange("d k n -> k d n"), (), ["g_wg"])
            S.dma("sp", gp[:], I["glap"][:, l], (), ["g_gp"])
            S.dma("sp", km[:], I["kmask"], (), ["g_km"])
            self.ts("dve", nbg[:], gp[:, 0:4], -1.0, None, ALU.mult, None, ["g_gp"], ["g_nbg"])
            pb = 0
            ai = 0
            oi = 0
            ki = 0
            for r in range(2):
                S.dma("sp", q[:], X["qS"][r * 128:(r + 1) * 128, :], ["qS_all"], ["g_q"])
                S.dma("sp", k[:], X["kS"][r * 128:(r + 1) * 128, :], ["kS_all"], ["g_k"])
                for d in range(2):
                    for c in range(nch):
                        cs = slice(c * SBK, (c + 1) * SBK)
                        bank = pb % 6
                        pb += 1
                        self.mm(self.ps[bank][:, 0:SBK], wg[:, d, r * 128:(r + 1) * 128], glr[:, cs], True, True,
                                ["g_wg", "g_glr"], [("ps", bank)])
                        self.act(ex[:, cs], self.ps[bank][:, 0:SBK], AF.Exp, [("ps", bank), "g_nbg"], [("g_ex", c)],
                                 bias=nbg[:, 2 * d + r:2 * d + r + 1], scale=-1.0)
                    exk = [("g_ex", c) for c in range(nch)]
                    self.act(ex[:], ex[:], AF.Ln, exk, ["g_ex"], bias=1.0)
                    if d == 0:
                        self.scan(bb[:], km[:, 0:NT], ex[:], 0.0, ["g_km", "g_ex"] + exk, ["g_b"])
                        self.cp("dve", nbl[:], bb[:, 127:NT:128], ["g_b"], ["g_nbl"])
                    else:
                        self.scan(bb[:, ::-1], km[:, 1:NT + 1][:, ::-1], ex[:, ::-1], 0.0, ["g_km", "g_ex"] + exk, ["g_b"])
                        self.cp("dve", nbl[:], bb[:, 0:NT:128], ["g_b"], ["g_nbl"])
                    self.ts("dve", nbl[:], nbl[:], -1.0 / 16.0, None, ALU.mult, None, ["g_nbl"], ["g_nbl"])
                    self.act(edec[:], nbl[:], AF.Exp, ["g_nbl"], ["g_edec"])
                    self.act(ex[:], bb[:], AF.Exp, ["g_b", "g_ex"], ["g_ex"], bias=math.log(0.125), scale=-1.0 / 16.0)
                    self.tt("dve", qe[d][:], q[:], ex[:], ALU.mult, ["g_q", "g_ex"], [("g_qe", d)])
                    self.act(ex[:], bb[:], AF.Exp, ["g_b", "g_ex"], ["g_ex"], scale=1.0 / 16.0)
                    self.tt("dve", ke[d][:], k[:], ex[:], ALU.mult, ["g_k", "g_ex"], [("g_ke", d)])
                    for c in range(NCH):
                        cs = slice(c * 128, (c + 1) * 128)
                        self.act(ex[:, cs], bb[:, cs], AF.Exp, ["g_b", "g_nbl", "g_ex"], ["g_ex"], bias=nbl[:, c:c + 1], scale=1.0 / 16.0)
                    self.tt("dve", kd[:], k[:], ex[:], ALU.mult, ["g_k", "g_ex"], ["g_kd"])
                    for c in range(NCH):
                        cs = slice(c * 128, (c + 1) * 128)
                        pbk = psbs[ki % 2]
                        self.tr(self.psb[:, pbk], kd[:, cs], self.identb[:], ["g_kd", "identb"], [("psb", ki % 2)])
                        kt = kdt[ki % 2]
                        kk = ("g_kdt", ki % 2)
                        self.cp("act", kt[:], self.psb[:, pbk], [("psb", ki % 2)], [kk])
                        ki += 1
                        bank = pb % 6
                        pb += 1
                        self.mm(self.ps[bank][:, 0:256], kt[:], vt[:, c, r * 256:(r + 1) * 256], True, True, [kk, "g_v"], [("ps", bank)])
                        for hp in range(2):
                            self.cp("act" if hp == 0 else "dve", dsall[hp * 64:(hp + 1) * 64, c, :],
                                    self.ps[bank][hp * 64:(hp + 1) * 64, hp * 128:(hp + 1) * 128], [("ps", bank)], [("g_ds", c)])
                    cur = 0
                    self.memset("dve", scur[0][:], 0.0, [("g_sc", 0)])
                    order = range(NCH) if d == 0 else range(NCH - 1, -1, -1)
                    for c in order:
                        if (d == 0 and c == HC) or (d == 1 and c == HC - 1):
                            self.ts("dve", scur[cur][:], scur[cur][:], self.carry[:, 0:1], None, ALU.mult, None,
                                    [("g_sc", cur), "carry"], [("g_sc", cur)])
                        self.cp("dve", sbf[d][:, c, :], scur[cur][:], [("g_sc", cur)], [("g_sbf", d)])
                        nxt = 1 - cur
                        self.stt("dve", scur[nxt][:], scur[cur][:], edec[:, c:c + 1], dsall[:, c, :],
                                 ALU.mult, ALU.add, [("g_sc", cur), "g_edec", ("g_ds", c)], [("g_sc", nxt)])
                        cur = nxt
                for hp in range(2):
                    h = 2 * r + hp
                    ps_ = slice(hp * 64, (hp + 1) * 64)
                    y = yo[hp]
                    yk = ("g_yo", 0)
                    for c in range(NCH):
                        cs = slice(c * 128, (c + 1) * 128)
                        ats = []
                        for d in range(2):
                            bank = pb % 6
                            pb += 1
                            self.mm(self.ps[bank][:, 0:128], ke[d][ps_, cs], qe[d][ps_, cs], True, True,
                                    [("g_ke", d), ("g_qe", d)], [("ps", bank)])
                            at = attm[ai % 4]
                            ak = ("g_att", ai % 4)
                            ai += 1
                            self.tt("dve", at[:], self.ps[bank][:, 0:128], cst[:, 1 + d, :], ALU.mult, [("ps", bank), "cst"], [ak])
                            ats.append((at, ak))
                        bank = pb % 6
                        pb += 1
                        ob = self.ps[bank][:, 0:128]
                        self.mm(ob, vt[:, c, h * 128:(h + 1) * 128], ats[0][0][:], True, False, ["g_v", ats[0][1]], [("ps", bank)])
                        self.mm(ob, vt[:, c, h * 128:(h + 1) * 128], ats[1][0][:], False, False, ["g_v", ats[1][1]], [("ps", bank)])
                        self.mm(ob, sbf[0][ps_, c, :], qe[0][ps_, cs], False, False, [("g_sbf", 0), ("g_qe", 0)], [("ps", bank)])
                        self.mm(ob, sbf[1][ps_, c, :], qe[1][ps_, cs], False, True, [("g_sbf", 1), ("g_qe", 1)], [("ps", bank)])
                        o2 = osq[oi % 2]
                        o2k = ("g_osq", oi % 2)
                        oi += 1
                        self.cp("act", bb[:, cs], ob, [("ps", bank), "g_b"], [("g_of", c)])
                        self.act(o2[:], ob, AF.Square, [("ps", bank)], [o2k])
                        b2 = 6
                        self.mm(self.ps[b2][:, 0:128], self.ones[:], o2[:], True, True, ["ones", o2k], [("ps", b2)])
                        self.cp("dve", ex[:, cs], self.ps[b2][:, 0:128], [("ps", b2), "g_ex"], [("g_sf", c)])
                    ofk = [("g_of", c) for c in range(NCH)]
                    sfk = [("g_sf", c) for c in range(NCH)]
                    self.act(ex[:], ex[:], AF.Sqrt, sfk, ["g_ex"] + sfk, bias=EPS, scale=1.0 / 128.0)
                    self.recip("dve", ex[:], ex[:], ["g_ex"], ["g_ex"] + sfk)
                    self.stt("dve", y[:], bb[:], gp[:, 4 + h:5 + h], ex[:], ALU.mult, ALU.mult, ofk + sfk + ["g_ex", "g_gp"], [yk])
                    self.S.bufs["g_b"] = [None, {"dve": self.S.latest["dve"]}]
                    self.S.bufs["g_ex"] = [None, {"dve": self.S.latest["dve"]}]
                    for c in range(NCH):
                        self.S.bufs[("g_ex", c)] = self.S.bufs["g_ex"]
                    S.dma("sp", X["ybS"][h * 128:(h + 1) * 128, :], y[:], [yk], ["ybS_all"])

    def phase_s5(self, l):
        S, I, X, NT, HALF, NCH = self.S, self.I, self.X, self.NT, self.HALF, self.NCH
        HC = NCH // 2
        cst = self.cst
        jidx = cst[:, 3, :]
        PI = math.pi
        with ExitStack() as st:
            cs_t = self.sb(st, "s_cs", [128, 2, 16, 128], F32)
            sn_t = self.sb(st, "s_sn", [128, 2, 16, 128], F32)
            rt = self.sb(st, "s_rt", [128, 2, 16, 128], F32)
            bbw = self.sb(st, "s_bbw", [128, 2, 2, 16, 128], BF16)
            cw = self.sb(st, "s_cw", [128, 2, 2, 16, 128], BF16)
            cwn = self.sb(st, "s_cwn", [128, 2, 16, 128], BF16)
            rho = self.sb(st, "s_rho", [128, 2, 16], F32)
            sp_ = self.sb(st, "s_sp", [128, 8], F32)
            wglu = self.sb(st, "s_wglu", [128, 4, 512], BF16)
            S.dma("sp", sp_[:], I["s5p"][:, l], (), ["s_sp"])
            S.dma("pool", wglu[:], I["w_glu"][l].rearrange("(kc p) n -> p kc n", p=128), (), ["s_wglu"])
            for d in range(2):
                for part in range(2):
                    S.dma("pool", cw[:, d, part], I["cexp"][l, d, part], (), ["s_cw"])
            for d in range(2):
                self.ts("dve", cw[:, d, 1], cw[:, d, 1], -1.0, None, ALU.mult, None, ["s_cw"], ["s_cw"])
                self.ts("dve", cwn[:, d], cw[:, d, 0], -1.0, None, ALU.mult, None, ["s_cw"], ["s_cw"])
            with ExitStack() as st2:
                prm = self.sb(st2, "s_prm", [128, 3, 2, 16], F32)
                dt = self.sb(st2, "s_dt", [128, 2, 16], F32)
                th = self.sb(st2, "s_th", [128, 2, 16], F32)
                ang = self.sb(st2, "s_ang", [128, 2, 16, 128], F32)
                tmp = self.sb(st2, "s_tmp", [128, 2, 16, 128], F32)
                S.dma("sp", prm[:], I["s5s"][:, l], (), ["s_prm"])
                self.act(dt[:], prm[:, 2], AF.Exp, ["s_prm"], ["s_dt"])
                self.tt("dve", th[:], prm[:, 1], dt[:], ALU.mult, ["s_prm", "s_dt"], ["s_th"])
                self.tt("dve", dt[:], prm[:, 0], dt[:], ALU.mult, ["s_prm", "s_dt"], ["s_dt"])
                if l == 0:
                    self.dbg("prm", prm[:], [128, 3, 2, 16], F32, ["s_prm"])
                    self.dbg("th", th[:], [128, 2, 16], F32, ["s_th"])
                    self.dbg("dt2", dt[:], [128, 2, 16], F32, ["s_dt"])
                self.act(rho[:], dt[:], AF.Exp, ["s_dt"], ["s_rho"])
                if l == 0:
                    self.dbg("rho0", rho[:], [128, 2, 16], F32, ["s_rho"])
                for d in range(2):
                    for p in range(16):
                        self.ts("dve", ang[:, d, p, :], jidx, th[:, d, p:p + 1], None, ALU.mult, None, ["cst", "s_th"], ["s_ang"])
                        self.ts("dve", rt[:, d, p, :], cst[:, 4, :], rho[:, d, p:p + 1], None, ALU.mult, None, ["cst", "s_rho"], ["s_rt"])
                itmp = self.sb(st2, "s_itmp", [128, 2, 16, 128], mybir.dt.int32)
                self.sin_of(sn_t[:], ang[:], 0.0, tmp[:], itmp[:], ["s_ang"], ["s_sn"])
                self.sin_of(cs_t[:], ang[:], 0.5 * PI, tmp[:], itmp[:], ["s_ang"], ["s_cs"])
            S.barrier()
            with ExitStack() as st2:
                rw = self.sb(st2, "s_rw", [128, 3, 2048], F32)
                e1 = self.sb(st2, "s_e1", [128, 2048], F32)
                e2 = self.sb(st2, "s_e2", [128, 2048], F32)
                e3 = self.sb(st2, "s_e3", [128, 2048], F32)
                e4 = self.sb(st2, "s_e4", [128, 2048], F32)
                e5 = self.sb(st2, "s_e5", [128, 2048], F32)
                e6 = self.sb(st2, "s_e6", [128, 2048], F32)
                ei = self.sb(st2, "s_ei", [128, 2048], mybir.dt.int32)
                bx = self.sb(st2, "s_bx", [128, 2, 16, 128], F32)
                S.dma("sp", bx[:], I["bexp"][l].rearrange("a c p s -> c a p s"), (), ["s_bx"])
                for d in range(2):
                    for i3 in range(3):
                        S.dma("sp", rw[:, i3, :], I["s5r"][l, i3, d:d + 1, :].partition_broadcast(128), (), ["s_rw"])
                    lr, li = rw[:, 0, :], rw[:, 1, :]
                    R_, W_ = ["s_rw", "s_e"], ["s_e"]
                    self.act(e1[:], rw[:, 2, :], AF.Exp, R_, W_)
                    self.tt("dve", e2[:], li, e1[:], ALU.mult, R_, W_)
                    self.tt("dve", e1[:], lr, e1[:], ALU.mult, R_, W_)
                    self.act(e1[:], e1[:], AF.Exp, R_, W_)
                    self.sin_of(e3[:], e2[:], 0.0, e6[:], ei[:], R_, W_)
                    self.sin_of(e4[:], e2[:], 0.5 * PI, e6[:], ei[:], R_, W_)
                    self.tt("dve", e3[:], e3[:], e1[:], ALU.mult, R_, W_)
                    self.tt("dve", e4[:], e4[:], e1[:], ALU.mult, R_, W_)
                    self.ts("dve", e4[:], e4[:], -1.0, None, ALU.add, None, R_, W_)
                    self.tt("dve", e5[:], lr, lr, ALU.mult, R_, W_)
                    self.tt("dve", e6[:], li, li, ALU.mult, R_, W_)
                    self.tt("dve", e5[:], e5[:], e6[:], ALU.add, R_, W_)
                    self.S.op("dve", lambda e, o=e5[:]: e.reciprocal(out=o, in_=o), R_, W_)
                    self.tt("dve", e1[:], e4[:], lr, ALU.mult, R_, W_)
                    self.tt("dve", e6[:], e3[:], li, ALU.mult, R_, W_)
                    self.tt("dve", e1[:], e1[:], e6[:], ALU.add, R_, W_)
                    self.tt("dve", e1[:], e1[:], e5[:], ALU.mult, R_, W_)
                    self.tt("dve", e2[:], e3[:], lr, ALU.mult, R_, W_)
                    self.tt("dve", e6[:], e4[:], li, ALU.mult, R_, W_)
                    self.tt("dve", e2[:], e2[:], e6[:], ALU.subtract, R_, W_)
                    self.tt("dve", e2[:], e2[:], e5[:], ALU.mult, R_, W_)
                    fr = e1[:].rearrange("c (p s) -> c p s", p=16)
                    fi = e2[:].rearrange("c (p s) -> c p s", p=16)
                    t1 = e3[:].rearrange("c (p s) -> c p s", p=16)
                    t2 = e4[:].rearrange("c (p s) -> c p s", p=16)
                    RB = R_ + ["s_bx"]
                    self.tt("dve", t1, fr, bx[:, 0], ALU.mult, RB, W_)
                    self.tt("dve", t2, fi, bx[:, 1], ALU.mult, RB, W_)
                    self.tt("dve", bbw[:, d, 0], t1, t2, ALU.subtract, R_, ["s_bbw"])
                    self.tt("dve", t1, fr, bx[:, 1], ALU.mult, RB + ["s_bbw"], W_)
                    self.tt("dve", t2, fi, bx[:, 0], ALU.mult, RB, W_)
                    self.tt("dve", bbw[:, d, 1], t1, t2, ALU.add, R_, ["s_bbw"])
            S.barrier()
            if l == 0:
                self.dbg("cs", cs_t[:], [128, 2, 16, 128], F32, ["s_cs"])
                self.dbg("sn", sn_t[:], [128, 2, 16, 128], F32, ["s_sn"])
                self.dbg("rt", rt[:], [128, 2, 16, 128], F32, ["s_rt"])
                self.dbg("rho", rho[:], [128, 2, 16], F32, ["s_rho"])
                self.dbg("bbw", bbw[:], [128, 2, 2, 16, 128], BF16, ["s_bbw"])
                self.dbg("cw", cw[:], [128, 2, 2, 16, 128], BF16, ["s_cw"])
            with ExitStack() as st2:
                ufr = [self.sb(st2, f"s_uf{i}", [128, 4, 128], BF16) for i in range(3)]
                ps7 = st2.enter_context(self.nc.psum_tensor(f"ps7_{l}", [128, 512], F32))
                gblocks = self.gate_blocks()[self.n_lru_gate_blocks():]
                ggen = self.gate_gen(l, st2, [(ps7, ("ps", 7)), (self.ps[6], ("ps", 6))], gblocks)
                grate = len(gblocks) * 32.0 / (4 * NCH)
                gacc = 0.0
                bh2 = [[self.sb(st2, f"s_bh{j}{i}", [128, 8, 128], F32) for i in range(2)] for j in range(2)]
                t_ = [self.sb(st2, f"s_t{i}", [128, 8, 128], F32) for i in range(2)]
                bsb2 = [[self.sb(st2, f"s_bsb{j}{i}", [128, 8, 128], F32) for i in range(2)] for j in range(2)]
                pbf = [[self.sb(st2, f"s_pbf{i}{j}", [128, 8, 128], BF16) for j in range(4)] for i in range(1)]
                hprev = self.sb(st2, "s_hprev", [128, 2, 16], F32)
                hl = self.sb(st2, "s_hl", [128, 4, 8], F32)
                yf = [self.sb(st2, f"s_yf{i}", [128, 4, 128], F32) for i in range(2)]
                yy = self.sb(st2, "s_yy", [128, 4, 128], F32)
                zb = self.sb(st2, "s_zb", [128, 4, 128], BF16)
                sg = self.sb(st2, "s_sg", [128, 4, 128], F32)
                ya = [self.sb(st2, f"s_ya{i}", [128, 4, 128], BF16) for i in range(2)]
                hi_ = 0
                fcnt = 0
                pending = None
                pending2 = []
                steps = []
                for d in range(2):
                    order = range(NCH) if d == 0 else range(NCH - 1, -1, -1)
                    for f in order:
                        for hh in range(2):
                            steps.append((d, f, hh))

                fidx = {}
                def stageA(d, f, hh):
                    fs = slice(f * 128, (f + 1) * 128)
                    P0 = 8 * hh
                    if hh == 0:
                        fidx[(d, f)] = len(fidx) % 3
                        ui = fidx[(d, f)]
                        S.dma("sp", ufr[ui][:], X["uS"][:, fs].rearrange("(o p) t -> p o t", p=128), ["uS_all"], [("s_uf", ui)])
                    ui = fidx[(d, f)]
                    u_, uk = ufr[ui], ("s_uf", ui)
                    for pl in range(8):
                        p = P0 + pl
                        for part in range(2):
                            bank = 2 * part + pl // 4
                            urhs = u_[:, p // 4, :] if d == 0 else u_[:, p // 4, ::-1]
                            self.mm(self.ps[bank][:, (pl % 4) * 128:(pl % 4 + 1) * 128], bbw[:, d, part, p, :], urhs,
                                    True, True, ["s_bbw", uk], [("ps", bank)])
                    bsb = bsb2[hh]
                    for part in range(2):
                        for a2 in range(2):
                            bank = 2 * part + a2
                            self.cp("act", bsb[part][:, 4 * a2:4 * a2 + 4, :], self.ps[bank][:].rearrange("q (a t) -> q a t", a=4),
                                    [("ps", bank)], [f"s_bsb{hh}{part}"])

                stageA(*steps[0])
                for si, (d, f, hh) in enumerate(steps):
                    if si + 1 < len(steps):
                        stageA(*steps[si + 1])
                    gacc += grate
                    while gacc >= 1.0:
                        next(ggen, None)
                        gacc -= 1.0
                    fs = slice(f * 128, (f + 1) * 128)
                    P0 = 8 * hh
                    first_of_dir = (f == (0 if d == 0 else NCH - 1)) and hh == 0
                    if first_of_dir:
                        if pending is not None:
                            pending()
                            pending = None
                        self.memset("dve", hprev[:], 0.0, [("s_hprev", 0), ("s_hprev", 1)])
                    if hh == 0:
                        ybank = 4 + (fcnt % 2)
                        fcnt += 1
                        if (d == 0 and f == HC) or (d == 1 and f == HC - 1):
                            self.ts("dve", hprev[:], hprev[:], self.carry[:, 0:1], None, ALU.mult, None,
                                    [("s_hprev", 0), ("s_hprev", 1), "carry"], [("s_hprev", 0), ("s_hprev", 1)])
                        if d == 1:
                            yfl = yf[f % 2]
                            yfk = ("s_yf", f % 2)
                            S.dma("sp", yfl[:], X["yfS"][:, fs].rearrange("(o p) t -> p o t", p=128), ["yfS_all"], [yfk])
                    bsb = bsb2[hh]
                    bh = bh2[hh]
                    kb0, kb1 = f"s_bh{hh}0", f"s_bh{hh}1"
                    kbh = [kb0, kb1]
                    ct = cs_t[:, d, P0:P0 + 8, :]
                    stb = sn_t[:, d, P0:P0 + 8, :]
                    BR, BI = bsb[0][:], bsb[1][:]
                    self.tt("dve", t_[0][:], BR, ct, ALU.mult, [f"s_bsb{hh}0", "s_cs"], ["s_t0"])
                    self.tt("dve", t_[1][:], BI, stb, ALU.mult, [f"s_bsb{hh}1", "s_sn"], ["s_t1"])
                    self.tt("dve", bh[0][:], t_[0][:], t_[1][:], ALU.add, ["s_t0", "s_t1"], [kb0])
                    self.tt("dve", t_[0][:], BI, ct, ALU.mult, [f"s_bsb{hh}1", "s_cs", kb0], ["s_t0"])
                    self.tt("dve", t_[1][:], BR, stb, ALU.mult, [f"s_bsb{hh}0", "s_sn", kb0], ["s_t1"])
                    self.tt("dve", bh[1][:], t_[0][:], t_[1][:], ALU.subtract, ["s_t0", "s_t1"], [kb1])
                    edge = 0
                    for part in range(2):
                        self.tt("dve", bh[part][:, :, edge], bh[part][:, :, edge], hprev[:, part, P0:P0 + 8], ALU.add,
                                [kbh[part], ("s_hprev", hh)], [kbh[part]])
                    for part in range(2):
                        fl = bh[part][:].rearrange("q a t -> q (a t)")
                        rtf = rt[:, d, P0:P0 + 8, :].rearrange("q a t -> q (a t)")
                        self.scan(fl, rtf, fl, 0.0, [kbh[part], "s_rt"], [kbh[part]])
                    last = 127
                    tl = 127
                    gr, gi = bh[0][:, :, last], bh[1][:, :, last]
                    cl, sl_ = cs_t[:, d, P0:P0 + 8, tl], sn_t[:, d, P0:P0 + 8, tl]
                    RK = [kb0, kb1, "s_cs", "s_sn", "s_hl"]
                    E = "pool"
                    self.tt(E, hl[:, 0, :], gr, cl, ALU.mult, RK, ["s_hl"])
                    self.tt(E, hl[:, 1, :], gi, sl_, ALU.mult, RK, ["s_hl"])
                    self.tt(E, hl[:, 0, :], hl[:, 0, :], hl[:, 1, :], ALU.subtract, RK, ["s_hl"])
                    self.tt(E, hl[:, 2, :], gr, sl_, ALU.mult, RK, ["s_hl"])
                    self.tt(E, hl[:, 3, :], gi, cl, ALU.mult, RK, ["s_hl"])
                    self.tt(E, hl[:, 2, :], hl[:, 2, :], hl[:, 3, :], ALU.add, RK, ["s_hl"])
                    self.tt(E, hprev[:, 0, P0:P0 + 8], hl[:, 0, :], rho[:, d, P0:P0 + 8], ALU.mult, ["s_hl", "s_rho", ("s_hprev", hh)], [("s_hprev", hh)])
                    self.tt(E, hprev[:, 1, P0:P0 + 8], hl[:, 2, :], rho[:, d, P0:P0 + 8], ALU.mult, ["s_hl", "s_rho", ("s_hprev", hh)], [("s_hprev", hh)])
                    hb_ = pbf[0]
                    hk = ("s_pbf", 0)
                    hi_ += 1
                    GR, GI = bh[0][:], bh[1][:]
                    PE_ = self.S5_PROD_ENG
                    self.tt(PE_, hb_[0][:], GR, ct, ALU.mult, [kb0, "s_cs"], [hk])
                    self.tt(PE_, hb_[1][:], GI, stb, ALU.mult, [kb1, "s_sn"], [hk])
                    self.tt(PE_, hb_[2][:], GR, stb, ALU.mult, [kb0, "s_sn"], [hk])
                    self.tt("dve", hb_[3][:], GI, ct, ALU.mult, [kb1, "s_cs"], [(hk, 3)])
                    wsel = [cw[:, d, 0], cwn[:, d], cw[:, d, 1], cw[:, d, 1]]
                    for o2 in range(2):
                        oc = 2 * hh + o2
                        yo_ = self.ps[ybank][:, oc * 128:(oc + 1) * 128]
                        n_ = 0
                        for pl in range(4 * o2, 4 * o2 + 4):
                            p = P0 + pl
                            for k4 in range(4):
                                crhs = hb_[k4][:, pl, :] if d == 0 else hb_[k4][:, pl, ::-1]
                                self.mm(yo_, wsel[k4][:, p, :], crhs, n_ == 0, n_ == 15, ["s_cw", hk, (hk, 3)], [("ps", ybank)])
                                n_ += 1
                    if hh == 1 and pending2:
                        pending2.pop(0)()
                    if hh == 0 and pending is not None:
                        pending()
                        pending = None
                    if hh == 1:
                        def epilogue(f=f, fs=fs, d=d, ybank=ybank, yfl=(yf[f % 2]), yfk=("s_yf", f % 2), u_=ufr[fidx[(d, f)]], uk=("s_uf", fidx[(d, f)])):
                            y4 = self.ps[ybank][:].rearrange("q (o t) -> q o t", o=4)
                            if d == 0:
                                self.cp("act", yfl[:], y4, [("ps", ybank)], [yfk])
                                S.dma("sp", X["yfS"][:, fs].rearrange("(o p) t -> p o t", p=128), yfl[:], [yfk], ["yfS_all"])
                            else:
                                for oc in range(4):
                                    self.stt("dve", yy[:, oc, :], u_[:, oc, :], sp_[:, oc:oc + 1], self.ps[ybank][:, oc * 128:(oc + 1) * 128],
                                             ALU.mult, ALU.add, [uk, "s_sp", ("ps", ybank)], ["s_yy"])
                                self.tt("dve", yy[:], yy[:], yfl[:], ALU.add, ["s_yy", yfk], ["s_yy"])
                                self.act(yy[:], yy[:], AF.Gelu_apprx_tanh, ["s_yy"], ["s_yy"])
                                self.cp("act", zb[:], yy[:], ["s_yy"], ["s_zb"])
                                yal = ya[f % 2]
                                yak = ("s_ya", f % 2)
                                for oc in range(4):
                                    go = self.ps[6][:, oc * 128:(oc + 1) * 128]
                                    for kc in range(4):
                                        self.mm(go, wglu[:, kc, oc * 128:(oc + 1) * 128], zb[:, kc, :], kc == 0, kc == 3, ["s_wglu", "s_zb"], [("ps", 6)])
                                    self.act(sg[:, oc, :], go, AF.Sigmoid, [("ps", 6), "s_sp"], ["s_sg"], bias=sp_[:, 4 + oc:5 + oc])
                                def part2(yal=yal, yak=yak, fs=fs):
                                    self.tt("dve", yal[:], yy[:], sg[:], ALU.mult, ["s_yy", "s_sg"], [yak])
                                    S.dma("sp", X["yaS"][:, fs].rearrange("(o p) t -> p o t", p=128), yal[:], [yak], ["yaS_all"])
                                pending2.append(part2)
                        pending = epilogue
                if pending is not None:
                    pending()
                    pending = None
                while pending2:
                    pending2.pop(0)()
                for _ in ggen:
                    pass


    def n_lru_gate_blocks(self):
        nb = len(self.gate_blocks())
        return min(2, nb - 1)

    def gate_blocks(self):
        return [(o, min(512, self.NT - o)) for o in range(0, self.NT, 512)]

    def gate_gen(self, l, st, banks, blocks):
        S, I, X, NT = self.S, self.I, self.X, self.NT
        w_in = I["w_in"]
        hTg = self.sb(st, "gg_h", [128, KC, 512], BF16)
        ring = [self.sb(st, f"gg_w{i}", [128, KC, 256], BF16) for i in range(3)]
        gst = [self.sb(st, f"gg_s{i}", [128, 512], BF16) for i in range(3)]
        tiles = []
        for (c0, r0, nrow) in [(C_GA, 0, 4), (C_GB, 4, 4), (C_GC, 8, 8)]:
            for g2 in range(nrow // 2):
                tiles.append((c0 + g2 * 256, "sgS", r0 + g2 * 2, AF.Silu))
        for br in range(3):
            for g2 in range(8):
                tiles.append((C_M + br * D + g2 * 256, "mgS", br * 16 + g2 * 2, AF.Sigmoid))
        wi = 0
        gi = 0
        pb = 0
        for (t0, BG) in blocks:
            S.dma("sp", hTg[:, :, 0:BG], X["hS"][:, t0:t0 + BG].rearrange("(kc p) t -> p kc t", p=128), ["hS_all"], ["gg_h"])
            for (c0, dst, row0, fn) in tiles:
                wt = ring[wi % 3]
                wk = ("gg_w", wi % 3)
                wi += 1
                S.dma("pool", wt[:], w_in[l, :, c0:c0 + 256].rearrange("(kc p) n -> p kc n", p=128), (), [wk])
                for mi in range(2):
                    g = gst[gi % 3]
                    gk = ("gg_s", gi % 3)
                    gi += 1
                    bank, bkey = banks[pb % len(banks)]
                    pb += 1
                    for kc in range(KC):
                        self.mm(bank[:, 0:BG], wt[:, kc, mi * 128:(mi + 1) * 128], hTg[:, kc, 0:BG],
                                kc == 0, kc == KC - 1, [wk, "gg_h"], [bkey])
                    self.act(g[:, 0:BG], bank[:, 0:BG], fn, [bkey], [gk])
                    row = row0 + mi
                    S.dma("sp", X[dst][row * 128:(row + 1) * 128, t0:t0 + BG], g[:, 0:BG], [gk], [(dst, "all")])
                yield

    def phase3(self, l, zin, last):
        S, I, X, NB, SBK, NT, L = self.S, self.I, self.X, self.NB, self.SBK, self.NT, self.L
        with ExitStack() as st:
            sgt = self.sb(st, "p3_sg", [128, KC, NB], BF16)
            yg = self.sb(st, "p3_y", [128, KC, NB], BF16)
            m = self.sb(st, "p3_m", [128, KC, NB], BF16)
            MW = max(w_ for _, w_ in self.MSUB)
            mgt = [self.sb(st, f"p3_mg{i}", [128, 6, NB], BF16) for i in range(2)]
            tmp = [self.sb(st, f"p3_t{i}", [128, MW], F32) for i in range(2)]
            macc = [self.sb(st, f"p3_ma{i}", [128, MW], F32) for i in range(2)]
            zt = [self.sb(st, f"p3_z{i}", [128, NB], F32) for i in range(2)]
            self.wring_init(st, nbuf=6)
            pb = 0
            ti = 0
            zi = 0
            mgi = 0
            for b in range(NT // NB):
                t0 = b * NB
                for (ra, rb) in [(0, 4), (4, 8), (8, 16)]:
                    S.dma("sp", sgt[:, ra:rb, :], X["sgS"][ra * 128:rb * 128, t0:t0 + NB].rearrange("(kc p) t -> p kc t", p=128),
                          ["sgS_all"], [("p3_sg", ra)])
                S.dma("sp", yg[:, 0:4, :], X["yaS"][:, t0:t0 + NB].rearrange("(kc p) t -> p kc t", p=128), ["yaS_all"], [("p3_y", r) for r in range(0, 4)])
                S.dma("sp", yg[:, 4:8, :], X["ybS"][:, t0:t0 + NB].rearrange("(kc p) t -> p kc t", p=128), ["ybS_all"], [("p3_y", r) for r in range(4, 8)])
                S.dma("sp", yg[:, 8:16, :], X["ycS"][:, t0:t0 + NB].rearrange("(kc p) t -> p kc t", p=128), ["ycS_all"], [("p3_y", r) for r in range(8, 16)])
                for row in range(16):
                    self.tt("dve", yg[:, row, :], yg[:, row, :], sgt[:, row, :], ALU.mult,
                            [("p3_y", row), ("p3_sg", 0 if row < 4 else (4 if row < 8 else 8))], [("p3_y", row)])
                ykeys = [("p3_y", r) for r in range(16)]
                wouts = [(I["w_out_a"], 4, 0), (I["w_out_b"], 4, 4), (I["w_out_c"], 8, 8)]
                for g2 in range(8):
                    mg = mgt[mgi % 2]
                    mgk = ("p3_mg", mgi % 2)
                    mgi += 1
                    for br in range(3):
                        r_ = br * 16 + g2 * 2
                        S.dma("sp", mg[:, 2 * br:2 * br + 2, :], X["mgS"][r_ * 128:(r_ + 2) * 128, t0:t0 + NB].rearrange("(a p) t -> p a t", p=128),
                              ["mgS_all"], [mgk])
                    tiles = []
                    for br in range(3):
                        wo_src, nk, r0 = wouts[br]
                        wo, wok = self.wload(wo_src[l, :, g2 * 256:(g2 + 1) * 256], nk, 256)
                        tiles.append((wo, wok, nk, r0))
                    for mi in range(2):
                        row = g2 * 2 + mi
                        ms = slice(mi * 128, (mi + 1) * 128)
                        for (so, sw) in self.MSUB:
                            ss = slice(so, so + sw)
                            ma = macc[zi % 2]
                            mak = ("p3_ma", zi % 2)
                            zi += 1
                            for br in range(3):
                                wo, wok, nk, r0 = tiles[br]
                                g = mg[:, 2 * br + mi, ss]
                                bank2 = pb % 6
                                pb += 1
                                for kc in range(nk):
                                    self.mm(self.ps[bank2][:, 0:sw], wo[:, kc, ms], yg[:, r0 + kc, ss], kc == 0, kc == nk - 1,
                                            [wok] + ykeys[r0:r0 + nk], [("ps", bank2)])
                                if br == 0:
                                    self.tt("dve", ma[:, 0:sw], g, self.ps[bank2][:, 0:sw], ALU.mult, [mgk, ("ps", bank2)], [mak])
                                else:
                                    t = tmp[ti % 2]
                                    tk = ("p3_t", ti % 2)
                                    ti += 1
                                    self.tt("dve", t[:, 0:sw], g, self.ps[bank2][:, 0:sw], ALU.mult, [mgk, ("ps", bank2)], [tk])
                                    if br == 1:
                                        self.tt("dve", ma[:, 0:sw], ma[:, 0:sw], t[:, 0:sw], ALU.add, [mak, tk], [mak])
                                    else:
                                        self.tt("dve", m[:, row, ss], ma[:, 0:sw], t[:, 0:sw], ALU.add, [mak, tk], [("p3_m", row)])
                mkeys = [("p3_m", r) for r in range(16)]
                for g2 in range(8):
                    wt, wk = self.wload(I["w_o"][l, :, g2 * 256:(g2 + 1) * 256], KC, 256)
                    for mi in range(2):
                        row = g2 * 2 + mi
                        z = zt[row % 2]
                        zk = ("p3_z", row % 2)
                        S.dma("sp", z[:], zin[row * 128:(row + 1) * 128, t0:t0 + NB], [("zT", b)] if zin is X["zT"] else (), [zk])
                        for (so, sw) in self.MSUB:
                            ss = slice(so, so + sw)
                            bank = pb % 6
                            pb += 1
                            for kc in range(KC):
                                self.mm(self.ps[bank][:, 0:sw], wt[:, kc, mi * 128:(mi + 1) * 128], m[:, kc, ss], kc == 0, kc == KC - 1,
                                        [wk] + mkeys, [("ps", bank)])
                            self.tt("dve", z[:, ss], z[:, ss], self.ps[bank][:, 0:sw], ALU.add, [zk, ("ps", bank)], [zk])
                        S.dma("sp", X["zT"][row * 128:(row + 1) * 128, t0:t0 + NB], z[:], [zk], [("zT", b)])

    def phase4(self):
        S, X, SBK, NT, L = self.S, self.X, self.SBK, self.NT, self.L
        with ExitStack() as st:
            zf = [self.sb(st, f"p4_z{i}", [128, KC, SBK], F32) for i in range(2)]
            sq = self.sb(st, "sq", [128, KC, SBK], F32)
            rstd = self.sb(st, "rstd", [128, SBK], F32)
            for s in range(NT // SBK):
                c0 = s * SBK
                z = zf[s % 2]
                zk = ("p4_z", s % 2)
                S.dma("sp", z[:], X["zT"][:, c0:c0 + SBK].rearrange("(kc p) t -> p kc t", p=128), (), [zk])
                self.rms_rstd(z, zk, sq, rstd, SBK, 6, D)
                for kc in range(KC):
                    self.stt("dve", z[:, kc, :], z[:, kc, :], self.normg[:, L, kc:kc + 1], rstd[:], ALU.mult, ALU.mult,
                             [zk, "rstd", "normg"], [zk])
                S.dma("sp", self.yT[:, c0:c0 + SBK].rearrange("(kc p) t -> p kc t", p=128), z[:], [zk], [("yT", s)])


def _slot_layout(x_prompt, x_sample, meta):
    Bp, Lp, _ = x_prompt.shape
    Bs, Ls, _ = x_sample.shape
    assert Lp == 2 * Ls and Bp == 2 and Bs == 8
    HALF = Ls + 128
    NT = 2 * HALF
    slots = np.zeros((8, NT, D), np.float32)
    carry = np.zeros((8,), np.float32)
    real = np.zeros((8, NT), np.float32)
    where = []
    for i in range(Bp):
        c = i
        slots[c, 112:128] = meta
        slots[c, 128:128 + Lp] = x_prompt[i]
        real[c, 112:128 + Lp] = 1.0
        carry[c] = 1.0
        where.append(("p", i, c, 128))
    for i in range(Bs):
        c = 2 + i // 2
        o = (i % 2) * HALF
        slots[c, o + 112:o + 128] = meta
        slots[c, o + 128:o + 128 + Ls] = x_sample[i]
        real[c, o + 112:o + 128 + Ls] = 1.0
        where.append(("s", i, c, o + 128))
    return slots, carry, real, where, Ls, NT


def _prep_shared(inp, L, NT):
    f = lambda a: np.ascontiguousarray(np.asarray(a, dtype=np.float32))
    sh = {}
    for k in ["w_in", "w_out_a", "w_out_b", "w_out_c", "w_o"]:
        sh[k] = f(inp[k][:L])
    sh["w_glu"] = f(inp["s5_w_glu"][:L])
    lw = np.stack([inp["lru_w_a"][:L, 0], inp["lru_w_x"][:L, 0], inp["lru_w_a"][:L, 1], inp["lru_w_x"][:L, 1]], axis=2)
    sh["lru_w"] = f(lw)
    ng = np.concatenate([np.asarray(inp["norm_g"][:L]), np.asarray(inp["final_norm_g"])[None]], axis=0)
    sh["normg"] = f(ng.reshape(L + 1, 16, 128).transpose(2, 0, 1))
    cw = np.asarray(inp["conv_w"][:L]).reshape(L, 4, 8, 128).transpose(3, 0, 2, 1)
    cb = np.asarray(inp["conv_b"][:L]).reshape(L, 8, 128).transpose(2, 0, 1)[..., None]
    def dn(a):
        return np.asarray(a[:L]).reshape(L, 2, 8, 128).transpose(3, 0, 2, 1)
    sh["lrup"] = f(np.concatenate([cw, cb, dn(inp["lru_b_a"]), dn(inp["lru_b_x"]), dn(inp["lru_lam"])], axis=3))
    wgp = np.zeros((L, 2, 32, 256), np.float32)
    for d in range(2):
        wgp[:, d, d * 16:(d + 1) * 16, :] = np.asarray(inp["gla_w_gate_up"][:L, d])
    sh["wgp"] = wgp
    bg = np.asarray(inp["gla_b_gate"][:L]).reshape(L, 2, 2, 128).transpose(3, 0, 1, 2).reshape(128, L, 4)
    gng = np.asarray(inp["gla_norm_g"][:L]).reshape(L, 4, 128).transpose(2, 0, 1)
    sh["glap"] = f(np.concatenate([bg, gng], axis=2))
    def sp_layout(a):
        return np.asarray(a).reshape(L, 2, 16, 2, 64).transpose(3, 4, 0, 1, 2).reshape(128, L, 2, 16)
    lstep = np.broadcast_to(np.asarray(inp["s5_log_step"][:L])[..., None], (L, 2, 32, 64))
    sh["s5s"] = f(np.stack([sp_layout(inp["s5_lam_re"][:L]), sp_layout(inp["s5_lam_im"][:L]), sp_layout(lstep)], axis=2))
    sh["s5r"] = f(np.stack([np.asarray(inp["s5_lam_re"][:L]).reshape(L, 2, 2048), np.asarray(inp["s5_lam_im"][:L]).reshape(L, 2, 2048),
                            lstep.reshape(L, 2, 2048)], axis=1))
    bexp = np.zeros((L, 2, 128, 16, 128), np.float32)
    cexp = np.zeros((L, 2, 2, 128, 16, 128), np.float32)
    bre, bim = np.asarray(inp["s5_b_re"][:L]), np.asarray(inp["s5_b_im"][:L])
    cre, cim = np.asarray(inp["s5_c_re"][:L]), np.asarray(inp["s5_c_im"][:L])
    for g in range(32):
        p, g2, go = g // 2, g % 2, g % 8
        for part, src in enumerate([bre, bim]):
            bexp[:, part, 16 * go:16 * go + 16, p, g2 * 64:(g2 + 1) * 64] = src[:, g].transpose(0, 2, 1)
        for part, src in enumerate([cre, cim]):
            cexp[:, :, part, g2 * 64:(g2 + 1) * 64, p, 16 * go:16 * go + 16] = src[:, :, g].transpose(0, 1, 3, 2)
    sh["bexp"], sh["cexp"] = bexp, cexp
    dsk = np.asarray(inp["s5_d"][:L]).reshape(L, 4, 128).transpose(2, 0, 1)
    bgl = np.asarray(inp["s5_b_glu"][:L]).reshape(L, 4, 128).transpose(2, 0, 1)
    sh["s5p"] = f(np.concatenate([dsk, bgl], axis=2))
    consts = np.zeros((128, 6, 128), np.float32)
    consts[:, 0] = np.eye(128)
    jj, ii = np.meshgrid(np.arange(128), np.arange(128), indexing="ij")
    consts[:, 1] = (jj <= ii)
    consts[:, 2] = (jj >= ii)
    consts[:, 3] = np.arange(1, 129)[None, :]
    consts[:, 4] = 1.0
    consts[:, 4, 0] = 0.0
    consts[:, 5] = 1.0
    consts[:, 5, 127] = 0.0
    sh["consts"] = consts
    km = np.ones((128, NT + 1), np.float32)
    km[:, 0::128] = 0.0
    sh["kmask"] = km
    return sh


_PROG_CACHE = {}


def _get_prog(Ls, L, debug=False, NB=None, SBK=None):
    key = (Ls, L, debug, NB, SBK)
    if key not in _PROG_CACHE:
        NT = 2 * (Ls + 128)
        if NB is None:
            NB = NT // 4
            SBK = NB // 4
        _PROG_CACHE[key] = Prog(Ls, L, NB, SBK, debug)
    return _PROG_CACHE[key]


def run(inp, L=None, debug=False, NB=None, SBK=None):
    x_prompt = np.asarray(inp["x_prompt"], np.float32)
    x_sample = np.asarray(inp["x_sample"], np.float32)
    meta = np.asarray(inp["meta_tokens"], np.float32)
    if L is None:
        L = int(np.asarray(inp["w_in"]).shape[0])
    slots, carry, real, where, Ls, NT = _slot_layout(x_prompt, x_sample, meta)
    sh = _prep_shared(inp, L, NT)
    prog = _get_prog(Ls, L, debug, NB, SBK)
    in_maps = []
    for c in range(8):
        m = dict(sh)
        m["xT"] = np.ascontiguousarray(slots[c].T)
        m["carry"] = np.full((128, 1), carry[c], np.float32)
        m["rmask"] = np.ascontiguousarray(np.broadcast_to(real[c][None, :], (128, NT)))
        in_maps.append(m)
    res = run_bass_kernel_spmd(prog.nc, in_maps, core_ids=list(range(8)))
    yp = np.zeros_like(x_prompt)
    ys = np.zeros_like(x_sample)
    for kind, i, c, start in where:
        yT = res.results[c]["yT"]
        if kind == "p":
            yp[i] = yT[:, start:start + x_prompt.shape[1]].T
        else:
            ys[i] = yT[:, start:start + Ls].T
    return (yp, ys), res


def kernel(**inputs):
    (yp, ys), _ = run(inputs)
    return yp, ys
```

```python
import math
from contextlib import ExitStack

import numpy as np
import concourse.bass as bass
import concourse.mybir as mybir
from concourse.bass_utils import run_bass_kernel_spmd

F32 = mybir.dt.float32
BF16 = mybir.dt.bfloat16
AF = mybir.ActivationFunctionType
ALU = mybir.AluOpType

D = 2048
KC = 16
N_IN = 10784
N_META = 16
EPS = 1e-6
C_U, C_GA, C_Q, C_K, C_V, C_GB, C_GLR, C_X, C_GC, C_M = 0, 512, 1024, 1280, 1536, 2048, 2560, 2592, 3616, 4640
TWO_PI = 2.0 * math.pi
NDQ = 12


class Sched:
    def __init__(self, nc, es):
        self.nc = nc
        self.engs = ["pe", "act", "dve", "pool", "sp"]
        self.ops = {e: [] for e in self.engs}
        self.semh = {}
        for e in ["pe", "act", "dve", "pool"]:
            self.semh[e] = es.enter_context(nc.semaphore("s_" + e))
        for q in ["sp", "pool", "act"]:
            for i in range(NDQ):
                self.semh[f"d_{q}{i}"] = es.enter_context(nc.semaphore(f"d_{q}{i}"))
        self.latest = {k: 0 for k in self.semh}
        self.dcnt = {"sp": 0, "pool": 0, "act": 0}
        self.seen = {e: {} for e in self.engs}
        self.bufs = {}
        self.nins = 0

    def _deps(self, reads, writes):
        toks = {}

        def add(k, v):
            if toks.get(k, 0) < v:
                toks[k] = v

        for key in reads:
            b = self.bufs.get(key)
            if b and b[0]:
                add(*b[0])
        for key in writes:
            b = self.bufs.get(key)
            if b:
                if b[0]:
                    add(*b[0])
                for k, v in b[1].items():
                    add(k, v)
        return toks

    def _wait(self, eng, toks, skip=None):
        for k, v in toks.items():
            if k == skip:
                continue
            if self.seen[eng].get(k, 0) >= v:
                continue
            self.seen[eng][k] = v
            sem = self.semh[k]
            self.ops[eng].append(lambda e, sem=sem, v=v: e.wait_ge(sem, v))
            self.nins += 1

    def _record(self, tok, reads, writes):
        for key in writes:
            self.bufs[key] = [tok, {}]
        k, v = tok
        for key in reads:
            if key in writes:
                continue
            b = self.bufs.setdefault(key, [None, {}])
            if b[1].get(k, 0) < v:
                b[1][k] = v

    def op(self, eng, fn, reads=(), writes=()):
        toks = self._deps(reads, writes)
        self._wait(eng, toks, skip=("pe" if eng == "pe" else None))
        self.latest[eng] += 1
        v = self.latest[eng]
        sem = self.semh[eng]
        self.ops[eng].append(lambda e, fn=fn, sem=sem: fn(e).then_inc(sem, 1))
        self.nins += 1
        self._record((eng, v), reads, writes)

    def dma(self, q, out, in_, reads=(), writes=()):
        toks = self._deps(reads, writes)
        i = self.dcnt[q]
        self.dcnt[q] += 1
        k = f"d_{q}{i % NDQ}"
        val = (i // NDQ + 1) * 16
        if val > 16:
            toks[k] = max(toks.get(k, 0), val - 16)
        self._wait(q, toks)
        sem = self.semh[k]
        self.ops[q].append(lambda e, out=out, in_=in_, sem=sem: e.dma_start(out=out, in_=in_).then_inc(sem, 16))
        self.nins += 1
        self.latest[k] = val
        self._record((k, val), reads, writes)

    def barrier(self):
        toks = {k: v for k, v in self.latest.items() if v > 0}
        for e in self.engs:
            self._wait(e, dict(toks))

    def replay(self, block):
        ops = self.ops

        @block.tensor
        def _(e):
            for f in ops["pe"]:
                f(e)

        @block.scalar
        def _(e):
            for f in ops["act"]:
                f(e)

        @block.vector
        def _(e):
            for f in ops["dve"]:
                f(e)

        @block.gpsimd
        def _(e):
            for f in ops["pool"]:
                f(e)

        @block.sync
        def _(e):
            for f in ops["sp"]:
                f(e)


class Prog:
    def __init__(self, Ls, depth, NB, SBK, debug=False):
        self.Ls, self.L = Ls, depth
        self.HALF = Ls + 128
        self.NT = 2 * self.HALF
        self.NB, self.SBK = NB, SBK
        assert self.NT % NB == 0 and NB % SBK == 0 and SBK <= 512 and self.NT % SBK == 0
        self.NCH = self.NT // 128
        self.GATE_PER_STEP = 2
        self.S5_PROD_ENG = "dve"
        self.MSUB = []
        o = 0
        while o < NB:
            w_ = min(512, NB - o)
            self.MSUB.append((o, w_))
            o += w_
        self.debug = debug
        self.nc = bass.Bass("TRN2", target_bir_lowering=False)
        self.es = ExitStack()
        self.build()

    def din(self, name, shape, dt=F32):
        return self.nc.dram_tensor(name, list(shape), dt, kind="ExternalInput").ap()

    def dscr(self, name, shape, dt):
        kind = "ExternalOutput" if self.debug else "Internal"
        return self.nc.dram_tensor(name, list(shape), dt, kind=kind).ap()

    def sb(self, st, name, shape, dt):
        self._uid = getattr(self, "_uid", 0) + 1
        return st.enter_context(self.nc.sbuf_tensor(f"sb{self._uid}_{name}", list(shape), dt))

    def mm(self, out, lhsT, rhs, start, stop, r, w):
        self.S.op("pe", lambda e: e.matmul(out, lhsT, rhs, start=start, stop=stop), r, w)

    def tr(self, out, in_, ident, r, w):
        self.S.op("pe", lambda e: e.transpose(out, in_, ident), r, w)

    def act(self, out, in_, func, r, w, bias=0.0, scale=1.0):
        self.S.op("act", lambda e: e.activation(out=out, in_=in_, func=func, bias=bias, scale=scale), r, w)

    def tt(self, eng, out, in0, in1, op, r, w):
        self.S.op(eng, lambda e: e.tensor_tensor(out=out, in0=in0, in1=in1, op=op), r, w)

    def ts(self, eng, out, in0, s1, s2, op0, op1, r, w):
        if op1 is None:
            self.S.op(eng, lambda e: e.tensor_scalar(out=out, in0=in0, scalar1=s1, scalar2=None, op0=op0), r, w)
        else:
            self.S.op(eng, lambda e: e.tensor_scalar(out=out, in0=in0, scalar1=s1, scalar2=s2, op0=op0, op1=op1), r, w)

    def stt(self, eng, out, in0, scalar, in1, op0, op1, r, w):
        self.S.op(eng, lambda e: e.scalar_tensor_tensor(out=out, in0=in0, scalar=scalar, in1=in1, op0=op0, op1=op1), r, w)

    def scan(self, out, d0, d1, init, r, w):
        self.S.op("dve", lambda e: e.tensor_tensor_scan(out=out, data0=d0, data1=d1, initial=init, op0=ALU.mult, op1=ALU.add), r, w)

    def cp(self, eng, out, in_, r, w):
        if eng == "act":
            self.S.op("act", lambda e: e.activation(out=out, in_=in_, func=AF.Copy), r, w)
        else:
            self.S.op(eng, lambda e: e.tensor_copy(out=out, in_=in_), r, w)

    def dbg(self, name, ap, shape, dt, r):
        if not self.debug:
            return
        t = self.nc.dram_tensor("dbg_" + name, list(shape), dt, kind="ExternalOutput").ap()
        self.S.dma("sp", t, ap, r, ["dbg_" + name])

    def recip(self, eng, out, in_, r, w):
        self.S.op(eng, lambda e: e.reciprocal(out=out, in_=in_), r, w)

    def sin_of(self, out, x, shift, tf, ti, r, w):
        k = ["_sincos"]
        if shift != 0.0:
            self.ts("dve", tf, x, shift, None, ALU.add, None, r + k, k)
            xs = tf
        else:
            xs = x
        self.ts("dve", out, xs, 1.0 / TWO_PI, None, ALU.mult, None, r + k, w)
        self.cp("dve", ti, out, w, k)
        self.cp("dve", out, ti, k, w)
        self.stt("dve", out, out, -TWO_PI, xs, ALU.mult, ALU.add, r + k + list(w), w)
        self.ts("dve", out, out, -3.141592, 3.141592, ALU.max, ALU.min, w, w)
        self.act(out, out, AF.Sin, w, w)

    def memset(self, eng, ap, val, w):
        self.S.op(eng, lambda e: e.memset(ap, val), (), w)

    def build(self):
        nc, es, L, NT = self.nc, self.es, self.L, self.NT
        I = self.I = {}
        I["xT"] = self.din("xT", [D, NT])
        I["carry"] = self.din("carry", [128, 1])
        I["rmask"] = self.din("rmask", [128, NT])
        I["kmask"] = self.din("kmask", [128, NT + 1])
        I["consts"] = self.din("consts", [128, 6, 128])
        I["w_in"] = self.din("w_in", [L, D, N_IN])
        I["w_out_a"] = self.din("w_out_a", [L, 512, D])
        I["w_out_b"] = self.din("w_out_b", [L, 512, D])
        I["w_out_c"] = self.din("w_out_c", [L, 1024, D])
        I["w_o"] = self.din("w_o", [L, D, D])
        I["w_glu"] = self.din("w_glu", [L, 512, 512])
        I["lru_w"] = self.din("lru_w", [L, 8, 4, 128, 128])
        I["normg"] = self.din("normg", [128, L + 1, 16])
        I["lrup"] = self.din("lrup", [128, L, 8, 11])
        I["wgp"] = self.din("wgp", [L, 2, 32, 256])
        I["glap"] = self.din("glap", [128, L, 8])
        I["s5s"] = self.din("s5s", [128, L, 3, 2, 16])
        I["s5r"] = self.din("s5r", [L, 3, 2, 2048])
        I["bexp"] = self.din("bexp", [L, 2, 128, 16, 128])
        I["cexp"] = self.din("cexp", [L, 2, 2, 128, 16, 128])
        I["s5p"] = self.din("s5p", [128, L, 8])
        self.yT = nc.dram_tensor("yT", [D, NT], F32, kind="ExternalOutput").ap()

        X = self.X = {}
        X["zT"] = self.dscr("zT", [D, NT], F32)
        X["hS"] = self.dscr("hS", [D, NT], BF16)
        X["uS"] = self.dscr("uS", [512, NT], BF16)
        X["qS"] = self.dscr("qS", [256, NT], BF16)
        X["kS"] = self.dscr("kS", [256, NT], BF16)
        X["vS"] = self.dscr("vS", [NT, 512], BF16)
        X["gS"] = self.dscr("gS", [32, NT], BF16)
        X["xS"] = self.dscr("xS", [1024, NT], BF16)
        X["yfS"] = self.dscr("yfS", [512, NT], F32)
        X["yaS"] = self.dscr("yaS", [512, NT], BF16)
        X["ybS"] = self.dscr("ybS", [512, NT], BF16)
        X["ycS"] = self.dscr("ycS", [1024, NT], BF16)
        X["sgS"] = self.dscr("sgS", [2048, NT], BF16)
        X["mgS"] = self.dscr("mgS", [3 * 2048, NT], BF16)

        self.S = Sched(nc, es)
        top = es
        self.ps = [top.enter_context(nc.psum_tensor(f"ps{i}", [128, 512], F32)) for i in range(7)]
        self.cst = self.sb(top, "cst", [128, 6, 128], F32)
        self.identb = self.sb(top, "identb", [128, 128], BF16)
        self.ones = self.sb(top, "ones", [128, 128], F32)
        self.normg = self.sb(top, "normg", [128, L + 1, 16], F32)
        self.carry = self.sb(top, "carry", [128, 1], F32)
        S = self.S
        S.dma("sp", self.cst[:], I["consts"], (), ["cst"])
        S.dma("pool", self.identb[:], I["consts"][:, 0, :], (), ["identb"])
        S.dma("sp", self.normg[:], I["normg"], (), ["normg"])
        S.dma("sp", self.carry[:], I["carry"], (), ["carry"])
        self.memset("dve", self.ones[:], 1.0, ["ones"])

        for l in range(L):
            zin = I["xT"] if l == 0 else X["zT"]
            self.phase1(l, zin)
            S.barrier()
            self.phase_lru(l)
            S.barrier()
            self.phase_gla(l)
            S.barrier()
            self.phase_s5(l)
            S.barrier()
            self.phase3(l, zin, last=(l == L - 1))
            S.barrier()
        self.phase4()
        S.barrier()
        with nc.Block() as block:
            S.replay(block)

    def wring_init(self, st, nbuf=4, cols=256):
        self.wr = [self.sb(st, f"wr{i}", [128, 16, cols], BF16) for i in range(nbuf)]
        self.wri = 0

    def wload(self, src, kc, ncols):
        i = self.wri % len(self.wr)
        self.wri += 1
        t = self.wr[i]
        self.S.dma("pool", t[:, 0:kc, 0:ncols], src.rearrange("(kc p) n -> p kc n", p=128), (), [("wr", i)])
        return t, ("wr", i)

    def rms_rstd(self, zt, zkey, sq, rstd, n, pbank, dim):
        S = self.S
        self.act(sq[:, :, 0:n], zt[:, :, 0:n], AF.Square, [zkey], ["sq"])
        for kc in range(KC):
            self.mm(self.ps[pbank][:, 0:n], self.ones[:], sq[:, kc, 0:n], kc == 0, kc == KC - 1,
                    ["ones", "sq"], [("ps", pbank)])
        self.act(rstd[:, 0:n], self.ps[pbank][:, 0:n], AF.Sqrt, [("ps", pbank)], ["rstd"], bias=EPS, scale=1.0 / dim)
        self.recip("dve", rstd[:, 0:n], rstd[:, 0:n], ["rstd"], ["rstd"])

    def phase1(self, l, zin):
        S, I, X, NB, SBK, NT = self.S, self.I, self.X, self.NB, self.SBK, self.NT
        nsub = NB // SBK
        w_in = I["w_in"]
        with ExitStack() as st:
            hT2 = [self.sb(st, f"hT{i}", [128, KC, NB], BF16) for i in range(2)]
            zt = [self.sb(st, f"zt{i}", [128, KC, SBK], F32) for i in range(2)]
            sq = self.sb(st, "sq", [128, KC, SBK], F32)
            rstd = self.sb(st, "rstd", [128, SBK], F32)
            og = [self.sb(st, f"og{i}", [128, NB], BF16) for i in range(3)]
            ov = [self.sb(st, f"ov{i}", [128, 512], BF16) for i in range(2)]
            wv = self.sb(st, "wv", [128, KC, 512], BF16)
            self.wring_init(st)
            S.dma("pool", wv[:], w_in[l, :, C_V:C_V + 512].rearrange("(kc p) n -> p kc n", p=128), (), ["wv"])
            ogi = 0
            ovi = 0
            pb = 0
            def norm_block(b):
                t0 = b * NB
                hT = hT2[b % 2]
                for s in range(nsub):
                    z = zt[s % 2]
                    zk = ("zt", s % 2)
                    c0 = t0 + s * SBK
                    S.dma("sp", z[:], zin[:, c0:c0 + SBK].rearrange("(kc p) t -> p kc t", p=128),
                          [("zT", b)] if zin is X["zT"] else (), [zk])
                    self.rms_rstd(z, zk, sq, rstd, SBK, 6, D)
                    for kc in range(KC):
                        self.stt("dve", hT[:, kc, s * SBK:(s + 1) * SBK], z[:, kc, :], self.normg[:, l, kc:kc + 1],
                                 rstd[:], ALU.mult, ALU.mult, [zk, "rstd", "normg"], [("hT", b % 2, s)])
                hk_ = [("hT", b % 2, s) for s in range(nsub)]
                S.dma("sp", X["hS"][:, t0:t0 + NB].rearrange("(kc p) t -> p kc t", p=128), hT[:], hk_, [("hS", b)])

            nblk = NT // NB
            norm_block(0)
            for b in range(nblk):
                t0 = b * NB
                hT = hT2[b % 2]
                hkeys = [("hT", b % 2, s) for s in range(nsub)]
                gcount = 0
                groups = [(C_U, 512, "uS", 0), (C_Q, 256, "qS", 0), (C_K, 256, "kS", 0),
                          (C_X, 256, "xS", 0), (C_X + 256, 256, "xS", 256), (C_X + 512, 256, "xS", 512),
                          (C_X + 768, 256, "xS", 768), (C_GLR, 32, "gS", 0)]
                groups = [(C_U, 256, "uS", 0), (C_U + 256, 256, "uS", 256)] + groups[1:]
                for (c0, ncols, dst, r0) in groups:
                    gcount += 1
                    if gcount == 4 and b + 1 < nblk:
                        norm_block(b + 1)
                    wt, wk = self.wload(w_in[l, :, c0:c0 + ncols], KC, ncols)
                    for m0 in range(0, ncols, 128):
                        mw = min(128, ncols - m0)
                        o = og[ogi % 3]
                        ok = ("og", ogi % 3)
                        ogi += 1
                        for (so, sw) in self.MSUB:
                            bank = pb % 6
                            pb += 1
                            for kc in range(KC):
                                self.mm(self.ps[bank][0:mw, 0:sw], wt[:, kc, m0:m0 + mw], hT[:, kc, so:so + sw],
                                        kc == 0, kc == KC - 1, [wk] + hkeys, [("ps", bank)])
                            eng = "act" if (pb % 2) else "dve"
                            self.cp(eng, o[0:mw, so:so + sw], self.ps[bank][0:mw, 0:sw], [("ps", bank)], [ok])
                        S.dma("sp", X[dst][r0 + m0:r0 + m0 + mw, t0:t0 + NB], o[0:mw, :], [ok], [(dst, b)])
                tt0 = 0
                while tt0 < NB:
                    tw = min(128, NB - tt0)
                    bank = pb % 6
                    pb += 1
                    for kc in range(KC):
                        self.mm(self.ps[bank][0:tw, 0:512], hT[:, kc, tt0:tt0 + tw], wv[:, kc, :], kc == 0, kc == KC - 1,
                                ["wv"] + hkeys, [("ps", bank)])
                    o = ov[ovi % 2]
                    ok = ("ov", ovi % 2)
                    ovi += 1
                    self.cp("act" if (pb % 2) else "dve", o[0:tw, :], self.ps[bank][0:tw, 0:512], [("ps", bank)], [ok])
                    S.dma("sp", X["vS"][t0 + tt0:t0 + tt0 + tw, :], o[0:tw, :], [ok], [("vS", b)])
                    tt0 += tw

    def phase_lru(self, l):
        S, I, X, NT, HALF, SBK = self.S, self.I, self.X, self.NT, self.HALF, self.SBK
        nch = NT // SBK
        with ExitStack() as st:
            xr = self.sb(st, "l_xr", [128, NT + 4], BF16)
            acc = self.sb(st, "l_acc", [128, NT], F32)
            xcb = self.sb(st, "l_xcb", [128, NT], BF16)
            rmask = self.sb(st, "l_rm", [128, NT], F32)
            a_t = self.sb(st, "l_a", [128, NT], F32)
            bt = self.sb(st, "l_bt", [128, NT], F32)
            hf = self.sb(st, "l_hf", [128, NT], F32)
            hb = self.sb(st, "l_hb", [128, NT], F32)
            rfull = self.sb(st, "l_rf", [128, NT], F32)
            ifull = self.sb(st, "l_if", [128, NT], F32)
            w4 = [self.sb(st, f"l_w{i}", [128, 4, 128], BF16) for i in range(2)]
            prm = self.sb(st, "l_prm", [128, 8, 11], F32)
            sc = self.sb(st, "l_sc", [128, 8, 4], F32)
            ini = self.sb(st, "l_ini", [128, 2], F32)
            nlg = self.n_lru_gate_blocks()
            if nlg > 0:
                ps7 = st.enter_context(self.nc.psum_tensor(f"ps7l_{l}", [128, 512], F32))
                ggen = self.gate_gen(l, st, [(ps7, ("ps", 7)), (self.ps[6], ("ps", 6))], self.gate_blocks()[:nlg])
            else:
                ggen = iter(())
            g_per_hook = (nlg * 32 + 15) // 16
            hook_every = max(1, nch // max(1, g_per_hook))
            hooks_left = [g_per_hook]
            S.dma("sp", rmask[:], I["rmask"], (), ["l_rm"])
            S.dma("sp", prm[:], I["lrup"][:, l], (), ["l_prm"])
            self.act(sc[:, :, 0:2], prm[:, :, 9:11], AF.Exp, ["l_prm"], ["l_sc"], scale=-1.0)
            self.act(sc[:, :, 0:2], sc[:, :, 0:2], AF.Ln, ["l_sc"], ["l_sc"], bias=1.0)
            self.ts("dve", sc[:, :, 2:4], sc[:, :, 0:2], -16.0, None, ALU.mult, None, ["l_sc"], ["l_sc2"])
            self.ts("dve", sc[:, :, 0:2], sc[:, :, 0:2], -8.0, None, ALU.mult, None, ["l_sc", "l_sc2"], ["l_sc"])
            self.memset("dve", xr[:, 0:2], 0.0, ["l_xr"])
            self.memset("dve", xr[:, NT + 2:NT + 4], 0.0, ["l_xr"])
            pb = 0
            ri = 0
            for n in range(8):
                w = w4[n % 2]
                wk = ("l_w", n % 2)
                S.dma("pool", w[:], I["lru_w"][l, n].rearrange("f c d -> c f d"), (), [wk])
                S.dma("sp", xr[:, 2:2 + NT], X["xS"][n * 128:(n + 1) * 128, :], ["xS_all"], ["l_xr"])
                self.ts("dve", acc[:], xr[:, 0:NT], prm[:, n, 0:1], prm[:, n, 4:5], ALU.mult, ALU.add, ["l_xr", "l_prm"], ["l_acc"])
                for j in range(1, 4):
                    self.stt("dve", acc[:], xr[:, j:j + NT], prm[:, n, j:j + 1], acc[:], ALU.mult, ALU.add,
                             ["l_xr", "l_prm", "l_acc"], ["l_acc"])
                self.tt("dve", acc[:], acc[:], rmask[:], ALU.mult, ["l_acc", "l_rm"], ["l_acc"])
                self.cp("act", xcb[:], acc[:], ["l_acc"], ["l_xcb"])
                for d in range(2):
                    for c in range(nch):
                        cs = slice(c * SBK, (c + 1) * SBK)
                        b0 = pb % 6
                        b1 = (pb + 1) % 6
                        pb += 2
                        self.mm(self.ps[b0][:, 0:SBK], w[:, 2 * d, :], xcb[:, cs], True, True, [wk, "l_xcb"], [("ps", b0)])
                        self.mm(self.ps[b1][:, 0:SBK], w[:, 2 * d + 1, :], xcb[:, cs], True, True, [wk, "l_xcb"], [("ps", b1)])
                        self.act(rfull[:, cs], self.ps[b0][:, 0:SBK], AF.Sigmoid, [("ps", b0), "l_prm"], [("l_rf", c)], bias=prm[:, n, 5 + d:6 + d])
                        self.act(ifull[:, cs], self.ps[b1][:, 0:SBK], AF.Sigmoid, [("ps", b1), "l_prm"], [("l_if", c)], bias=prm[:, n, 7 + d:8 + d])
                        if (c + 1) % hook_every == 0 and hooks_left[0] > 0:
                            hooks_left[0] -= 1
                            next(ggen, None)
                    rk = [("l_rf", c) for c in range(nch)]
                    ik = [("l_if", c) for c in range(nch)]
                    self.act(a_t[:], rfull[:], AF.Exp, rk + ["l_sc"], ["l_a"], scale=sc[:, n, d:d + 1])
                    self.act(rfull[:], rfull[:], AF.Exp, rk + ["l_sc2"], ["l_rf"] + rk, scale=sc[:, n, 2 + d:3 + d])
                    self.act(rfull[:], rfull[:], AF.Sqrt, ["l_rf"], ["l_rf"], bias=1.0, scale=-1.0)
                    self.tt("dve", bt[:], ifull[:], acc[:], ALU.mult, ik + ["l_acc"], ["l_bt"])
                    self.tt("dve", bt[:], bt[:], rfull[:], ALU.mult, ["l_rf", "l_bt"], ["l_bt"])
                    for c in range(nch):
                        self.S.bufs[("l_rf", c)] = self.S.bufs["l_rf"]
                    hooks_left[0] = g_per_hook
                    akeys = ["l_a", "l_bt"]
                    H = HALF
                    if d == 0:
                        self.scan(hf[:, 0:H], a_t[:, 0:H], bt[:, 0:H], 0.0, akeys, ["l_hf"])
                        self.tt("dve", ini[:, 0:1], hf[:, H - 1:H], self.carry[:], ALU.mult, ["l_hf", "carry"], ["l_ini0"])
                        self.scan(hf[:, H:NT], a_t[:, H:NT], bt[:, H:NT], ini[:, 0:1], akeys + ["l_ini0", "l_hf"], ["l_hf"])
                    else:
                        self.scan(hb[:, H:NT][:, ::-1], a_t[:, H:NT][:, ::-1], bt[:, H:NT][:, ::-1], 0.0, akeys, ["l_hb"])
                        self.tt("dve", ini[:, 1:2], hb[:, H:H + 1], self.carry[:], ALU.mult, ["l_hb", "carry"], ["l_ini1"])
                        self.scan(hb[:, 0:H][:, ::-1], a_t[:, 0:H][:, ::-1], bt[:, 0:H][:, ::-1], ini[:, 1:2],
                                  akeys + ["l_ini1", "l_hb"], ["l_hb"])
                self.tt("dve", xcb[:], hf[:], hb[:], ALU.add, ["l_hf", "l_hb"], ["l_xcb"])
                S.dma("sp", X["ycS"][n * 128:(n + 1) * 128, :], xcb[:], ["l_xcb"], ["ycS_all"])
            for _ in ggen:
                pass

    def phase_gla(self, l):
        S, I, X, NT, HALF, SBK, NCH = self.S, self.I, self.X, self.NT, self.HALF, self.SBK, self.NCH
        nch = NT // SBK
        HC = NCH // 2
        cst = self.cst
        with ExitStack() as st:
            self.psb = st.enter_context(self.nc.psum_tensor(f"psb_{l}", [128, 1024], BF16))
            vt = self.sb(st, "g_v", [128, NCH, 512], BF16)
            glr = self.sb(st, "g_glr", [32, NT], BF16)
            wg = self.sb(st, "g_wg", [32, 2, 256], BF16)
            gp = self.sb(st, "g_gp", [128, 8], F32)
            nbg = self.sb(st, "g_nbg", [128, 4], F32)
            km = self.sb(st, "g_km", [128, NT + 1], F32)
            q = self.sb(st, "g_q", [128, NT], BF16)
            k = self.sb(st, "g_k", [128, NT], BF16)
            bb = self.sb(st, "g_b", [128, NT], F32)
            ex = self.sb(st, "g_ex", [128, NT], F32)
            qe = [self.sb(st, f"g_qe{d}", [128, NT], BF16) for d in range(2)]
            ke = [self.sb(st, f"g_ke{d}", [128, NT], BF16) for d in range(2)]
            kd = self.sb(st, "g_kd", [128, NT], BF16)
            nbl = self.sb(st, "g_nbl", [128, NCH], F32)
            edec = self.sb(st, "g_edec", [128, NCH], F32)
            sbf = [self.sb(st, f"g_sbf{d}", [128, NCH, 128], BF16) for d in range(2)]
            scur = [self.sb(st, f"g_sc{i}", [128, 128], F32) for i in range(2)]
            kdt = [self.sb(st, f"g_kdt{i}", [128, 128], BF16) for i in range(2)]
            attm = [self.sb(st, f"g_att{i}", [128, 128], BF16) for i in range(4)]
            osb = [self.sb(st, f"g_o{i}", [128, 128], F32) for i in range(2)]
            osq = [self.sb(st, f"g_osq{i}", [128, 128], F32) for i in range(2)]
            ors = [self.sb(st, f"g_ors{i}", [128, 128], F32) for i in range(2)]
            yo = [self.sb(st, "g_yo0", [128, NT], BF16)] * 2
            dsall = self.sb(st, "g_ds", [128, NCH, 128], F32)
            psbs = [slice(0, 128), slice(512, 640)]
            S.dma("sp", vt[:], X["vS"].rearrange("(c p) v -> p c v", p=128), ["vS_all"], ["g_v"])
            S.dma("sp", glr[:], X["gS"], ["gS_all"], ["g_glr"])
            S.dma("pool", wg[:], I["wgp"][l].rearrange("d k n -> k d n"), (), ["g_wg"])
            S.dma("sp", gp[:], I["glap"][:, l], (), ["g_gp"])
            S.dma("sp", km[:], I["kmask"], (), ["g_km"])
            self.ts("dve", nbg[:], gp[:, 0:4], -1.0, None, ALU.mult, None, ["g_gp"], ["g_nbg"])
            pb = 0
            ai = 0
            oi = 0
            ki = 0
            for r in range(2):
                S.dma("sp", q[:], X["qS"][r * 128:(r + 1) * 128, :], ["qS_all"], ["g_q"])
                S.dma("sp", k[:], X["kS"][r * 128:(r + 1) * 128, :], ["kS_all"], ["g_k"])
                for d in range(2):
                    for c in range(nch):
                        cs = slice(c * SBK, (c + 1) * SBK)
                        bank = pb % 6
                        pb += 1
                        self.mm(self.ps[bank][:, 0:SBK], wg[:, d, r * 128:(r + 1) * 128], glr[:, cs], True, True,
                                ["g_wg", "g_glr"], [("ps", bank)])
                        self.act(ex[:, cs], self.ps[bank][:, 0:SBK], AF.Exp, [("ps", bank), "g_nbg"], [("g_ex", c)],
                                 bias=nbg[:, 2 * d + r:2 * d + r + 1], scale=-1.0)
                    exk = [("g_ex", c) for c in range(nch)]
                    self.act(ex[:], ex[:], AF.Ln, exk, ["g_ex"], bias=1.0)
                    if d == 0:
                        self.scan(bb[:], km[:, 0:NT], ex[:], 0.0, ["g_km", "g_ex"] + exk, ["g_b"])
                        self.cp("dve", nbl[:], bb[:, 127:NT:128], ["g_b"], ["g_nbl"])
                    else:
                        self.scan(bb[:, ::-1], km[:, 1:NT + 1][:, ::-1], ex[:, ::-1], 0.0, ["g_km", "g_ex"] + exk, ["g_b"])
                        self.cp("dve", nbl[:], bb[:, 0:NT:128], ["g_b"], ["g_nbl"])
                    self.ts("dve", nbl[:], nbl[:], -1.0 / 16.0, None, ALU.mult, None, ["g_nbl"], ["g_nbl"])
                    self.act(edec[:], nbl[:], AF.Exp, ["g_nbl"], ["g_edec"])
                    self.act(ex[:], bb[:], AF.Exp, ["g_b", "g_ex"], ["g_ex"], bias=math.log(0.125), scale=-1.0 / 16.0)
                    self.tt("dve", qe[d][:], q[:], ex[:], ALU.mult, ["g_q", "g_ex"], [("g_qe", d)])
                    self.act(ex[:], bb[:], AF.Exp, ["g_b", "g_ex"], ["g_ex"], scale=1.0 / 16.0)
                    self.tt("dve", ke[d][:], k[:], ex[:], ALU.mult, ["g_k", "g_ex"], [("g_ke", d)])
                    for c in range(NCH):
                        cs = slice(c * 128, (c + 1) * 128)
                        self.act(ex[:, cs], bb[:, cs], AF.Exp, ["g_b", "g_nbl", "g_ex"], ["g_ex"], bias=nbl[:, c:c + 1], scale=1.0 / 16.0)
                    self.tt("dve", kd[:], k[:], ex[:], ALU.mult, ["g_k", "g_ex"], ["g_kd"])
                    for c in range(NCH):
                        cs = slice(c * 128, (c + 1) * 128)
                        pbk = psbs[ki % 2]
                        self.tr(self.psb[:, pbk], kd[:, cs], self.identb[:], ["g_kd", "identb"], [("psb", ki % 2)])
                        kt = kdt[ki % 2]
                        kk = ("g_kdt", ki % 2)
                        self.cp("act", kt[:], self.psb[:, pbk], [("psb", ki % 2)], [kk])
                        ki += 1
                        bank = pb % 6
                        pb += 1
                        self.mm(self.ps[bank][:, 0:256], kt[:], vt[:, c, r * 256:(r + 1) * 256], True, True, [kk, "g_v"], [("ps", bank)])
                        for hp in range(2):
                            self.cp("act" if hp == 0 else "dve", dsall[hp * 64:(hp + 1) * 64, c, :],
                                    self.ps[bank][hp * 64:(hp + 1) * 64, hp * 128:(hp + 1) * 128], [("ps", bank)], [("g_ds", c)])
                    cur = 0
                    self.memset("dve", scur[0][:], 0.0, [("g_sc", 0)])
                    order = range(NCH) if d == 0 else range(NCH - 1, -1, -1)
                    for c in order:
                        if (d == 0 and c == HC) or (d == 1 and c == HC - 1):
                            self.ts("dve", scur[cur][:], scur[cur][:], self.carry[:, 0:1], None, ALU.mult, None,
                                    [("g_sc", cur), "carry"], [("g_sc", cur)])
                        self.cp("dve", sbf[d][:, c, :], scur[cur][:], [("g_sc", cur)], [("g_sbf", d)])
                        nxt = 1 - cur
                        self.stt("dve", scur[nxt][:], scur[cur][:], edec[:, c:c + 1], dsall[:, c, :],
                                 ALU.mult, ALU.add, [("g_sc", cur), "g_edec", ("g_ds", c)], [("g_sc", nxt)])
                        cur = nxt
                for hp in range(2):
                    h = 2 * r + hp
                    ps_ = slice(hp * 64, (hp + 1) * 64)
                    y = yo[hp]
                    yk = ("g_yo", 0)
                    for c in range(NCH):
                        cs = slice(c * 128, (c + 1) * 128)
                        ats = []
                        for d in range(2):
                            bank = pb % 6
                            pb += 1
                            self.mm(self.ps[bank][:, 0:128], ke[d][ps_, cs], qe[d][ps_, cs], True, True,
                                    [("g_ke", d), ("g_qe", d)], [("ps", bank)])
                            at = attm[ai % 4]
                            ak = ("g_att", ai % 4)
                            ai += 1
                            self.tt("dve", at[:], self.ps[bank][:, 0:128], cst[:, 1 + d, :], ALU.mult, [("ps", bank), "cst"], [ak])
                            ats.append((at, ak))
                        bank = pb % 6
                        pb += 1
                        ob = self.ps[bank][:, 0:128]
                        self.mm(ob, vt[:, c, h * 128:(h + 1) * 128], ats[0][0][:], True, False, ["g_v", ats[0][1]], [("ps", bank)])
                        self.mm(ob, vt[:, c, h * 128:(h + 1) * 128], ats[1][0][:], False, False, ["g_v", ats[1][1]], [("ps", bank)])
                        self.mm(ob, sbf[0][ps_, c, :], qe[0][ps_, cs], False, False, [("g_sbf", 0), ("g_qe", 0)], [("ps", bank)])
                        self.mm(ob, sbf[1][ps_, c, :], qe[1][ps_, cs], False, True, [("g_sbf", 1), ("g_qe", 1)], [("ps", bank)])
                        o2 = osq[oi % 2]
                        o2k = ("g_osq", oi % 2)
                        oi += 1
                        self.cp("act", bb[:, cs], ob, [("ps", bank), "g_b"], [("g_of", c)])
                        self.act(o2[:], ob, AF.Square, [("ps", bank)], [o2k])
                        b2 = 6
                        self.mm(self.ps[b2][:, 0:128], self.ones[:], o2[:], True, True, ["ones", o2k], [("ps", b2)])
                        self.cp("dve", ex[:, cs], self.ps[b2][:, 0:128], [("ps", b2), "g_ex"], [("g_sf", c)])
                    ofk = [("g_of", c) for c in range(NCH)]
                    sfk = [("g_sf", c) for c in range(NCH)]
                    self.act(ex[:], ex[:], AF.Ln, sfk, ["g_ex"] + sfk, bias=EPS, scale=1.0 / 128.0)
                    self.act(ex[:], ex[:], AF.Exp, ["g_ex"], ["g_ex"] + sfk, scale=-0.5)
                    self.stt("dve", y[:], bb[:], gp[:, 4 + h:5 + h], ex[:], ALU.mult, ALU.mult, ofk + sfk + ["g_ex", "g_gp"], [yk])
                    self.S.bufs["g_b"] = [None, {"dve": self.S.latest["dve"]}]
                    self.S.bufs["g_ex"] = [None, {"dve": self.S.latest["dve"]}]
                    for c in range(NCH):
                        self.S.bufs[("g_ex", c)] = self.S.bufs["g_ex"]
                    S.dma("sp", X["ybS"][h * 128:(h + 1) * 128, :], y[:], [yk], ["ybS_all"])

    def phase_s5(self, l):
        S, I, X, NT, HALF, NCH = self.S, self.I, self.X, self.NT, self.HALF, self.NCH
        HC = NCH // 2
        cst = self.cst
        jidx = cst[:, 3, :]
        PI = math.pi
        with ExitStack() as st:
            cs_t = self.sb(st, "s_cs", [128, 2, 16, 128], F32)
            sn_t = self.sb(st, "s_sn", [128, 2, 16, 128], F32)
            rt = self.sb(st, "s_rt", [128, 2, 16, 128], F32)
            bbw = self.sb(st, "s_bbw", [128, 2, 2, 16, 128], BF16)
            cw = self.sb(st, "s_cw", [128, 2, 2, 16, 128], BF16)
            cwn = self.sb(st, "s_cwn", [128, 2, 16, 128], BF16)
            rho = self.sb(st, "s_rho", [128, 2, 16], F32)
            sp_ = self.sb(st, "s_sp", [128, 8], F32)
            wglu = self.sb(st, "s_wglu", [128, 4, 512], BF16)
            S.dma("sp", sp_[:], I["s5p"][:, l], (), ["s_sp"])
            S.dma("pool", wglu[:], I["w_glu"][l].rearrange("(kc p) n -> p kc n", p=128), (), ["s_wglu"])
            for d in range(2):
                for part in range(2):
                    S.dma("pool", cw[:, d, part], I["cexp"][l, d, part], (), ["s_cw"])
            for d in range(2):
                self.ts("dve", cw[:, d, 1], cw[:, d, 1], -1.0, None, ALU.mult, None, ["s_cw"], ["s_cw"])
                self.ts("dve", cwn[:, d], cw[:, d, 0], -1.0, None, ALU.mult, None, ["s_cw"], ["s_cw"])
            with ExitStack() as st2:
                prm = self.sb(st2, "s_prm", [128, 3, 2, 16], F32)
                dt = self.sb(st2, "s_dt", [128, 2, 16], F32)
                th = self.sb(st2, "s_th", [128, 2, 16], F32)
                ang = self.sb(st2, "s_ang", [128, 2, 16, 128], F32)
                tmp = self.sb(st2, "s_tmp", [128, 2, 16, 128], F32)
                S.dma("sp", prm[:], I["s5s"][:, l], (), ["s_prm"])
                self.act(dt[:], prm[:, 2], AF.Exp, ["s_prm"], ["s_dt"])
                self.tt("dve", th[:], prm[:, 1], dt[:], ALU.mult, ["s_prm", "s_dt"], ["s_th"])
                self.tt("dve", dt[:], prm[:, 0], dt[:], ALU.mult, ["s_prm", "s_dt"], ["s_dt"])
                if l == 0:
                    self.dbg("prm", prm[:], [128, 3, 2, 16], F32, ["s_prm"])
                    self.dbg("th", th[:], [128, 2, 16], F32, ["s_th"])
                    self.dbg("dt2", dt[:], [128, 2, 16], F32, ["s_dt"])
                self.act(rho[:], dt[:], AF.Exp, ["s_dt"], ["s_rho"])
                if l == 0:
                    self.dbg("rho0", rho[:], [128, 2, 16], F32, ["s_rho"])
                for d in range(2):
                    for p in range(16):
                        self.ts("dve", ang[:, d, p, :], jidx, th[:, d, p:p + 1], None, ALU.mult, None, ["cst", "s_th"], ["s_ang"])
                        self.ts("dve", rt[:, d, p, :], cst[:, 4, :], rho[:, d, p:p + 1], None, ALU.mult, None, ["cst", "s_rho"], ["s_rt"])
                itmp = self.sb(st2, "s_itmp", [128, 2, 16, 128], mybir.dt.int32)
                self.sin_of(sn_t[:], ang[:], 0.0, tmp[:], itmp[:], ["s_ang"], ["s_sn"])
                self.sin_of(cs_t[:], ang[:], 0.5 * PI, tmp[:], itmp[:], ["s_ang"], ["s_cs"])
            S.barrier()
            with ExitStack() as st2:
                rw = self.sb(st2, "s_rw", [128, 3, 2048], F32)
                e1 = self.sb(st2, "s_e1", [128, 2048], F32)
                e2 = self.sb(st2, "s_e2", [128, 2048], F32)
                e3 = self.sb(st2, "s_e3", [128, 2048], F32)
                e4 = self.sb(st2, "s_e4", [128, 2048], F32)
                e5 = self.sb(st2, "s_e5", [128, 2048], F32)
                e6 = self.sb(st2, "s_e6", [128, 2048], F32)
                ei = self.sb(st2, "s_ei", [128, 2048], mybir.dt.int32)
                bx = self.sb(st2, "s_bx", [128, 2, 16, 128], F32)
                S.dma("sp", bx[:], I["bexp"][l].rearrange("a c p s -> c a p s"), (), ["s_bx"])
                for d in range(2):
                    for i3 in range(3):
                        S.dma("sp", rw[:, i3, :], I["s5r"][l, i3, d:d + 1, :].partition_broadcast(128), (), ["s_rw"])
                    lr, li = rw[:, 0, :], rw[:, 1, :]
                    R_, W_ = ["s_rw", "s_e"], ["s_e"]
                    self.act(e1[:], rw[:, 2, :], AF.Exp, R_, W_)
                    self.tt("dve", e2[:], li, e1[:], ALU.mult, R_, W_)
                    self.tt("dve", e1[:], lr, e1[:], ALU.mult, R_, W_)
                    self.act(e1[:], e1[:], AF.Exp, R_, W_)
                    self.sin_of(e3[:], e2[:], 0.0, e6[:], ei[:], R_, W_)
                    self.sin_of(e4[:], e2[:], 0.5 * PI, e6[:], ei[:], R_, W_)
                    self.tt("dve", e3[:], e3[:], e1[:], ALU.mult, R_, W_)
                    self.tt("dve", e4[:], e4[:], e1[:], ALU.mult, R_, W_)
                    self.ts("dve", e4[:], e4[:], -1.0, None, ALU.add, None, R_, W_)
                    self.tt("dve", e5[:], lr, lr, ALU.mult, R_, W_)
                    self.tt("dve", e6[:], li, li, ALU.mult, R_, W_)
                    self.tt("dve", e5[:], e5[:], e6[:], ALU.add, R_, W_)
                    self.S.op("dve", lambda e, o=e5[:]: e.reciprocal(out=o, in_=o), R_, W_)
                    self.tt("dve", e1[:], e4[:], lr, ALU.mult, R_, W_)
                    self.tt("dve", e6[:], e3[:], li, ALU.mult, R_, W_)
                    self.tt("dve", e1[:], e1[:], e6[:], ALU.add, R_, W_)
                    self.tt("dve", e1[:], e1[:], e5[:], ALU.mult, R_, W_)
                    self.tt("dve", e2[:], e3[:], lr, ALU.mult, R_, W_)
                    self.tt("dve", e6[:], e4[:], li, ALU.mult, R_, W_)
                    self.tt("dve", e2[:], e2[:], e6[:], ALU.subtract, R_, W_)
                    self.tt("dve", e2[:], e2[:], e5[:], ALU.mult, R_, W_)
                    fr = e1[:].rearrange("c (p s) -> c p s", p=16)
                    fi = e2[:].rearrange("c (p s) -> c p s", p=16)
                    t1 = e3[:].rearrange("c (p s) -> c p s", p=16)
                    t2 = e4[:].rearrange("c (p s) -> c p s", p=16)
                    RB = R_ + ["s_bx"]
                    self.tt("dve", t1, fr, bx[:, 0], ALU.mult, RB, W_)
                    self.tt("dve", t2, fi, bx[:, 1], ALU.mult, RB, W_)
                    self.tt("dve", bbw[:, d, 0], t1, t2, ALU.subtract, R_, ["s_bbw"])
                    self.tt("dve", t1, fr, bx[:, 1], ALU.mult, RB + ["s_bbw"], W_)
                    self.tt("dve", t2, fi, bx[:, 0], ALU.mult, RB, W_)
                    self.tt("dve", bbw[:, d, 1], t1, t2, ALU.add, R_, ["s_bbw"])
            S.barrier()
            if l == 0:
                self.dbg("cs", cs_t[:], [128, 2, 16, 128], F32, ["s_cs"])
                self.dbg("sn", sn_t[:], [128, 2, 16, 128], F32, ["s_sn"])
                self.dbg("rt", rt[:], [128, 2, 16, 128], F32, ["s_rt"])
                self.dbg("rho", rho[:], [128, 2, 16], F32, ["s_rho"])
                self.dbg("bbw", bbw[:], [128, 2, 2, 16, 128], BF16, ["s_bbw"])
                self.dbg("cw", cw[:], [128, 2, 2, 16, 128], BF16, ["s_cw"])
            with ExitStack() as st2:
                ufr = [self.sb(st2, f"s_uf{i}", [128, 4, 128], BF16) for i in range(3)]
                ps7 = st2.enter_context(self.nc.psum_tensor(f"ps7_{l}", [128, 512], F32))
                gblocks = self.gate_blocks()[self.n_lru_gate_blocks():]
                ggen = self.gate_gen(l, st2, [(ps7, ("ps", 7)), (self.ps[6], ("ps", 6))], gblocks)
                grate = len(gblocks) * 32.0 / (4 * NCH)
                gacc = 0.0
                bh2 = [[self.sb(st2, f"s_bh{j}{i}", [128, 8, 128], F32) for i in range(2)] for j in range(2)]
                t_ = [self.sb(st2, f"s_t{i}", [128, 8, 128], F32) for i in range(2)]
                bsb2 = [[self.sb(st2, f"s_bsb{j}{i}", [128, 8, 128], F32) for i in range(2)] for j in range(2)]
                pbf = [[self.sb(st2, f"s_pbf{i}{j}", [128, 8, 128], BF16) for j in range(4)] for i in range(1)]
                hprev = self.sb(st2, "s_hprev", [128, 2, 16], F32)
                hl = self.sb(st2, "s_hl", [128, 4, 8], F32)
                yf = [self.sb(st2, f"s_yf{i}", [128, 4, 128], F32) for i in range(2)]
                yy = self.sb(st2, "s_yy", [128, 4, 128], F32)
                zb = self.sb(st2, "s_zb", [128, 4, 128], BF16)
                sg = self.sb(st2, "s_sg", [128, 4, 128], F32)
                ya = [self.sb(st2, f"s_ya{i}", [128, 4, 128], BF16) for i in range(2)]
                hi_ = 0
                fcnt = 0
                pending = None
                pending2 = []
                steps = []
                for d in range(2):
                    order = range(NCH) if d == 0 else range(NCH - 1, -1, -1)
                    for f in order:
                        for hh in range(2):
                            steps.append((d, f, hh))

                fidx = {}
                def stageA(d, f, hh):
                    fs = slice(f * 128, (f + 1) * 128)
                    P0 = 8 * hh
                    if hh == 0:
                        fidx[(d, f)] = len(fidx) % 3
                        ui = fidx[(d, f)]
                        S.dma("sp", ufr[ui][:], X["uS"][:, fs].rearrange("(o p) t -> p o t", p=128), ["uS_all"], [("s_uf", ui)])
                    ui = fidx[(d, f)]
                    u_, uk = ufr[ui], ("s_uf", ui)
                    for pl in range(8):
                        p = P0 + pl
                        for part in range(2):
                            bank = 2 * part + pl // 4
                            urhs = u_[:, p // 4, :] if d == 0 else u_[:, p // 4, ::-1]
                            self.mm(self.ps[bank][:, (pl % 4) * 128:(pl % 4 + 1) * 128], bbw[:, d, part, p, :], urhs,
                                    True, True, ["s_bbw", uk], [("ps", bank)])
                    bsb = bsb2[hh]
                    for part in range(2):
                        for a2 in range(2):
                            bank = 2 * part + a2
                            self.cp("act", bsb[part][:, 4 * a2:4 * a2 + 4, :], self.ps[bank][:].rearrange("q (a t) -> q a t", a=4),
                                    [("ps", bank)], [f"s_bsb{hh}{part}"])

                stageA(*steps[0])
                for si, (d, f, hh) in enumerate(steps):
                    if si + 1 < len(steps):
                        stageA(*steps[si + 1])
                    gacc += grate
                    while gacc >= 1.0:
                        next(ggen, None)
                        gacc -= 1.0
                    fs = slice(f * 128, (f + 1) * 128)
                    P0 = 8 * hh
                    first_of_dir = (f == (0 if d == 0 else NCH - 1)) and hh == 0
                    if first_of_dir:
                        if pending is not None:
                            pending()
                            pending = None
                        self.memset("dve", hprev[:], 0.0, [("s_hprev", 0), ("s_hprev", 1)])
                    if hh == 0:
                        ybank = 4 + (fcnt % 2)
                        fcnt += 1
                        if (d == 0 and f == HC) or (d == 1 and f == HC - 1):
                            self.ts("dve", hprev[:], hprev[:], self.carry[:, 0:1], None, ALU.mult, None,
                                    [("s_hprev", 0), ("s_hprev", 1), "carry"], [("s_hprev", 0), ("s_hprev", 1)])
                        if d == 1:
                            yfl = yf[f % 2]
                            yfk = ("s_yf", f % 2)
                            S.dma("sp", yfl[:], X["yfS"][:, fs].rearrange("(o p) t -> p o t", p=128), ["yfS_all"], [yfk])
                    bsb = bsb2[hh]
                    bh = bh2[hh]
                    kb0, kb1 = f"s_bh{hh}0", f"s_bh{hh}1"
                    kbh = [kb0, kb1]
                    ct = cs_t[:, d, P0:P0 + 8, :]
                    stb = sn_t[:, d, P0:P0 + 8, :]
                    BR, BI = bsb[0][:], bsb[1][:]
                    self.tt("dve", t_[0][:], BR, ct, ALU.mult, [f"s_bsb{hh}0", "s_cs"], ["s_t0"])
                    self.tt("dve", t_[1][:], BI, stb, ALU.mult, [f"s_bsb{hh}1", "s_sn"], ["s_t1"])
                    self.tt("dve", bh[0][:], t_[0][:], t_[1][:], ALU.add, ["s_t0", "s_t1"], [kb0])
                    self.tt("dve", t_[0][:], BI, ct, ALU.mult, [f"s_bsb{hh}1", "s_cs", kb0], ["s_t0"])
                    self.tt("dve", t_[1][:], BR, stb, ALU.mult, [f"s_bsb{hh}0", "s_sn", kb0], ["s_t1"])
                    self.tt("dve", bh[1][:], t_[0][:], t_[1][:], ALU.subtract, ["s_t0", "s_t1"], [kb1])
                    edge = 0
                    for part in range(2):
                        self.tt("dve", bh[part][:, :, edge], bh[part][:, :, edge], hprev[:, part, P0:P0 + 8], ALU.add,
                                [kbh[part], ("s_hprev", hh)], [kbh[part]])
                    for part in range(2):
                        fl = bh[part][:].rearrange("q a t -> q (a t)")
                        rtf = rt[:, d, P0:P0 + 8, :].rearrange("q a t -> q (a t)")
                        self.scan(fl, rtf, fl, 0.0, [kbh[part], "s_rt"], [kbh[part]])
                    last = 127
                    tl = 127
                    gr, gi = bh[0][:, :, last], bh[1][:, :, last]
                    cl, sl_ = cs_t[:, d, P0:P0 + 8, tl], sn_t[:, d, P0:P0 + 8, tl]
                    RK = [kb0, kb1, "s_cs", "s_sn", "s_hl"]
                    E = "pool"
                    self.tt(E, hl[:, 0, :], gr, cl, ALU.mult, RK, ["s_hl"])
                    self.tt(E, hl[:, 1, :], gi, sl_, ALU.mult, RK, ["s_hl"])
                    self.tt(E, hl[:, 0, :], hl[:, 0, :], hl[:, 1, :], ALU.subtract, RK, ["s_hl"])
                    self.tt(E, hl[:, 2, :], gr, sl_, ALU.mult, RK, ["s_hl"])
                    self.tt(E, hl[:, 3, :], gi, cl, ALU.mult, RK, ["s_hl"])
                    self.tt(E, hl[:, 2, :], hl[:, 2, :], hl[:, 3, :], ALU.add, RK, ["s_hl"])
                    self.tt(E, hprev[:, 0, P0:P0 + 8], hl[:, 0, :], rho[:, d, P0:P0 + 8], ALU.mult, ["s_hl", "s_rho", ("s_hprev", hh)], [("s_hprev", hh)])
                    self.tt(E, hprev[:, 1, P0:P0 + 8], hl[:, 2, :], rho[:, d, P0:P0 + 8], ALU.mult, ["s_hl", "s_rho", ("s_hprev", hh)], [("s_hprev", hh)])
                    hb_ = pbf[0]
                    hk = ("s_pbf", 0)
                    hi_ += 1
                    GR, GI = bh[0][:], bh[1][:]
                    PE_ = self.S5_PROD_ENG
                    self.tt(PE_, hb_[0][:], GR, ct, ALU.mult, [kb0, "s_cs"], [hk])
                    self.tt(PE_, hb_[1][:], GI, stb, ALU.mult, [kb1, "s_sn"], [hk])
                    self.tt(PE_, hb_[2][:], GR, stb, ALU.mult, [kb0, "s_sn"], [hk])
                    self.tt("dve", hb_[3][:], GI, ct, ALU.mult, [kb1, "s_cs"], [(hk, 3)])
                    wsel = [cw[:, d, 0], cwn[:, d], cw[:, d, 1], cw[:, d, 1]]
                    for o2 in range(2):
                        oc = 2 * hh + o2
                        yo_ = self.ps[ybank][:, oc * 128:(oc + 1) * 128]
                        n_ = 0
                        for pl in range(4 * o2, 4 * o2 + 4):
                            p = P0 + pl
                            for k4 in range(4):
                                crhs = hb_[k4][:, pl, :] if d == 0 else hb_[k4][:, pl, ::-1]
                                self.mm(yo_, wsel[k4][:, p, :], crhs, n_ == 0, n_ == 15, ["s_cw", hk, (hk, 3)], [("ps", ybank)])
                                n_ += 1
                    if hh == 1 and pending2:
                        pending2.pop(0)()
                    if hh == 0 and pending is not None:
                        pending()
                        pending = None
                    if hh == 1:
                        def epilogue(f=f, fs=fs, d=d, ybank=ybank, yfl=(yf[f % 2]), yfk=("s_yf", f % 2), u_=ufr[fidx[(d, f)]], uk=("s_uf", fidx[(d, f)])):
                            y4 = self.ps[ybank][:].rearrange("q (o t) -> q o t", o=4)
                            if d == 0:
                                self.cp("act", yfl[:], y4, [("ps", ybank)], [yfk])
                                S.dma("sp", X["yfS"][:, fs].rearrange("(o p) t -> p o t", p=128), yfl[:], [yfk], ["yfS_all"])
                            else:
                                for oc in range(4):
                                    self.stt("dve", yy[:, oc, :], u_[:, oc, :], sp_[:, oc:oc + 1], self.ps[ybank][:, oc * 128:(oc + 1) * 128],
                                             ALU.mult, ALU.add, [uk, "s_sp", ("ps", ybank)], ["s_yy"])
                                self.tt("dve", yy[:], yy[:], yfl[:], ALU.add, ["s_yy", yfk], ["s_yy"])
                                self.act(yy[:], yy[:], AF.Gelu_apprx_tanh, ["s_yy"], ["s_yy"])
                                self.cp("act", zb[:], yy[:], ["s_yy"], ["s_zb"])
                                yal = ya[f % 2]
                                yak = ("s_ya", f % 2)
                                for oc in range(4):
                                    go = self.ps[6][:, oc * 128:(oc + 1) * 128]
                                    for kc in range(4):
                                        self.mm(go, wglu[:, kc, oc * 128:(oc + 1) * 128], zb[:, kc, :], kc == 0, kc == 3, ["s_wglu", "s_zb"], [("ps", 6)])
                                    self.act(sg[:, oc, :], go, AF.Sigmoid, [("ps", 6), "s_sp"], ["s_sg"], bias=sp_[:, 4 + oc:5 + oc])
                                def part2(yal=yal, yak=yak, fs=fs):
                                    self.tt("dve", yal[:], yy[:], sg[:], ALU.mult, ["s_yy", "s_sg"], [yak])
                                    S.dma("sp", X["yaS"][:, fs].rearrange("(o p) t -> p o t", p=128), yal[:], [yak], ["yaS_all"])
                                pending2.append(part2)
                        pending = epilogue
                if pending is not None:
                    pending()
                    pending = None
                while pending2:
                    pending2.pop(0)()
                for _ in ggen:
                    pass


    def n_lru_gate_blocks(self):
        nb = len(self.gate_blocks())
        return min(2, nb - 1)

    def gate_blocks(self):
        return [(o, min(512, self.NT - o)) for o in range(0, self.NT, 512)]

    def gate_gen(self, l, st, banks, blocks):
        S, I, X, NT = self.S, self.I, self.X, self.NT
        w_in = I["w_in"]
        hTg = self.sb(st, "gg_h", [128, KC, 512], BF16)
        ring = [self.sb(st, f"gg_w{i}", [128, KC, 256], BF16) for i in range(3)]
        gst = [self.sb(st, f"gg_s{i}", [128, 512], BF16) for i in range(3)]
        tiles = []
        for (c0, r0, nrow) in [(C_GA, 0, 4), (C_GB, 4, 4), (C_GC, 8, 8)]:
            for g2 in range(nrow // 2):
                tiles.append((c0 + g2 * 256, "sgS", r0 + g2 * 2, AF.Silu))
        for br in range(3):
            for g2 in range(8):
                tiles.append((C_M + br * D + g2 * 256, "mgS", br * 16 + g2 * 2, AF.Sigmoid))
        wi = 0
        gi = 0
        pb = 0
        for (t0, BG) in blocks:
            S.dma("sp", hTg[:, :, 0:BG], X["hS"][:, t0:t0 + BG].rearrange("(kc p) t -> p kc t", p=128), ["hS_all"], ["gg_h"])
            for (c0, dst, row0, fn) in tiles:
                wt = ring[wi % 3]
                wk = ("gg_w", wi % 3)
                wi += 1
                S.dma("pool", wt[:], w_in[l, :, c0:c0 + 256].rearrange("(kc p) n -> p kc n", p=128), (), [wk])
                for mi in range(2):
                    g = gst[gi % 3]
                    gk = ("gg_s", gi % 3)
                    gi += 1
                    bank, bkey = banks[pb % len(banks)]
                    pb += 1
                    for kc in range(KC):
                        self.mm(bank[:, 0:BG], wt[:, kc, mi * 128:(mi + 1) * 128], hTg[:, kc, 0:BG],
                                kc == 0, kc == KC - 1, [wk, "gg_h"], [bkey])
                    self.act(g[:, 0:BG], bank[:, 0:BG], fn, [bkey], [gk])
                    row = row0 + mi
                    S.dma("sp", X[dst][row * 128:(row + 1) * 128, t0:t0 + BG], g[:, 0:BG], [gk], [(dst, "all")])
                yield

    def phase3(self, l, zin, last):
        S, I, X, NB, SBK, NT, L = self.S, self.I, self.X, self.NB, self.SBK, self.NT, self.L
        with ExitStack() as st:
            sgt = self.sb(st, "p3_sg", [128, KC, NB], BF16)
            yg = self.sb(st, "p3_y", [128, KC, NB], BF16)
            m = self.sb(st, "p3_m", [128, KC, NB], BF16)
            MW = max(w_ for _, w_ in self.MSUB)
            mgt = [self.sb(st, f"p3_mg{i}", [128, 6, NB], BF16) for i in range(2)]
            tmp = [self.sb(st, f"p3_t{i}", [128, MW], F32) for i in range(2)]
            macc = [self.sb(st, f"p3_ma{i}", [128, MW], F32) for i in range(2)]
            zt = [self.sb(st, f"p3_z{i}", [128, NB], F32) for i in range(2)]
            self.wring_init(st, nbuf=6)
            pb = 0
            ti = 0
            zi = 0
            mgi = 0
            for b in range(NT // NB):
                t0 = b * NB
                for (ra, rb) in [(0, 4), (4, 8), (8, 16)]:
                    S.dma("sp", sgt[:, ra:rb, :], X["sgS"][ra * 128:rb * 128, t0:t0 + NB].rearrange("(kc p) t -> p kc t", p=128),
                          ["sgS_all"], [("p3_sg", ra)])
                S.dma("sp", yg[:, 0:4, :], X["yaS"][:, t0:t0 + NB].rearrange("(kc p) t -> p kc t", p=128), ["yaS_all"], [("p3_y", r) for r in range(0, 4)])
                S.dma("sp", yg[:, 4:8, :], X["ybS"][:, t0:t0 + NB].rearrange("(kc p) t -> p kc t", p=128), ["ybS_all"], [("p3_y", r) for r in range(4, 8)])
                S.dma("sp", yg[:, 8:16, :], X["ycS"][:, t0:t0 + NB].rearrange("(kc p) t -> p kc t", p=128), ["ycS_all"], [("p3_y", r) for r in range(8, 16)])
                for row in range(16):
                    self.tt("dve", yg[:, row, :], yg[:, row, :], sgt[:, row, :], ALU.mult,
                            [("p3_y", row), ("p3_sg", 0 if row < 4 else (4 if row < 8 else 8))], [("p3_y", row)])
                ykeys = [("p3_y", r) for r in range(16)]
                wouts = [(I["w_out_a"], 4, 0), (I["w_out_b"], 4, 4), (I["w_out_c"], 8, 8)]
                for g2 in range(8):
                    mg = mgt[mgi % 2]
                    mgk = ("p3_mg", mgi % 2)
                    mgi += 1
                    for br in range(3):
                        r_ = br * 16 + g2 * 2
                        S.dma("sp", mg[:, 2 * br:2 * br + 2, :], X["mgS"][r_ * 128:(r_ + 2) * 128, t0:t0 + NB].rearrange("(a p) t -> p a t", p=128),
                              ["mgS_all"], [mgk])
                    tiles = []
                    for br in range(3):
                        wo_src, nk, r0 = wouts[br]
                        wo, wok = self.wload(wo_src[l, :, g2 * 256:(g2 + 1) * 256], nk, 256)
                        tiles.append((wo, wok, nk, r0))
                    for mi in range(2):
                        row = g2 * 2 + mi
                        ms = slice(mi * 128, (mi + 1) * 128)
                        for (so, sw) in self.MSUB:
                            ss = slice(so, so + sw)
                            ma = macc[zi % 2]
                            mak = ("p3_ma", zi % 2)
                            zi += 1
                            for br in range(3):
                                wo, wok, nk, r0 = tiles[br]
                                g = mg[:, 2 * br + mi, ss]
                                bank2 = pb % 6
                                pb += 1
                                for kc in range(nk):
                                    self.mm(self.ps[bank2][:, 0:sw], wo[:, kc, ms], yg[:, r0 + kc, ss], kc == 0, kc == nk - 1,
                                            [wok] + ykeys[r0:r0 + nk], [("ps", bank2)])
                                if br == 0:
                                    self.tt("dve", ma[:, 0:sw], g, self.ps[bank2][:, 0:sw], ALU.mult, [mgk, ("ps", bank2)], [mak])
                                else:
                                    t = tmp[ti % 2]
                                    tk = ("p3_t", ti % 2)
                                    ti += 1
                                    self.tt("dve", t[:, 0:sw], g, self.ps[bank2][:, 0:sw], ALU.mult, [mgk, ("ps", bank2)], [tk])
                                    if br == 1:
                                        self.tt("dve", ma[:, 0:sw], ma[:, 0:sw], t[:, 0:sw], ALU.add, [mak, tk], [mak])
                                    else:
                                        self.tt("dve", m[:, row, ss], ma[:, 0:sw], t[:, 0:sw], ALU.add, [mak, tk], [("p3_m", row)])
                mkeys = [("p3_m", r) for r in range(16)]
                for g2 in range(8):
                    wt, wk = self.wload(I["w_o"][l, :, g2 * 256:(g2 + 1) * 256], KC, 256)
                    for mi in range(2):
                        row = g2 * 2 + mi
                        z = zt[row % 2]
                        zk = ("p3_z", row % 2)
                        S.dma("sp", z[:], zin[row * 128:(row + 1) * 128, t0:t0 + NB], [("zT", b)] if zin is X["zT"] else (), [zk])
                        for (so, sw) in self.MSUB:
                            ss = slice(so, so + sw)
                            bank = pb % 6
                            pb += 1
                            for kc in range(KC):
                                self.mm(self.ps[bank][:, 0:sw], wt[:, kc, mi * 128:(mi + 1) * 128], m[:, kc, ss], kc == 0, kc == KC - 1,
                                        [wk] + mkeys, [("ps", bank)])
                            self.tt("dve", z[:, ss], z[:, ss], self.ps[bank][:, 0:sw], ALU.add, [zk, ("ps", bank)], [zk])
                        S.dma("sp", X["zT"][row * 128:(row + 1) * 128, t0:t0 + NB], z[:], [zk], [("zT", b)])

    def phase4(self):
        S, X, SBK, NT, L = self.S, self.X, self.SBK, self.NT, self.L
        with ExitStack() as st:
            zf = [self.sb(st, f"p4_z{i}", [128, KC, SBK], F32) for i in range(2)]
            sq = self.sb(st, "sq", [128, KC, SBK], F32)
            rstd = self.sb(st, "rstd", [128, SBK], F32)
            for s in range(NT // SBK):
                c0 = s * SBK
                z = zf[s % 2]
                zk = ("p4_z", s % 2)
                S.dma("sp", z[:], X["zT"][:, c0:c0 + SBK].rearrange("(kc p) t -> p kc t", p=128), (), [zk])
                self.rms_rstd(z, zk, sq, rstd, SBK, 6, D)
                for kc in range(KC):
                    self.stt("dve", z[:, kc, :], z[:, kc, :], self.normg[:, L, kc:kc + 1], rstd[:], ALU.mult, ALU.mult,
                             [zk, "rstd", "normg"], [zk])
                S.dma("sp", self.yT[:, c0:c0 + SBK].rearrange("(kc p) t -> p kc t", p=128), z[:], [zk], [("yT", s)])


def _slot_layout(x_prompt, x_sample, meta):
    Bp, Lp, _ = x_prompt.shape
    Bs, Ls, _ = x_sample.shape
    assert Lp == 2 * Ls and Bp == 2 and Bs == 8
    HALF = Ls + 128
    NT = 2 * HALF
    slots = np.zeros((8, NT, D), np.float32)
    carry = np.zeros((8,), np.float32)
    real = np.zeros((8, NT), np.float32)
    where = []
    for i in range(Bp):
        c = i
        slots[c, 112:128] = meta
        slots[c, 128:128 + Lp] = x_prompt[i]
        real[c, 112:128 + Lp] = 1.0
        carry[c] = 1.0
        where.append(("p", i, c, 128))
    for i in range(Bs):
        c = 2 + i // 2
        o = (i % 2) * HALF
        slots[c, o + 112:o + 128] = meta
        slots[c, o + 128:o + 128 + Ls] = x_sample[i]
        real[c, o + 112:o + 128 + Ls] = 1.0
        where.append(("s", i, c, o + 128))
    return slots, carry, real, where, Ls, NT


def _prep_shared(inp, L, NT):
    f = lambda a: np.ascontiguousarray(np.asarray(a, dtype=np.float32))
    sh = {}
    for k in ["w_in", "w_out_a", "w_out_b", "w_out_c", "w_o"]:
        sh[k] = f(inp[k][:L])
    sh["w_glu"] = f(inp["s5_w_glu"][:L])
    lw = np.stack([inp["lru_w_a"][:L, 0], inp["lru_w_x"][:L, 0], inp["lru_w_a"][:L, 1], inp["lru_w_x"][:L, 1]], axis=2)
    sh["lru_w"] = f(lw)
    ng = np.concatenate([np.asarray(inp["norm_g"][:L]), np.asarray(inp["final_norm_g"])[None]], axis=0)
    sh["normg"] = f(ng.reshape(L + 1, 16, 128).transpose(2, 0, 1))
    cw = np.asarray(inp["conv_w"][:L]).reshape(L, 4, 8, 128).transpose(3, 0, 2, 1)
    cb = np.asarray(inp["conv_b"][:L]).reshape(L, 8, 128).transpose(2, 0, 1)[..., None]
    def dn(a):
        return np.asarray(a[:L]).reshape(L, 2, 8, 128).transpose(3, 0, 2, 1)
    sh["lrup"] = f(np.concatenate([cw, cb, dn(inp["lru_b_a"]), dn(inp["lru_b_x"]), dn(inp["lru_lam"])], axis=3))
    wgp = np.zeros((L, 2, 32, 256), np.float32)
    for d in range(2):
        wgp[:, d, d * 16:(d + 1) * 16, :] = np.asarray(inp["gla_w_gate_up"][:L, d])
    sh["wgp"] = wgp
    bg = np.asarray(inp["gla_b_gate"][:L]).reshape(L, 2, 2, 128).transpose(3, 0, 1, 2).reshape(128, L, 4)
    gng = np.asarray(inp["gla_norm_g"][:L]).reshape(L, 4, 128).transpose(2, 0, 1)
    sh["glap"] = f(np.concatenate([bg, gng], axis=2))
    def sp_layout(a):
        return np.asarray(a).reshape(L, 2, 16, 2, 64).transpose(3, 4, 0, 1, 2).reshape(128, L, 2, 16)
    lstep = np.broadcast_to(np.asarray(inp["s5_log_step"][:L])[..., None], (L, 2, 32, 64))
    sh["s5s"] = f(np.stack([sp_layout(inp["s5_lam_re"][:L]), sp_layout(inp["s5_lam_im"][:L]), sp_layout(lstep)], axis=2))
    sh["s5r"] = f(np.stack([np.asarray(inp["s5_lam_re"][:L]).reshape(L, 2, 2048), np.asarray(inp["s5_lam_im"][:L]).reshape(L, 2, 2048),
                            lstep.reshape(L, 2, 2048)], axis=1))
    bexp = np.zeros((L, 2, 128, 16, 128), np.float32)
    cexp = np.zeros((L, 2, 2, 128, 16, 128), np.float32)
    bre, bim = np.asarray(inp["s5_b_re"][:L]), np.asarray(inp["s5_b_im"][:L])
    cre, cim = np.asarray(inp["s5_c_re"][:L]), np.asarray(inp["s5_c_im"][:L])
    for g in range(32):
        p, g2, go = g // 2, g % 2, g % 8
        for part, src in enumerate([bre, bim]):
            bexp[:, part, 16 * go:16 * go + 16, p, g2 * 64:(g2 + 1) * 64] = src[:, g].transpose(0, 2, 1)
        for part, src in enumerate([cre, cim]):
            cexp[:, :, part, g2 * 64:(g2 + 1) * 64, p, 16 * go:16 * go + 16] = src[:, :, g].transpose(0, 1, 3, 2)
    sh["bexp"], sh["cexp"] = bexp, cexp
    dsk = np.asarray(inp["s5_d"][:L]).reshape(L, 4, 128).transpose(2, 0, 1)
    bgl = np.asarray(inp["s5_b_glu"][:L]).reshape(L, 4, 128).transpose(2, 0, 1)
    sh["s5p"] = f(np.concatenate([dsk, bgl], axis=2))
    consts = np.zeros((128, 6, 128), np.float32)
    consts[:, 0] = np.eye(128)
    jj, ii = np.meshgrid(np.arange(128), np.arange(128), indexing="ij")
    consts[:, 1] = (jj <= ii)
    consts[:, 2] = (jj >= ii)
    consts[:, 3] = np.arange(1, 129)[None, :]
    consts[:, 4] = 1.0
    consts[:, 4, 0] = 0.0
    consts[:, 5] = 1.0
    consts[:, 5, 127] = 0.0
    sh["consts"] = consts
    km = np.ones((128, NT + 1), np.float32)
    km[:, 0::128] = 0.0
    sh["kmask"] = km
    return sh


_PROG_CACHE = {}


def _get_prog(Ls, L, debug=False, NB=None, SBK=None):
    key = (Ls, L, debug, NB, SBK)
    if key not in _PROG_CACHE:
        NT = 2 * (Ls + 128)
        if NB is None:
            NB = NT // 4
            SBK = NB // 4
        _PROG_CACHE[key] = Prog(Ls, L, NB, SBK, debug)
    return _PROG_CACHE[key]


def run(inp, L=None, debug=False, NB=None, SBK=None):
    x_prompt = np.asarray(inp["x_prompt"], np.float32)
    x_sample = np.asarray(inp["x_sample"], np.float32)
    meta = np.asarray(inp["meta_tokens"], np.float32)
    if L is None:
        L = int(np.asarray(inp["w_in"]).shape[0])
    slots, carry, real, where, Ls, NT = _slot_layout(x_prompt, x_sample, meta)
    sh = _prep_shared(inp, L, NT)
    prog = _get_prog(Ls, L, debug, NB, SBK)
    in_maps = []
    for c in range(8):
        m = dict(sh)
        m["xT"] = np.ascontiguousarray(slots[c].T)
        m["carry"] = np.full((128, 1), carry[c], np.float32)
        m["rmask"] = np.ascontiguousarray(np.broadcast_to(real[c][None, :], (128, NT)))
        in_maps.append(m)
    res = run_bass_kernel_spmd(prog.nc, in_maps, core_ids=list(range(8)))
    yp = np.zeros_like(x_prompt)
    ys = np.zeros_like(x_sample)
    for kind, i, c, start in where:
        yT = res.results[c]["yT"]
        if kind == "p":
            yp[i] = yT[:, start:start + x_prompt.shape[1]].T
        else:
            ys[i] = yT[:, start:start + Ls].T
    return (yp, ys), res


def kernel(**inputs):
    (yp, ys), _ = run(inputs)
    return yp, ys
```

```python
import math
from contextlib import ExitStack

import numpy as np
import concourse.bass as bass
import concourse.mybir as mybir
from concourse.bass_utils import run_bass_kernel_spmd

F32 = mybir.dt.float32
BF16 = mybir.dt.bfloat16
AF = mybir.ActivationFunctionType
ALU = mybir.AluOpType

D = 2048
KC = 16
N_IN = 10784
N_META = 16
EPS = 1e-6
C_U, C_GA, C_Q, C_K, C_V, C_GB, C_GLR, C_X, C_GC, C_M = 0, 512, 1024, 1280, 1536, 2048, 2560, 2592, 3616, 4640
TWO_PI = 2.0 * math.pi
NDQ = 12


class Sched:
    def __init__(self, nc, es):
        self.nc = nc
        self.engs = ["pe", "act", "dve", "pool", "sp"]
        self.ops = {e: [] for e in self.engs}
        self.semh = {}
        for e in ["pe", "act", "dve", "pool"]:
            self.semh[e] = es.enter_context(nc.semaphore("s_" + e))
        for q in ["sp", "pool", "act"]:
            for i in range(NDQ):
                self.semh[f"d_{q}{i}"] = es.enter_context(nc.semaphore(f"d_{q}{i}"))
        self.latest = {k: 0 for k in self.semh}
        self.dcnt = {"sp": 0, "pool": 0, "act": 0}
        self.seen = {e: {} for e in self.engs}
        self.bufs = {}
        self.nins = 0

    def _deps(self, reads, writes):
        toks = {}

        def add(k, v):
            if toks.get(k, 0) < v:
                toks[k] = v

        for key in reads:
            b = self.bufs.get(key)
            if b and b[0]:
                add(*b[0])
        for key in writes:
            b = self.bufs.get(key)
            if b:
                if b[0]:
                    add(*b[0])
                for k, v in b[1].items():
                    add(k, v)
        return toks

    def _wait(self, eng, toks, skip=None):
        for k, v in toks.items():
            if k == skip:
                continue
            if self.seen[eng].get(k, 0) >= v:
                continue
            self.seen[eng][k] = v
            sem = self.semh[k]
            self.ops[eng].append(lambda e, sem=sem, v=v: e.wait_ge(sem, v))
            self.nins += 1

    def _record(self, tok, reads, writes):
        for key in writes:
            self.bufs[key] = [tok, {}]
        k, v = tok
        for key in reads:
            if key in writes:
                continue
            b = self.bufs.setdefault(key, [None, {}])
            if b[1].get(k, 0) < v:
                b[1][k] = v

    def op(self, eng, fn, reads=(), writes=()):
        toks = self._deps(reads, writes)
        self._wait(eng, toks, skip=("pe" if eng == "pe" else None))
        self.latest[eng] += 1
        v = self.latest[eng]
        sem = self.semh[eng]
        self.ops[eng].append(lambda e, fn=fn, sem=sem: fn(e).then_inc(sem, 1))
        self.nins += 1
        self._record((eng, v), reads, writes)

    def dma(self, q, out, in_, reads=(), writes=()):
        toks = self._deps(reads, writes)
        i = self.dcnt[q]
        self.dcnt[q] += 1
        k = f"d_{q}{i % NDQ}"
        val = (i // NDQ + 1) * 16
        if val > 16:
            toks[k] = max(toks.get(k, 0), val - 16)
        self._wait(q, toks)
        sem = self.semh[k]
        self.ops[q].append(lambda e, out=out, in_=in_, sem=sem: e.dma_start(out=out, in_=in_).then_inc(sem, 16))
        self.nins += 1
        self.latest[k] = val
        self._record((k, val), reads, writes)

    def barrier(self):
        toks = {k: v for k, v in self.latest.items() if v > 0}
        for e in self.engs:
            self._wait(e, dict(toks))

    def replay(self, block):
        ops = self.ops

        @block.tensor
        def _(e):
            for f in ops["pe"]:
                f(e)

        @block.scalar
        def _(e):
            for f in ops["act"]:
                f(e)

        @block.vector
        def _(e):
            for f in ops["dve"]:
                f(e)

        @block.gpsimd
        def _(e):
            for f in ops["pool"]:
                f(e)

        @block.sync
        def _(e):
            for f in ops["sp"]:
                f(e)


class Prog:
    def __init__(self, Ls, depth, NB, SBK, debug=False):
        self.Ls, self.L = Ls, depth
        self.HALF = Ls + 128
        self.NT = 2 * self.HALF
        self.NB, self.SBK = NB, SBK
        assert self.NT % NB == 0 and NB % SBK == 0 and SBK <= 512 and self.NT % SBK == 0
        self.NCH = self.NT // 128
        self.GATE_PER_STEP = 2
        self.S5_PROD_ENG = "dve"
        self.MSUB = []
        o = 0
        while o < NB:
            w_ = min(512, NB - o)
            self.MSUB.append((o, w_))
            o += w_
        self.debug = debug
        self.nc = bass.Bass("TRN2", target_bir_lowering=False)
        self.es = ExitStack()
        self.build()

    def din(self, name, shape, dt=F32):
        return self.nc.dram_tensor(name, list(shape), dt, kind="ExternalInput").ap()

    def dscr(self, name, shape, dt):
        kind = "ExternalOutput" if self.debug else "Internal"
        return self.nc.dram_tensor(name, list(shape), dt, kind=kind).ap()

    def sb(self, st, name, shape, dt):
        self._uid = getattr(self, "_uid", 0) + 1
        return st.enter_context(self.nc.sbuf_tensor(f"sb{self._uid}_{name}", list(shape), dt))

    def mm(self, out, lhsT, rhs, start, stop, r, w):
        self.S.op("pe", lambda e: e.matmul(out, lhsT, rhs, start=start, stop=stop), r, w)

    def tr(self, out, in_, ident, r, w):
        self.S.op("pe", lambda e: e.transpose(out, in_, ident), r, w)

    def act(self, out, in_, func, r, w, bias=0.0, scale=1.0):
        self.S.op("act", lambda e: e.activation(out=out, in_=in_, func=func, bias=bias, scale=scale), r, w)

    def tt(self, eng, out, in0, in1, op, r, w):
        self.S.op(eng, lambda e: e.tensor_tensor(out=out, in0=in0, in1=in1, op=op), r, w)

    def ts(self, eng, out, in0, s1, s2, op0, op1, r, w):
        if op1 is None:
            self.S.op(eng, lambda e: e.tensor_scalar(out=out, in0=in0, scalar1=s1, scalar2=None, op0=op0), r, w)
        else:
            self.S.op(eng, lambda e: e.tensor_scalar(out=out, in0=in0, scalar1=s1, scalar2=s2, op0=op0, op1=op1), r, w)

    def stt(self, eng, out, in0, scalar, in1, op0, op1, r, w):
        self.S.op(eng, lambda e: e.scalar_tensor_tensor(out=out, in0=in0, scalar=scalar, in1=in1, op0=op0, op1=op1), r, w)

    def scan(self, out, d0, d1, init, r, w):
        self.S.op("dve", lambda e: e.tensor_tensor_scan(out=out, data0=d0, data1=d1, initial=init, op0=ALU.mult, op1=ALU.add), r, w)

    def cp(self, eng, out, in_, r, w):
        if eng == "act":
            self.S.op("act", lambda e: e.activation(out=out, in_=in_, func=AF.Copy), r, w)
        else:
            self.S.op(eng, lambda e: e.tensor_copy(out=out, in_=in_), r, w)

    def dbg(self, name, ap, shape, dt, r):
        if not self.debug:
            return
        t = self.nc.dram_tensor("dbg_" + name, list(shape), dt, kind="ExternalOutput").ap()
        self.S.dma("sp", t, ap, r, ["dbg_" + name])

    def recip(self, eng, out, in_, r, w):
        self.S.op(eng, lambda e: e.reciprocal(out=out, in_=in_), r, w)

    def sin_of(self, out, x, shift, tf, ti, r, w):
        k = ["_sincos"]
        if shift != 0.0:
            self.ts("dve", tf, x, shift, None, ALU.add, None, r + k, k)
            xs = tf
        else:
            xs = x
        self.ts("dve", out, xs, 1.0 / TWO_PI, None, ALU.mult, None, r + k, w)
        self.cp("dve", ti, out, w, k)
        self.cp("dve", out, ti, k, w)
        self.stt("dve", out, out, -TWO_PI, xs, ALU.mult, ALU.add, r + k + list(w), w)
        self.ts("dve", out, out, -3.141592, 3.141592, ALU.max, ALU.min, w, w)
        self.act(out, out, AF.Sin, w, w)

    def memset(self, eng, ap, val, w):
        self.S.op(eng, lambda e: e.memset(ap, val), (), w)

    def build(self):
        nc, es, L, NT = self.nc, self.es, self.L, self.NT
        I = self.I = {}
        I["xT"] = self.din("xT", [D, NT])
        I["carry"] = self.din("carry", [128, 1])
        I["rmask"] = self.din("rmask", [128, NT])
        I["kmask"] = self.din("kmask", [128, NT + 1])
        I["consts"] = self.din("consts", [128, 6, 128])
        I["w_in"] = self.din("w_in", [L, D, N_IN])
        I["w_out_a"] = self.din("w_out_a", [L, 512, D])
        I["w_out_b"] = self.din("w_out_b", [L, 512, D])
        I["w_out_c"] = self.din("w_out_c", [L, 1024, D])
        I["w_o"] = self.din("w_o", [L, D, D])
        I["w_glu"] = self.din("w_glu", [L, 512, 512])
        I["lru_w"] = self.din("lru_w", [L, 8, 4, 128, 128])
        I["normg"] = self.din("normg", [128, L + 1, 16])
        I["lrup"] = self.din("lrup", [128, L, 8, 11])
        I["wgp"] = self.din("wgp", [L, 2, 32, 256])
        I["glap"] = self.din("glap", [128, L, 8])
        I["s5s"] = self.din("s5s", [128, L, 3, 2, 16])
        I["s5r"] = self.din("s5r", [L, 3, 2, 2048])
        I["bexp"] = self.din("bexp", [L, 2, 128, 16, 128])
        I["cexp"] = self.din("cexp", [L, 2, 2, 128, 16, 128])
        I["s5p"] = self.din("s5p", [128, L, 8])
        self.yT = nc.dram_tensor("yT", [D, NT], F32, kind="ExternalOutput").ap()

        X = self.X = {}
        X["zT"] = self.dscr("zT", [D, NT], F32)
        X["hS"] = self.dscr("hS", [D, NT], BF16)
        X["uS"] = self.dscr("uS", [512, NT], BF16)
        X["qS"] = self.dscr("qS", [256, NT], BF16)
        X["kS"] = self.dscr("kS", [256, NT], BF16)
        X["vS"] = self.dscr("vS", [NT, 512], BF16)
        X["gS"] = self.dscr("gS", [32, NT], BF16)
        X["xS"] = self.dscr("xS", [1024, NT], BF16)
        X["yfS"] = self.dscr("yfS", [512, NT], F32)
        X["yaS"] = self.dscr("yaS", [512, NT], BF16)
        X["ybS"] = self.dscr("ybS", [512, NT], BF16)
        X["ycS"] = self.dscr("ycS", [1024, NT], BF16)
        X["sgS"] = self.dscr("sgS", [2048, NT], BF16)
        X["mgS"] = self.dscr("mgS", [3 * 2048, NT], BF16)

        self.S = Sched(nc, es)
        top = es
        self.ps = [top.enter_context(nc.psum_tensor(f"ps{i}", [128, 512], F32)) for i in range(7)]
        self.cst = self.sb(top, "cst", [128, 6, 128], F32)
        self.identb = self.sb(top, "identb", [128, 128], BF16)
        self.ones = self.sb(top, "ones", [128, 128], F32)
        self.normg = self.sb(top, "normg", [128, L + 1, 16], F32)
        self.carry = self.sb(top, "carry", [128, 1], F32)
        S = self.S
        S.dma("sp", self.cst[:], I["consts"], (), ["cst"])
        S.dma("pool", self.identb[:], I["consts"][:, 0, :], (), ["identb"])
        S.dma("sp", self.normg[:], I["normg"], (), ["normg"])
        S.dma("sp", self.carry[:], I["carry"], (), ["carry"])
        self.memset("dve", self.ones[:], 1.0, ["ones"])

        for l in range(L):
            zin = I["xT"] if l == 0 else X["zT"]
            self.phase1(l, zin)
            S.barrier()
            self.phase_lru(l)
            S.barrier()
            self.phase_gla(l)
            S.barrier()
            self.phase_s5(l)
            S.barrier()
            self.phase3(l, zin, last=(l == L - 1))
            S.barrier()
        self.phase4()
        S.barrier()
        with nc.Block() as block:
            S.replay(block)

    def wring_init(self, st, nbuf=4, cols=256):
        self.wr = [self.sb(st, f"wr{i}", [128, 16, cols], BF16) for i in range(nbuf)]
        self.wri = 0

    def wload(self, src, kc, ncols):
        i = self.wri % len(self.wr)
        self.wri += 1
        t = self.wr[i]
        self.S.dma("pool", t[:, 0:kc, 0:ncols], src.rearrange("(kc p) n -> p kc n", p=128), (), [("wr", i)])
        return t, ("wr", i)

    def rms_rstd(self, zt, zkey, sq, rstd, n, pbank, dim):
        S = self.S
        self.act(sq[:, :, 0:n], zt[:, :, 0:n], AF.Square, [zkey], ["sq"])
        for kc in range(KC):
            self.mm(self.ps[pbank][:, 0:n], self.ones[:], sq[:, kc, 0:n], kc == 0, kc == KC - 1,
                    ["ones", "sq"], [("ps", pbank)])
        self.act(rstd[:, 0:n], self.ps[pbank][:, 0:n], AF.Sqrt, [("ps", pbank)], ["rstd"], bias=EPS, scale=1.0 / dim)
        self.recip("dve", rstd[:, 0:n], rstd[:, 0:n], ["rstd"], ["rstd"])

    def phase1(self, l, zin):
        S, I, X, NB, SBK, NT = self.S, self.I, self.X, self.NB, self.SBK, self.NT
        nsub = NB // SBK
        w_in = I["w_in"]
        with ExitStack() as st:
            hT2 = [self.sb(st, f"hT{i}", [128, KC, NB], BF16) for i in range(2)]
            zt = [self.sb(st, f"zt{i}", [128, KC, SBK], F32) for i in range(2)]
            sq = self.sb(st, "sq", [128, KC, SBK], F32)
            rstd = self.sb(st, "rstd", [128, SBK], F32)
            og = [self.sb(st, f"og{i}", [128, NB], BF16) for i in range(3)]
            ov = [self.sb(st, f"ov{i}", [128, 512], BF16) for i in range(2)]
            wv = self.sb(st, "wv", [128, KC, 512], BF16)
            self.wring_init(st)
            S.dma("pool", wv[:], w_in[l, :, C_V:C_V + 512].rearrange("(kc p) n -> p kc n", p=128), (), ["wv"])
            ogi = 0
            ovi = 0
            pb = 0
            def norm_block(b):
                t0 = b * NB
                hT = hT2[b % 2]
                for s in range(nsub):
                    z = zt[s % 2]
                    zk = ("zt", s % 2)
                    c0 = t0 + s * SBK
                    S.dma("sp", z[:], zin[:, c0:c0 + SBK].rearrange("(kc p) t -> p kc t", p=128),
                          [("zT", b)] if zin is X["zT"] else (), [zk])
                    self.rms_rstd(z, zk, sq, rstd, SBK, 6, D)
                    for kc in range(KC):
                        self.stt("dve", hT[:, kc, s * SBK:(s + 1) * SBK], z[:, kc, :], self.normg[:, l, kc:kc + 1],
                                 rstd[:], ALU.mult, ALU.mult, [zk, "rstd", "normg"], [("hT", b % 2, s)])
                hk_ = [("hT", b % 2, s) for s in range(nsub)]
                S.dma("sp", X["hS"][:, t0:t0 + NB].rearrange("(kc p) t -> p kc t", p=128), hT[:], hk_, [("hS", b)])

            nblk = NT // NB
            norm_block(0)
            for b in range(nblk):
                t0 = b * NB
                hT = hT2[b % 2]
                hkeys = [("hT", b % 2, s) for s in range(nsub)]
                gcount = 0
                groups = [(C_U, 512, "uS", 0), (C_Q, 256, "qS", 0), (C_K, 256, "kS", 0),
                          (C_X, 256, "xS", 0), (C_X + 256, 256, "xS", 256), (C_X + 512, 256, "xS", 512),
                          (C_X + 768, 256, "xS", 768), (C_GLR, 32, "gS", 0)]
                groups = [(C_U, 256, "uS", 0), (C_U + 256, 256, "uS", 256)] + groups[1:]
                for (c0, ncols, dst, r0) in groups:
                    gcount += 1
                    if gcount == 4 and b + 1 < nblk:
                        norm_block(b + 1)
                    wt, wk = self.wload(w_in[l, :, c0:c0 + ncols], KC, ncols)
                    for m0 in range(0, ncols, 128):
                        mw = min(128, ncols - m0)
                        o = og[ogi % 3]
                        ok = ("og", ogi % 3)
                        ogi += 1
                        for (so, sw) in self.MSUB:
                            bank = pb % 6
                            pb += 1
                            for kc in range(KC):
                                self.mm(self.ps[bank][0:mw, 0:sw], wt[:, kc, m0:m0 + mw], hT[:, kc, so:so + sw],
                                        kc == 0, kc == KC - 1, [wk] + hkeys, [("ps", bank)])
                            eng = "act" if (pb % 2) else "dve"
                            self.cp(eng, o[0:mw, so:so + sw], self.ps[bank][0:mw, 0:sw], [("ps", bank)], [ok])
                        S.dma("sp", X[dst][r0 + m0:r0 + m0 + mw, t0:t0 + NB], o[0:mw, :], [ok], [(dst, b)])
                tt0 = 0
                while tt0 < NB:
                    tw = min(128, NB - tt0)
                    bank = pb % 6
                    pb += 1
                    for kc in range(KC):
                        self.mm(self.ps[bank][0:tw, 0:512], hT[:, kc, tt0:tt0 + tw], wv[:, kc, :], kc == 0, kc == KC - 1,
                                ["wv"] + hkeys, [("ps", bank)])
                    o = ov[ovi % 2]
                    ok = ("ov", ovi % 2)
                    ovi += 1
                    self.cp("act" if (pb % 2) else "dve", o[0:tw, :], self.ps[bank][0:tw, 0:512], [("ps", bank)], [ok])
                    S.dma("sp", X["vS"][t0 + tt0:t0 + tt0 + tw, :], o[0:tw, :], [ok], [("vS", b)])
                    tt0 += tw

    def phase_lru(self, l):
        S, I, X, NT, HALF, SBK = self.S, self.I, self.X, self.NT, self.HALF, self.SBK
        nch = NT // SBK
        with ExitStack() as st:
            xr = self.sb(st, "l_xr", [128, NT + 4], BF16)
            acc = self.sb(st, "l_acc", [128, NT], F32)
            xcb = self.sb(st, "l_xcb", [128, NT], BF16)
            rmask = self.sb(st, "l_rm", [128, NT], F32)
            a_t = self.sb(st, "l_a", [128, NT], F32)
            bt = self.sb(st, "l_bt", [128, NT], F32)
            hf = self.sb(st, "l_hf", [128, NT], F32)
            hb = self.sb(st, "l_hb", [128, NT], F32)
            rfull = self.sb(st, "l_rf", [128, NT], F32)
            ifull = self.sb(st, "l_if", [128, NT], F32)
            w4 = [self.sb(st, f"l_w{i}", [128, 4, 128], BF16) for i in range(2)]
            prm = self.sb(st, "l_prm", [128, 8, 11], F32)
            sc = self.sb(st, "l_sc", [128, 8, 4], F32)
            ini = self.sb(st, "l_ini", [128, 2], F32)
            nlg = self.n_lru_gate_blocks()
            if nlg > 0:
                ps7 = st.enter_context(self.nc.psum_tensor(f"ps7l_{l}", [128, 512], F32))
                ggen = self.gate_gen(l, st, [(ps7, ("ps", 7)), (self.ps[6], ("ps", 6))], self.gate_blocks()[:nlg])
            else:
                ggen = iter(())
            g_per_hook = (nlg * 32 + 15) // 16
            hook_every = max(1, nch // max(1, g_per_hook))
            hooks_left = [g_per_hook]
            S.dma("sp", rmask[:], I["rmask"], (), ["l_rm"])
            S.dma("sp", prm[:], I["lrup"][:, l], (), ["l_prm"])
            self.act(sc[:, :, 0:2], prm[:, :, 9:11], AF.Exp, ["l_prm"], ["l_sc"], scale=-1.0)
            self.act(sc[:, :, 0:2], sc[:, :, 0:2], AF.Ln, ["l_sc"], ["l_sc"], bias=1.0)
            self.ts("dve", sc[:, :, 2:4], sc[:, :, 0:2], -16.0, None, ALU.mult, None, ["l_sc"], ["l_sc2"])
            self.ts("dve", sc[:, :, 0:2], sc[:, :, 0:2], -8.0, None, ALU.mult, None, ["l_sc", "l_sc2"], ["l_sc"])
            self.memset("dve", xr[:, 0:2], 0.0, ["l_xr"])
            self.memset("dve", xr[:, NT + 2:NT + 4], 0.0, ["l_xr"])
            pb = 0
            ri = 0
            for n in range(8):
                w = w4[n % 2]
                wk = ("l_w", n % 2)
                S.dma("pool", w[:], I["lru_w"][l, n].rearrange("f c d -> c f d"), (), [wk])
                S.dma("sp", xr[:, 2:2 + NT], X["xS"][n * 128:(n + 1) * 128, :], ["xS_all"], ["l_xr"])
                self.ts("dve", acc[:], xr[:, 0:NT], prm[:, n, 0:1], prm[:, n, 4:5], ALU.mult, ALU.add, ["l_xr", "l_prm"], ["l_acc"])
                for j in range(1, 4):
                    self.stt("dve", acc[:], xr[:, j:j + NT], prm[:, n, j:j + 1], acc[:], ALU.mult, ALU.add,
                             ["l_xr", "l_prm", "l_acc"], ["l_acc"])
                self.tt("dve", acc[:], acc[:], rmask[:], ALU.mult, ["l_acc", "l_rm"], ["l_acc"])
                self.cp("act", xcb[:], acc[:], ["l_acc"], ["l_xcb"])
                for d in range(2):
                    for c in range(nch):
                        cs = slice(c * SBK, (c + 1) * SBK)
                        b0 = pb % 6
                        b1 = (pb + 1) % 6
                        pb += 2
                        self.mm(self.ps[b0][:, 0:SBK], w[:, 2 * d, :], xcb[:, cs], True, True, [wk, "l_xcb"], [("ps", b0)])
                        self.mm(self.ps[b1][:, 0:SBK], w[:, 2 * d + 1, :], xcb[:, cs], True, True, [wk, "l_xcb"], [("ps", b1)])
                        self.act(rfull[:, cs], self.ps[b0][:, 0:SBK], AF.Sigmoid, [("ps", b0), "l_prm"], [("l_rf", c)], bias=prm[:, n, 5 + d:6 + d])
                        self.act(ifull[:, cs], self.ps[b1][:, 0:SBK], AF.Sigmoid, [("ps", b1), "l_prm"], [("l_if", c)], bias=prm[:, n, 7 + d:8 + d])
                        if (c + 1) % hook_every == 0 and hooks_left[0] > 0:
                            hooks_left[0] -= 1
                            next(ggen, None)
                    rk = [("l_rf", c) for c in range(nch)]
                    ik = [("l_if", c) for c in range(nch)]
                    self.act(a_t[:], rfull[:], AF.Exp, rk + ["l_sc"], ["l_a"], scale=sc[:, n, d:d + 1])
                    self.act(rfull[:], rfull[:], AF.Exp, rk + ["l_sc2"], ["l_rf"] + rk, scale=sc[:, n, 2 + d:3 + d])
                    self.act(rfull[:], rfull[:], AF.Sqrt, ["l_rf"], ["l_rf"], bias=1.0, scale=-1.0)
                    self.tt("dve", bt[:], ifull[:], acc[:], ALU.mult, ik + ["l_acc"], ["l_bt"])
                    self.tt("dve", bt[:], bt[:], rfull[:], ALU.mult, ["l_rf", "l_bt"], ["l_bt"])
                    for c in range(nch):
                        self.S.bufs[("l_rf", c)] = self.S.bufs["l_rf"]
                    hooks_left[0] = g_per_hook
                    akeys = ["l_a", "l_bt"]
                    H = HALF
                    if d == 0:
                        self.scan(hf[:, 0:H], a_t[:, 0:H], bt[:, 0:H], 0.0, akeys, ["l_hf"])
                        self.tt("dve", ini[:, 0:1], hf[:, H - 1:H], self.carry[:], ALU.mult, ["l_hf", "carry"], ["l_ini0"])
                        self.scan(hf[:, H:NT], a_t[:, H:NT], bt[:, H:NT], ini[:, 0:1], akeys + ["l_ini0", "l_hf"], ["l_hf"])
                    else:
                        self.scan(hb[:, H:NT][:, ::-1], a_t[:, H:NT][:, ::-1], bt[:, H:NT][:, ::-1], 0.0, akeys, ["l_hb"])
                        self.tt("dve", ini[:, 1:2], hb[:, H:H + 1], self.carry[:], ALU.mult, ["l_hb", "carry"], ["l_ini1"])
                        self.scan(hb[:, 0:H][:, ::-1], a_t[:, 0:H][:, ::-1], bt[:, 0:H][:, ::-1], ini[:, 1:2],
                                  akeys + ["l_ini1", "l_hb"], ["l_hb"])
                self.tt("dve", xcb[:], hf[:], hb[:], ALU.add, ["l_hf", "l_hb"], ["l_xcb"])
                S.dma("sp", X["ycS"][n * 128:(n + 1) * 128, :], xcb[:], ["l_xcb"], ["ycS_all"])
            for _ in ggen:
                pass

    def phase_gla(self, l):
        S, I, X, NT, HALF, SBK, NCH = self.S, self.I, self.X, self.NT, self.HALF, self.SBK, self.NCH
        nch = NT // SBK
        HC = NCH // 2
        cst = self.cst
        with ExitStack() as st:
            self.psb = st.enter_context(self.nc.psum_tensor(f"psb_{l}", [128, 1024], BF16))
            vt = self.sb(st, "g_v", [128, NCH, 512], BF16)
            glr = self.sb(st, "g_glr", [32, NT], BF16)
            wg = self.sb(st, "g_wg", [32, 2, 256], BF16)
            gp = self.sb(st, "g_gp", [128, 8], F32)
            nbg = self.sb(st, "g_nbg", [128, 4], F32)
            km = self.sb(st, "g_km", [128, NT + 1], F32)
            q = self.sb(st, "g_q", [128, NT], BF16)
            k = self.sb(st, "g_k", [128, NT], BF16)
            bb = self.sb(st, "g_b", [128, NT], F32)
            ex = self.sb(st, "g_ex", [128, NT], F32)
            qe = [self.sb(st, f"g_qe{d}", [128, NT], BF16) for d in range(2)]
            ke = [self.sb(st, f"g_ke{d}", [128, NT], BF16) for d in range(2)]
            kd = self.sb(st, "g_kd", [128, NT], BF16)
            nbl = self.sb(st, "g_nbl", [128, NCH], F32)
            edec = self.sb(st, "g_edec", [128, NCH], F32)
            sbf = [self.sb(st, f"g_sbf{d}", [128, NCH, 128], BF16) for d in range(2)]
            scur = [self.sb(st, f"g_sc{i}", [128, 128], F32) for i in range(2)]
            kdt = [self.sb(st, f"g_kdt{i}", [128, 128], BF16) for i in range(2)]
            attm = [self.sb(st, f"g_att{i}", [128, 128], BF16) for i in range(4)]
            osb = [self.sb(st, f"g_o{i}", [128, 128], F32) for i in range(2)]
            osq = [self.sb(st, f"g_osq{i}", [128, 128], F32) for i in range(2)]
            ors = [self.sb(st, f"g_ors{i}", [128, 128], F32) for i in range(2)]
            yo = [self.sb(st, "g_yo0", [128, NT], BF16)] * 2
            dsall = self.sb(st, "g_ds", [128, NCH, 128], F32)
            psbs = [slice(0, 128), slice(512, 640)]
            S.dma("sp", vt[:], X["vS"].rearrange("(c p) v -> p c v", p=128), ["vS_all"], ["g_v"])
            S.dma("sp", glr[:], X["gS"], ["gS_all"], ["g_glr"])
            S.dma("pool", wg[:], I["wgp"][l].rearrange("d k n -> k d n"), (), ["g_wg"])
            S.dma("sp", gp[:], I["glap"][:, l], (), ["g_gp"])
            S.dma("sp", km[:], I["kmask"], (), ["g_km"])
            self.ts("dve", nbg[:], gp[:, 0:4], -1.0, None, ALU.mult, None, ["g_gp"], ["g_nbg"])
            pb = 0
            ai = 0
            oi = 0
            ki = 0
            for r in range(2):
                S.dma("sp", q[:], X["qS"][r * 128:(r + 1) * 128, :], ["qS_all"], ["g_q"])
                S.dma("sp", k[:], X["kS"][r * 128:(r + 1) * 128, :], ["kS_all"], ["g_k"])
                for d in range(2):
                    for c in range(nch):
                        cs = slice(c * SBK, (c + 1) * SBK)
                        bank = pb % 6
                        pb += 1
                        self.mm(self.ps[bank][:, 0:SBK], wg[:, d, r * 128:(r + 1) * 128], glr[:, cs], True, True,
                                ["g_wg", "g_glr"], [("ps", bank)])
                        self.act(ex[:, cs], self.ps[bank][:, 0:SBK], AF.Exp, [("ps", bank), "g_nbg"], [("g_ex", c)],
                                 bias=nbg[:, 2 * d + r:2 * d + r + 1], scale=-1.0)
                    exk = [("g_ex", c) for c in range(nch)]
                    self.act(ex[:], ex[:], AF.Ln, exk, ["g_ex"], bias=1.0)
                    if d == 0:
                        self.scan(bb[:], km[:, 0:NT], ex[:], 0.0, ["g_km", "g_ex"] + exk, ["g_b"])
                        self.cp("dve", nbl[:], bb[:, 127:NT:128], ["g_b"], ["g_nbl"])
                    else:
                        self.scan(bb[:, ::-1], km[:, 1:NT + 1][:, ::-1], ex[:, ::-1], 0.0, ["g_km", "g_ex"] + exk, ["g_b"])
                        self.cp("dve", nbl[:], bb[:, 0:NT:128], ["g_b"], ["g_nbl"])
                    self.ts("dve", nbl[:], nbl[:], -1.0 / 16.0, None, ALU.mult, None, ["g_nbl"], ["g_nbl"])
                    self.act(edec[:], nbl[:], AF.Exp, ["g_nbl"], ["g_edec"])
                    self.act(ex[:], bb[:], AF.Exp, ["g_b", "g_ex"], ["g_ex"], bias=math.log(0.125), scale=-1.0 / 16.0)
                    self.tt("dve", qe[d][:], q[:], ex[:], ALU.mult, ["g_q", "g_ex"], [("g_qe", d)])
                    self.act(ex[:], bb[:], AF.Exp, ["g_b", "g_ex"], ["g_ex"], scale=1.0 / 16.0)
                    self.tt("dve", ke[d][:], k[:], ex[:], ALU.mult, ["g_k", "g_ex"], [("g_ke", d)])
                    for c in range(NCH):
                        cs = slice(c * 128, (c + 1) * 128)
                        self.act(ex[:, cs], bb[:, cs], AF.Exp, ["g_b", "g_nbl", "g_ex"], ["g_ex"], bias=nbl[:, c:c + 1], scale=1.0 / 16.0)
                    self.tt("dve", kd[:], k[:], ex[:], ALU.mult, ["g_k", "g_ex"], ["g_kd"])
                    for c in range(NCH):
                        cs = slice(c * 128, (c + 1) * 128)
                        pbk = psbs[ki % 2]
                        self.tr(self.psb[:, pbk], kd[:, cs], self.identb[:], ["g_kd", "identb"], [("psb", ki % 2)])
                        kt = kdt[ki % 2]
                        kk = ("g_kdt", ki % 2)
                        self.cp("act", kt[:], self.psb[:, pbk], [("psb", ki % 2)], [kk])
                        ki += 1
                        bank = pb % 6
                        pb += 1
                        self.mm(self.ps[bank][:, 0:256], kt[:], vt[:, c, r * 256:(r + 1) * 256], True, True, [kk, "g_v"], [("ps", bank)])
                        for hp in range(2):
                            self.cp("act" if hp == 0 else "dve", dsall[hp * 64:(hp + 1) * 64, c, :],
                                    self.ps[bank][hp * 64:(hp + 1) * 64, hp * 128:(hp + 1) * 128], [("ps", bank)], [("g_ds", c)])
                    cur = 0
                    self.memset("dve", scur[0][:], 0.0, [("g_sc", 0)])
                    order = range(NCH) if d == 0 else range(NCH - 1, -1, -1)
                    for c in order:
                        if (d == 0 and c == HC) or (d == 1 and c == HC - 1):
                            self.ts("dve", scur[cur][:], scur[cur][:], self.carry[:, 0:1], None, ALU.mult, None,
                                    [("g_sc", cur), "carry"], [("g_sc", cur)])
                        self.cp("dve", sbf[d][:, c, :], scur[cur][:], [("g_sc", cur)], [("g_sbf", d)])
                        nxt = 1 - cur
                        self.stt("dve", scur[nxt][:], scur[cur][:], edec[:, c:c + 1], dsall[:, c, :],
                                 ALU.mult, ALU.add, [("g_sc", cur), "g_edec", ("g_ds", c)], [("g_sc", nxt)])
                        cur = nxt
                for hp in range(2):
                    h = 2 * r + hp
                    ps_ = slice(hp * 64, (hp + 1) * 64)
                    y = yo[hp]
                    yk = ("g_yo", 0)
                    for c in range(NCH):
                        cs = slice(c * 128, (c + 1) * 128)
                        ats = []
                        for d in range(2):
                            bank = pb % 6
                            pb += 1
                            self.mm(self.ps[bank][:, 0:128], ke[d][ps_, cs], qe[d][ps_, cs], True, True,
                                    [("g_ke", d), ("g_qe", d)], [("ps", bank)])
                            at = attm[ai % 4]
                            ak = ("g_att", ai % 4)
                            ai += 1
                            self.tt("dve", at[:], self.ps[bank][:, 0:128], cst[:, 1 + d, :], ALU.mult, [("ps", bank), "cst"], [ak])
                            ats.append((at, ak))
                        bank = pb % 6
                        pb += 1
                        ob = self.ps[bank][:, 0:128]
                        self.mm(ob, vt[:, c, h * 128:(h + 1) * 128], ats[0][0][:], True, False, ["g_v", ats[0][1]], [("ps", bank)])
                        self.mm(ob, vt[:, c, h * 128:(h + 1) * 128], ats[1][0][:], False, False, ["g_v", ats[1][1]], [("ps", bank)])
                        self.mm(ob, sbf[0][ps_, c, :], qe[0][ps_, cs], False, False, [("g_sbf", 0), ("g_qe", 0)], [("ps", bank)])
                        self.mm(ob, sbf[1][ps_, c, :], qe[1][ps_, cs], False, True, [("g_sbf", 1), ("g_qe", 1)], [("ps", bank)])
                        o2 = osq[oi % 2]
                        o2k = ("g_osq", oi % 2)
                        oi += 1
                        self.cp("act", bb[:, cs], ob, [("ps", bank), "g_b"], [("g_of", c)])
                        self.act(o2[:], ob, AF.Square, [("ps", bank)], [o2k])
                        b2 = 6
                        self.mm(self.ps[b2][:, 0:128], self.ones[:], o2[:], True, True, ["ones", o2k], [("ps", b2)])
                        self.cp("dve", ex[:, cs], self.ps[b2][:, 0:128], [("ps", b2), "g_ex"], [("g_sf", c)])
                    ofk = [("g_of", c) for c in range(NCH)]
                    sfk = [("g_sf", c) for c in range(NCH)]
                    self.act(ex[:], ex[:], AF.Ln, sfk, ["g_ex"] + sfk, bias=EPS, scale=1.0 / 128.0)
                    self.act(ex[:], ex[:], AF.Exp, ["g_ex"], ["g_ex"] + sfk, scale=-0.5)
                    self.stt("dve", y[:], bb[:], gp[:, 4 + h:5 + h], ex[:], ALU.mult, ALU.mult, ofk + sfk + ["g_ex", "g_gp"], [yk])
                    self.S.bufs["g_b"] = [None, {"dve": self.S.latest["dve"]}]
                    self.S.bufs["g_ex"] = [None, {"dve": self.S.latest["dve"]}]
                    for c in range(NCH):
                        self.S.bufs[("g_ex", c)] = self.S.bufs["g_ex"]
                    S.dma("sp", X["ybS"][h * 128:(h + 1) * 128, :], y[:], [yk], ["ybS_all"])

    def phase_s5(self, l):
        S, I, X, NT, HALF, NCH = self.S, self.I, self.X, self.NT, self.HALF, self.NCH
        HC = NCH // 2
        cst = self.cst
        jidx = cst[:, 3, :]
        PI = math.pi
        with ExitStack() as st:
            cs_t = self.sb(st, "s_cs", [128, 2, 16, 128], F32)
            sn_t = self.sb(st, "s_sn", [128, 2, 16, 128], F32)
            rt = self.sb(st, "s_rt", [128, 2, 16, 128], F32)
            bbw = self.sb(st, "s_bbw", [128, 2, 2, 16, 128], BF16)
            cw = self.sb(st, "s_cw", [128, 2, 2, 16, 128], BF16)
            cwn = self.sb(st, "s_cwn", [128, 2, 16, 128], BF16)
            rho = self.sb(st, "s_rho", [128, 2, 16], F32)
            sp_ = self.sb(st, "s_sp", [128, 8], F32)
            wglu = self.sb(st, "s_wglu", [128, 4, 512], BF16)
            S.dma("sp", sp_[:], I["s5p"][:, l], (), ["s_sp"])
            S.dma("pool", wglu[:], I["w_glu"][l].rearrange("(kc p) n -> p kc n", p=128), (), ["s_wglu"])
            for d in range(2):
                for part in range(2):
                    S.dma("pool", cw[:, d, part], I["cexp"][l, d, part], (), ["s_cw"])
            for d in range(2):
                self.ts("dve", cw[:, d, 1], cw[:, d, 1], -1.0, None, ALU.mult, None, ["s_cw"], ["s_cw"])
                self.ts("dve", cwn[:, d], cw[:, d, 0], -1.0, None, ALU.mult, None, ["s_cw"], ["s_cw"])
            with ExitStack() as st2:
                prm = self.sb(st2, "s_prm", [128, 3, 2, 16], F32)
                dt = self.sb(st2, "s_dt", [128, 2, 16], F32)
                th = self.sb(st2, "s_th", [128, 2, 16], F32)
                ang = self.sb(st2, "s_ang", [128, 2, 16, 128], F32)
                tmp = self.sb(st2, "s_tmp", [128, 2, 16, 128], F32)
                S.dma("sp", prm[:], I["s5s"][:, l], (), ["s_prm"])
                self.act(dt[:], prm[:, 2], AF.Exp, ["s_prm"], ["s_dt"])
                self.tt("dve", th[:], prm[:, 1], dt[:], ALU.mult, ["s_prm", "s_dt"], ["s_th"])
                self.tt("dve", dt[:], prm[:, 0], dt[:], ALU.mult, ["s_prm", "s_dt"], ["s_dt"])
                if l == 0:
                    self.dbg("prm", prm[:], [128, 3, 2, 16], F32, ["s_prm"])
                    self.dbg("th", th[:], [128, 2, 16], F32, ["s_th"])
                    self.dbg("dt2", dt[:], [128, 2, 16], F32, ["s_dt"])
                self.act(rho[:], dt[:], AF.Exp, ["s_dt"], ["s_rho"])
                if l == 0:
                    self.dbg("rho0", rho[:], [128, 2, 16], F32, ["s_rho"])
                for d in range(2):
                    for p in range(16):
                        self.ts("dve", ang[:, d, p, :], jidx, th[:, d, p:p + 1], None, ALU.mult, None, ["cst", "s_th"], ["s_ang"])
                        self.ts("dve", rt[:, d, p, :], cst[:, 4, :], rho[:, d, p:p + 1], None, ALU.mult, None, ["cst", "s_rho"], ["s_rt"])
                itmp = self.sb(st2, "s_itmp", [128, 2, 16, 128], mybir.dt.int32)
                self.sin_of(sn_t[:], ang[:], 0.0, tmp[:], itmp[:], ["s_ang"], ["s_sn"])
                self.sin_of(cs_t[:], ang[:], 0.5 * PI, tmp[:], itmp[:], ["s_ang"], ["s_cs"])
            S.barrier()
            with ExitStack() as st2:
                rw = self.sb(st2, "s_rw", [128, 3, 2048], F32)
                e1 = self.sb(st2, "s_e1", [128, 2048], F32)
                e2 = self.sb(st2, "s_e2", [128, 2048], F32)
                e3 = self.sb(st2, "s_e3", [128, 2048], F32)
                e4 = self.sb(st2, "s_e4", [128, 2048], F32)
                e5 = self.sb(st2, "s_e5", [128, 2048], F32)
                e6 = self.sb(st2, "s_e6", [128, 2048], F32)
                ei = self.sb(st2, "s_ei", [128, 2048], mybir.dt.int32)
                bx = self.sb(st2, "s_bx", [128, 2, 16, 128], F32)
                S.dma("sp", bx[:], I["bexp"][l].rearrange("a c p s -> c a p s"), (), ["s_bx"])
                for d in range(2):
                    for i3 in range(3):
                        S.dma("sp", rw[:, i3, :], I["s5r"][l, i3, d:d + 1, :].partition_broadcast(128), (), ["s_rw"])
                    lr, li = rw[:, 0, :], rw[:, 1, :]
                    R_, W_ = ["s_rw", "s_e"], ["s_e"]
                    self.act(e1[:], rw[:, 2, :], AF.Exp, R_, W_)
                    self.tt("dve", e2[:], li, e1[:], ALU.mult, R_, W_)
                    self.tt("dve", e1[:], lr, e1[:], ALU.mult, R_, W_)
                    self.act(e1[:], e1[:], AF.Exp, R_, W_)
                    self.sin_of(e3[:], e2[:], 0.0, e6[:], ei[:], R_, W_)
                    self.sin_of(e4[:], e2[:], 0.5 * PI, e6[:], ei[:], R_, W_)
                    self.tt("dve", e3[:], e3[:], e1[:], ALU.mult, R_, W_)
                    self.tt("dve", e4[:], e4[:], e1[:], ALU.mult, R_, W_)
                    self.ts("dve", e4[:], e4[:], -1.0, None, ALU.add, None, R_, W_)
                    self.tt("dve", e5[:], lr, lr, ALU.mult, R_, W_)
                    self.tt("dve", e6[:], li, li, ALU.mult, R_, W_)
                    self.tt("dve", e5[:], e5[:], e6[:], ALU.add, R_, W_)
                    self.S.op("dve", lambda e, o=e5[:]: e.reciprocal(out=o, in_=o), R_, W_)
                    self.tt("dve", e1[:], e4[:], lr, ALU.mult, R_, W_)
                    self.tt("dve", e6[:], e3[:], li, ALU.mult, R_, W_)
                    self.tt("dve", e1[:], e1[:], e6[:], ALU.add, R_, W_)
                    self.tt("dve", e1[:], e1[:], e5[:], ALU.mult, R_, W_)
                    self.tt("dve", e2[:], e3[:], lr, ALU.mult, R_, W_)
                    self.tt("dve", e6[:], e4[:], li, ALU.mult, R_, W_)
                    self.tt("dve", e2[:], e2[:], e6[:], ALU.subtract, R_, W_)
                    self.tt("dve", e2[:], e2[:], e5[:], ALU.mult, R_, W_)
                    fr = e1[:].rearrange("c (p s) -> c p s", p=16)
                    fi = e2[:].rearrange("c (p s) -> c p s", p=16)
                    t1 = e3[:].rearrange("c (p s) -> c p s", p=16)
                    t2 = e4[:].rearrange("c (p s) -> c p s", p=16)
                    RB = R_ + ["s_bx"]
                    self.tt("dve", t1, fr, bx[:, 0], ALU.mult, RB, W_)
                    self.tt("dve", t2, fi, bx[:, 1], ALU.mult, RB, W_)
                    self.tt("dve", bbw[:, d, 0], t1, t2, ALU.subtract, R_, ["s_bbw"])
                    self.tt("dve", t1, fr, bx[:, 1], ALU.mult, RB + ["s_bbw"], W_)
                    self.tt("dve", t2, fi, bx[:, 0], ALU.mult, RB, W_)
                    self.tt("dve", bbw[:, d, 1], t1, t2, ALU.add, R_, ["s_bbw"])
            S.barrier()
            if l == 0:
                self.dbg("cs", cs_t[:], [128, 2, 16, 128], F32, ["s_cs"])
                self.dbg("sn", sn_t[:], [128, 2, 16, 128], F32, ["s_sn"])
                self.dbg("rt", rt[:], [128, 2, 16, 128], F32, ["s_rt"])
                self.dbg("rho", rho[:], [128, 2, 16], F32, ["s_rho"])
                self.dbg("bbw", bbw[:], [128, 2, 2, 16, 128], BF16, ["s_bbw"])
                self.dbg("cw", cw[:], [128, 2, 2, 16, 128], BF16, ["s_cw"])
            with ExitStack() as st2:
                ufr = [self.sb(st2, f"s_uf{i}", [128, 4, 128], BF16) for i in range(3)]
                ps7 = st2.enter_context(self.nc.psum_tensor(f"ps7_{l}", [128, 512], F32))
                gblocks = self.gate_blocks()[self.n_lru_gate_blocks():]
                ggen = self.gate_gen(l, st2, [(ps7, ("ps", 7)), (self.ps[6], ("ps", 6))], gblocks)
                grate = len(gblocks) * 32.0 / (4 * NCH)
                gacc = 0.0
                bh2 = [[self.sb(st2, f"s_bh{j}{i}", [128, 8, 128], F32) for i in range(2)] for j in range(2)]
                t_ = [self.sb(st2, f"s_t{i}", [128, 8, 128], F32) for i in range(2)]
                bsb2 = [[self.sb(st2, f"s_bsb{j}{i}", [128, 8, 128], F32) for i in range(2)] for j in range(2)]
                pbf = [[self.sb(st2, f"s_pbf{i}{j}", [128, 8, 128], BF16) for j in range(4)] for i in range(1)]
                hprev = self.sb(st2, "s_hprev", [128, 2, 16], F32)
                hl = self.sb(st2, "s_hl", [128, 4, 8], F32)
                yf = [self.sb(st2, f"s_yf{i}", [128, 4, 128], F32) for i in range(2)]
                yy = self.sb(st2, "s_yy", [128, 4, 128], F32)
                zb = self.sb(st2, "s_zb", [128, 4, 128], BF16)
                sg = self.sb(st2, "s_sg", [128, 4, 128], F32)
                ya = [self.sb(st2, f"s_ya{i}", [128, 4, 128], BF16) for i in range(2)]
                hi_ = 0
                fcnt = 0
                pending = None
                pending2 = []
                steps = []
                for d in range(2):
                    order = range(NCH) if d == 0 else range(NCH - 1, -1, -1)
                    for f in order:
                        for hh in range(2):
                            steps.append((d, f, hh))

                fidx = {}
                def stageA(d, f, hh):
                    fs = slice(f * 128, (f + 1) * 128)
                    P0 = 8 * hh
                    if hh == 0:
                        fidx[(d, f)] = len(fidx) % 3
                        ui = fidx[(d, f)]
                        S.dma("sp", ufr[ui][:], X["uS"][:, fs].rearrange("(o p) t -> p o t", p=128), ["uS_all"], [("s_uf", ui)])
                    ui = fidx[(d, f)]
                    u_, uk = ufr[ui], ("s_uf", ui)
                    for pl in range(8):
                        p = P0 + pl
                        for part in range(2):
                            bank = 2 * part + pl // 4
                            urhs = u_[:, p // 4, :] if d == 0 else u_[:, p // 4, ::-1]
                            self.mm(self.ps[bank][:, (pl % 4) * 128:(pl % 4 + 1) * 128], bbw[:, d, part, p, :], urhs,
                                    True, True, ["s_bbw", uk], [("ps", bank)])
                    bsb = bsb2[hh]
                    for part in range(2):
                        for a2 in range(2):
                            bank = 2 * part + a2
                            self.cp("act", bsb[part][:, 4 * a2:4 * a2 + 4, :], self.ps[bank][:].rearrange("q (a t) -> q a t", a=4),
                                    [("ps", bank)], [f"s_bsb{hh}{part}"])

                stageA(*steps[0])
                for si, (d, f, hh) in enumerate(steps):
                    if si + 1 < len(steps):
                        stageA(*steps[si + 1])
                    gacc += grate
                    while gacc >= 1.0:
                        next(ggen, None)
                        gacc -= 1.0
                    fs = slice(f * 128, (f + 1) * 128)
                    P0 = 8 * hh
                    first_of_dir = (f == (0 if d == 0 else NCH - 1)) and hh == 0
                    if first_of_dir:
                        if pending is not None:
                            pending()
                            pending = None
                        self.memset("dve", hprev[:], 0.0, [("s_hprev", 0), ("s_hprev", 1)])
                    if hh == 0:
                        ybank = 4 + (fcnt % 2)
                        fcnt += 1
                        if (d == 0 and f == HC) or (d == 1 and f == HC - 1):
                            self.ts("dve", hprev[:], hprev[:], self.carry[:, 0:1], None, ALU.mult, None,
                                    [("s_hprev", 0), ("s_hprev", 1), "carry"], [("s_hprev", 0), ("s_hprev", 1)])
                        if d == 1:
                            yfl = yf[f % 2]
                            yfk = ("s_yf", f % 2)
                            S.dma("sp", yfl[:], X["yfS"][:, fs].rearrange("(o p) t -> p o t", p=128), ["yfS_all"], [yfk])
                    bsb = bsb2[hh]
                    bh = bh2[hh]
                    kb0, kb1 = f"s_bh{hh}0", f"s_bh{hh}1"
                    kbh = [kb0, kb1]
                    ct = cs_t[:, d, P0:P0 + 8, :]
                    stb = sn_t[:, d, P0:P0 + 8, :]
                    BR, BI = bsb[0][:], bsb[1][:]
                    self.tt("dve", t_[0][:], BR, ct, ALU.mult, [f"s_bsb{hh}0", "s_cs"], ["s_t0"])
                    self.tt("dve", t_[1][:], BI, stb, ALU.mult, [f"s_bsb{hh}1", "s_sn"], ["s_t1"])
                    self.tt("dve", bh[0][:], t_[0][:], t_[1][:], ALU.add, ["s_t0", "s_t1"], [kb0])
                    self.tt("dve", t_[0][:], BI, ct, ALU.mult, [f"s_bsb{hh}1", "s_cs", kb0], ["s_t0"])
                    self.tt("dve", t_[1][:], BR, stb, ALU.mult, [f"s_bsb{hh}0", "s_sn", kb0], ["s_t1"])
                    self.tt("dve", bh[1][:], t_[0][:], t_[1][:], ALU.subtract, ["s_t0", "s_t1"], [kb1])
                    edge = 0
                    for part in range(2):
                        self.tt("dve", bh[part][:, :, edge], bh[part][:, :, edge], hprev[:, part, P0:P0 + 8], ALU.add,
                                [kbh[part], ("s_hprev", hh)], [kbh[part]])
                    for part in range(2):
                        fl = bh[part][:].rearrange("q a t -> q (a t)")
                        rtf = rt[:, d, P0:P0 + 8, :].rearrange("q a t -> q (a t)")
                        self.scan(fl, rtf, fl, 0.0, [kbh[part], "s_rt"], [kbh[part]])
                    last = 127
                    tl = 127
                    gr, gi = bh[0][:, :, last], bh[1][:, :, last]
                    cl, sl_ = cs_t[:, d, P0:P0 + 8, tl], sn_t[:, d, P0:P0 + 8, tl]
                    RK = [kb0, kb1, "s_cs", "s_sn", "s_hl"]
                    E = "pool"
                    self.tt(E, hl[:, 0, :], gr, cl, ALU.mult, RK, ["s_hl"])
                    self.tt(E, hl[:, 1, :], gi, sl_, ALU.mult, RK, ["s_hl"])
                    self.tt(E, hl[:, 0, :], hl[:, 0, :], hl[:, 1, :], ALU.subtract, RK, ["s_hl"])
                    self.tt(E, hl[:, 2, :], gr, sl_, ALU.mult, RK, ["s_hl"])
                    self.tt(E, hl[:, 3, :], gi, cl, ALU.mult, RK, ["s_hl"])
                    self.tt(E, hl[:, 2, :], hl[:, 2, :], hl[:, 3, :], ALU.add, RK, ["s_hl"])
                    self.tt(E, hprev[:, 0, P0:P0 + 8], hl[:, 0, :], rho[:, d, P0:P0 + 8], ALU.mult, ["s_hl", "s_rho", ("s_hprev", hh)], [("s_hprev", hh)])
                    self.tt(E, hprev[:, 1, P0:P0 + 8], hl[:, 2, :], rho[:, d, P0:P0 + 8], ALU.mult, ["s_hl", "s_rho", ("s_hprev", hh)], [("s_hprev", hh)])
                    hb_ = pbf[0]
                    hk = ("s_pbf", 0)
                    hi_ += 1
                    GR, GI = bh[0][:], bh[1][:]
                    PE_ = self.S5_PROD_ENG
                    self.tt(PE_, hb_[0][:], GR, ct, ALU.mult, [kb0, "s_cs"], [hk])
                    self.tt(PE_, hb_[1][:], GI, stb, ALU.mult, [kb1, "s_sn"], [hk])
                    self.tt(PE_, hb_[2][:], GR, stb, ALU.mult, [kb0, "s_sn"], [hk])
                    self.tt("dve", hb_[3][:], GI, ct, ALU.mult, [kb1, "s_cs"], [(hk, 3)])
                    wsel = [cw[:, d, 0], cwn[:, d], cw[:, d, 1], cw[:, d, 1]]
                    for o2 in range(2):
                        oc = 2 * hh + o2
                        yo_ = self.ps[ybank][:, oc * 128:(oc + 1) * 128]
                        n_ = 0
                        for pl in range(4 * o2, 4 * o2 + 4):
                            p = P0 + pl
                            for k4 in range(4):
                                crhs = hb_[k4][:, pl, :] if d == 0 else hb_[k4][:, pl, ::-1]
                                self.mm(yo_, wsel[k4][:, p, :], crhs, n_ == 0, n_ == 15, ["s_cw", hk, (hk, 3)], [("ps", ybank)])
                                n_ += 1
                    if hh == 1 and pending2:
                        pending2.pop(0)()
                    if hh == 0 and pending is not None:
                        pending()
                        pending = None
                    if hh == 1:
                        def epilogue(f=f, fs=fs, d=d, ybank=ybank, yfl=(yf[f % 2]), yfk=("s_yf", f % 2), u_=ufr[fidx[(d, f)]], uk=("s_uf", fidx[(d, f)])):
                            y4 = self.ps[ybank][:].rearrange("q (o t) -> q o t", o=4)
                            if d == 0:
                                self.cp("act", yfl[:], y4, [("ps", ybank)], [yfk])
                                S.dma("sp", X["yfS"][:, fs].rearrange("(o p) t -> p o t", p=128), yfl[:], [yfk], ["yfS_all"])
                            else:
                                for oc in range(4):
                                    self.stt("dve", yy[:, oc, :], u_[:, oc, :], sp_[:, oc:oc + 1], self.ps[ybank][:, oc * 128:(oc + 1) * 128],
                                             ALU.mult, ALU.add, [uk, "s_sp", ("ps", ybank)], ["s_yy"])
                                self.tt("dve", yy[:], yy[:], yfl[:], ALU.add, ["s_yy", yfk], ["s_yy"])
                                self.act(yy[:], yy[:], AF.Gelu_apprx_tanh, ["s_yy"], ["s_yy"])
                                self.cp("act", zb[:], yy[:], ["s_yy"], ["s_zb"])
                                yal = ya[f % 2]
                                yak = ("s_ya", f % 2)
                                for oc in range(4):
                                    go = self.ps[6][:, oc * 128:(oc + 1) * 128]
                                    for kc in range(4):
                                        self.mm(go, wglu[:, kc, oc * 128:(oc + 1) * 128], zb[:, kc, :], kc == 0, kc == 3, ["s_wglu", "s_zb"], [("ps", 6)])
                                    self.act(sg[:, oc, :], go, AF.Sigmoid, [("ps", 6), "s_sp"], ["s_sg"], bias=sp_[:, 4 + oc:5 + oc])
                                def part2(yal=yal, yak=yak, fs=fs):
                                    self.tt("dve", yal[:], yy[:], sg[:], ALU.mult, ["s_yy", "s_sg"], [yak])
                                    S.dma("sp", X["yaS"][:, fs].rearrange("(o p) t -> p o t", p=128), yal[:], [yak], ["yaS_all"])
                                pending2.append(part2)
                        pending = epilogue
                if pending is not None:
                    pending()
                    pending = None
                while pending2:
                    pending2.pop(0)()
                for _ in ggen:
                    pass


    def n_lru_gate_blocks(self):
        nb = len(self.gate_blocks())
        return min(3, nb - 1)

    def gate_blocks(self):
        return [(o, min(512, self.NT - o)) for o in range(0, self.NT, 512)]

    def gate_gen(self, l, st, banks, blocks):
        S, I, X, NT = self.S, self.I, self.X, self.NT
        w_in = I["w_in"]
        hTg = self.sb(st, "gg_h", [128, KC, 512], BF16)
        ring = [self.sb(st, f"gg_w{i}", [128, KC, 256], BF16) for i in range(3)]
        gst = [self.sb(st, f"gg_s{i}", [128, 512], BF16) for i in range(3)]
        tiles = []
        for (c0, r0, nrow) in [(C_GA, 0, 4), (C_GB, 4, 4), (C_GC, 8, 8)]:
            for g2 in range(nrow // 2):
                tiles.append((c0 + g2 * 256, "sgS", r0 + g2 * 2, AF.Silu))
        for br in range(3):
            for g2 in range(8):
                tiles.append((C_M + br * D + g2 * 256, "mgS", br * 16 + g2 * 2, AF.Sigmoid))
        wi = 0
        gi = 0
        pb = 0
        for (t0, BG) in blocks:
            S.dma("sp", hTg[:, :, 0:BG], X["hS"][:, t0:t0 + BG].rearrange("(kc p) t -> p kc t", p=128), ["hS_all"], ["gg_h"])
            for (c0, dst, row0, fn) in tiles:
                wt = ring[wi % 3]
                wk = ("gg_w", wi % 3)
                wi += 1
                S.dma("pool", wt[:], w_in[l, :, c0:c0 + 256].rearrange("(kc p) n -> p kc n", p=128), (), [wk])
                for mi in range(2):
                    g = gst[gi % 3]
                    gk = ("gg_s", gi % 3)
                    gi += 1
                    bank, bkey = banks[pb % len(banks)]
                    pb += 1
                    for kc in range(KC):
                        self.mm(bank[:, 0:BG], wt[:, kc, mi * 128:(mi + 1) * 128], hTg[:, kc, 0:BG],
                                kc == 0, kc == KC - 1, [wk, "gg_h"], [bkey])
                    self.act(g[:, 0:BG], bank[:, 0:BG], fn, [bkey], [gk])
                    row = row0 + mi
                    S.dma("sp", X[dst][row * 128:(row + 1) * 128, t0:t0 + BG], g[:, 0:BG], [gk], [(dst, "all")])
                yield

    def phase3(self, l, zin, last):
        S, I, X, NB, SBK, NT, L = self.S, self.I, self.X, self.NB, self.SBK, self.NT, self.L
        with ExitStack() as st:
            sgt = self.sb(st, "p3_sg", [128, KC, NB], BF16)
            yg = self.sb(st, "p3_y", [128, KC, NB], BF16)
            m = self.sb(st, "p3_m", [128, KC, NB], BF16)
            MW = max(w_ for _, w_ in self.MSUB)
            mgt = [self.sb(st, f"p3_mg{i}", [128, 6, NB], BF16) for i in range(2)]
            tmp = [self.sb(st, f"p3_t{i}", [128, MW], F32) for i in range(2)]
            macc = [self.sb(st, f"p3_ma{i}", [128, MW], F32) for i in range(2)]
            zt = [self.sb(st, f"p3_z{i}", [128, NB], F32) for i in range(2)]
            self.wring_init(st, nbuf=6)
            pb = 0
            ti = 0
            zi = 0
            mgi = 0
            for b in range(NT // NB):
                t0 = b * NB
                for (ra, rb) in [(0, 4), (4, 8), (8, 16)]:
                    S.dma("sp", sgt[:, ra:rb, :], X["sgS"][ra * 128:rb * 128, t0:t0 + NB].rearrange("(kc p) t -> p kc t", p=128),
                          ["sgS_all"], [("p3_sg", ra)])
                S.dma("sp", yg[:, 0:4, :], X["yaS"][:, t0:t0 + NB].rearrange("(kc p) t -> p kc t", p=128), ["yaS_all"], [("p3_y", r) for r in range(0, 4)])
                S.dma("sp", yg[:, 4:8, :], X["ybS"][:, t0:t0 + NB].rearrange("(kc p) t -> p kc t", p=128), ["ybS_all"], [("p3_y", r) for r in range(4, 8)])
                S.dma("sp", yg[:, 8:16, :], X["ycS"][:, t0:t0 + NB].rearrange("(kc p) t -> p kc t", p=128), ["ycS_all"], [("p3_y", r) for r in range(8, 16)])
                for row in range(16):
                    self.tt("dve", yg[:, row, :], yg[:, row, :], sgt[:, row, :], ALU.mult,
                            [("p3_y", row), ("p3_sg", 0 if row < 4 else (4 if row < 8 else 8))], [("p3_y", row)])
                ykeys = [("p3_y", r) for r in range(16)]
                wouts = [(I["w_out_a"], 4, 0), (I["w_out_b"], 4, 4), (I["w_out_c"], 8, 8)]
                for g2 in range(8):
                    mg = mgt[mgi % 2]
                    mgk = ("p3_mg", mgi % 2)
                    mgi += 1
                    for br in range(3):
                        r_ = br * 16 + g2 * 2
                        S.dma("sp", mg[:, 2 * br:2 * br + 2, :], X["mgS"][r_ * 128:(r_ + 2) * 128, t0:t0 + NB].rearrange("(a p) t -> p a t", p=128),
                              ["mgS_all"], [mgk])
                    tiles = []
                    for br in range(3):
                        wo_src, nk, r0 = wouts[br]
                        wo, wok = self.wload(wo_src[l, :, g2 * 256:(g2 + 1) * 256], nk, 256)
                        tiles.append((wo, wok, nk, r0))
                    for mi in range(2):
                        row = g2 * 2 + mi
                        ms = slice(mi * 128, (mi + 1) * 128)
                        for (so, sw) in self.MSUB:
                            ss = slice(so, so + sw)
                            ma = macc[zi % 2]
                            mak = ("p3_ma", zi % 2)
                            zi += 1
                            for br in range(3):
                                wo, wok, nk, r0 = tiles[br]
                                g = mg[:, 2 * br + mi, ss]
                                bank2 = pb % 6
                                pb += 1
                                for kc in range(nk):
                                    self.mm(self.ps[bank2][:, 0:sw], wo[:, kc, ms], yg[:, r0 + kc, ss], kc == 0, kc == nk - 1,
                                            [wok] + ykeys[r0:r0 + nk], [("ps", bank2)])
                                if br == 0:
                                    self.tt("dve", ma[:, 0:sw], g, self.ps[bank2][:, 0:sw], ALU.mult, [mgk, ("ps", bank2)], [mak])
                                else:
                                    t = tmp[ti % 2]
                                    tk = ("p3_t", ti % 2)
                                    ti += 1
                                    self.tt("dve", t[:, 0:sw], g, self.ps[bank2][:, 0:sw], ALU.mult, [mgk, ("ps", bank2)], [tk])
                                    if br == 1:
                                        self.tt("dve", ma[:, 0:sw], ma[:, 0:sw], t[:, 0:sw], ALU.add, [mak, tk], [mak])
                                    else:
                                        self.tt("dve", m[:, row, ss], ma[:, 0:sw], t[:, 0:sw], ALU.add, [mak, tk], [("p3_m", row)])
                mkeys = [("p3_m", r) for r in range(16)]
                for g2 in range(8):
                    wt, wk = self.wload(I["w_o"][l, :, g2 * 256:(g2 + 1) * 256], KC, 256)
                    for mi in range(2):
                        row = g2 * 2 + mi
                        z = zt[row % 2]
                        zk = ("p3_z", row % 2)
                        S.dma("sp", z[:], zin[row * 128:(row + 1) * 128, t0:t0 + NB], [("zT", b)] if zin is X["zT"] else (), [zk])
                        for (so, sw) in self.MSUB:
                            ss = slice(so, so + sw)
                            bank = pb % 6
                            pb += 1
                            for kc in range(KC):
                                self.mm(self.ps[bank][:, 0:sw], wt[:, kc, mi * 128:(mi + 1) * 128], m[:, kc, ss], kc == 0, kc == KC - 1,
                                        [wk] + mkeys, [("ps", bank)])
                            self.tt("dve", z[:, ss], z[:, ss], self.ps[bank][:, 0:sw], ALU.add, [zk, ("ps", bank)], [zk])
                        S.dma("sp", X["zT"][row * 128:(row + 1) * 128, t0:t0 + NB], z[:], [zk], [("zT", b)])

    def phase4(self):
        S, X, SBK, NT, L = self.S, self.X, self.SBK, self.NT, self.L
        with ExitStack() as st:
            zf = [self.sb(st, f"p4_z{i}", [128, KC, SBK], F32) for i in range(2)]
            sq = self.sb(st, "sq", [128, KC, SBK], F32)
            rstd = self.sb(st, "rstd", [128, SBK], F32)
            for s in range(NT // SBK):
                c0 = s * SBK
                z = zf[s % 2]
                zk = ("p4_z", s % 2)
                S.dma("sp", z[:], X["zT"][:, c0:c0 + SBK].rearrange("(kc p) t -> p kc t", p=128), (), [zk])
                self.rms_rstd(z, zk, sq, rstd, SBK, 6, D)
                for kc in range(KC):
                    self.stt("dve", z[:, kc, :], z[:, kc, :], self.normg[:, L, kc:kc + 1], rstd[:], ALU.mult, ALU.mult,
                             [zk, "rstd", "normg"], [zk])
                S.dma("sp", self.yT[:, c0:c0 + SBK].rearrange("(kc p) t -> p kc t", p=128), z[:], [zk], [("yT", s)])


def _slot_layout(x_prompt, x_sample, meta):
    Bp, Lp, _ = x_prompt.shape
    Bs, Ls, _ = x_sample.shape
    assert Lp == 2 * Ls and Bp == 2 and Bs == 8
    HALF = Ls + 128
    NT = 2 * HALF
    slots = np.zeros((8, NT, D), np.float32)
    carry = np.zeros((8,), np.float32)
    real = np.zeros((8, NT), np.float32)
    where = []
    for i in range(Bp):
        c = i
        slots[c, 112:128] = meta
        slots[c, 128:128 + Lp] = x_prompt[i]
        real[c, 112:128 + Lp] = 1.0
        carry[c] = 1.0
        where.append(("p", i, c, 128))
    for i in range(Bs):
        c = 2 + i // 2
        o = (i % 2) * HALF
        slots[c, o + 112:o + 128] = meta
        slots[c, o + 128:o + 128 + Ls] = x_sample[i]
        real[c, o + 112:o + 128 + Ls] = 1.0
        where.append(("s", i, c, o + 128))
    return slots, carry, real, where, Ls, NT


def _prep_shared(inp, L, NT):
    f = lambda a: np.ascontiguousarray(np.asarray(a, dtype=np.float32))
    sh = {}
    for k in ["w_in", "w_out_a", "w_out_b", "w_out_c", "w_o"]:
        sh[k] = f(inp[k][:L])
    sh["w_glu"] = f(inp["s5_w_glu"][:L])
    lw = np.stack([inp["lru_w_a"][:L, 0], inp["lru_w_x"][:L, 0], inp["lru_w_a"][:L, 1], inp["lru_w_x"][:L, 1]], axis=2)
    sh["lru_w"] = f(lw)
    ng = np.concatenate([np.asarray(inp["norm_g"][:L]), np.asarray(inp["final_norm_g"])[None]], axis=0)
    sh["normg"] = f(ng.reshape(L + 1, 16, 128).transpose(2, 0, 1))
    cw = np.asarray(inp["conv_w"][:L]).reshape(L, 4, 8, 128).transpose(3, 0, 2, 1)
    cb = np.asarray(inp["conv_b"][:L]).reshape(L, 8, 128).transpose(2, 0, 1)[..., None]
    def dn(a):
        return np.asarray(a[:L]).reshape(L, 2, 8, 128).transpose(3, 0, 2, 1)
    sh["lrup"] = f(np.concatenate([cw, cb, dn(inp["lru_b_a"]), dn(inp["lru_b_x"]), dn(inp["lru_lam"])], axis=3))
    wgp = np.zeros((L, 2, 32, 256), np.float32)
    for d in range(2):
        wgp[:, d, d * 16:(d + 1) * 16, :] = np.asarray(inp["gla_w_gate_up"][:L, d])
    sh["wgp"] = wgp
    bg = np.asarray(inp["gla_b_gate"][:L]).reshape(L, 2, 2, 128).transpose(3, 0, 1, 2).reshape(128, L, 4)
    gng = np.asarray(inp["gla_norm_g"][:L]).reshape(L, 4, 128).transpose(2, 0, 1)
    sh["glap"] = f(np.concatenate([bg, gng], axis=2))
    def sp_layout(a):
        return np.asarray(a).reshape(L, 2, 16, 2, 64).transpose(3, 4, 0, 1, 2).reshape(128, L, 2, 16)
    lstep = np.broadcast_to(np.asarray(inp["s5_log_step"][:L])[..., None], (L, 2, 32, 64))
    sh["s5s"] = f(np.stack([sp_layout(inp["s5_lam_re"][:L]), sp_layout(inp["s5_lam_im"][:L]), sp_layout(lstep)], axis=2))
    sh["s5r"] = f(np.stack([np.asarray(inp["s5_lam_re"][:L]).reshape(L, 2, 2048), np.asarray(inp["s5_lam_im"][:L]).reshape(L, 2, 2048),
                            lstep.reshape(L, 2, 2048)], axis=1))
    bexp = np.zeros((L, 2, 128, 16, 128), np.float32)
    cexp = np.zeros((L, 2, 2, 128, 16, 128), np.float32)
    bre, bim = np.asarray(inp["s5_b_re"][:L]), np.asarray(inp["s5_b_im"][:L])
    cre, cim = np.asarray(inp["s5_c_re"][:L]), np.asarray(inp["s5_c_im"][:L])
    for g in range(32):
        p, g2, go = g // 2, g % 2, g % 8
        for part, src in enumerate([bre, bim]):
            bexp[:, part, 16 * go:16 * go + 16, p, g2 * 64:(g2 + 1) * 64] = src[:, g].transpose(0, 2, 1)
        for part, src in enumerate([cre, cim]):
            cexp[:, :, part, g2 * 64:(g2 + 1) * 64, p, 16 * go:16 * go + 16] = src[:, :, g].transpose(0, 1, 3, 2)
    sh["bexp"], sh["cexp"] = bexp, cexp
    dsk = np.asarray(inp["s5_d"][:L]).reshape(L, 4, 128).transpose(2, 0, 1)
    bgl = np.asarray(inp["s5_b_glu"][:L]).reshape(L, 4, 128).transpose(2, 0, 1)
    sh["s5p"] = f(np.concatenate([dsk, bgl], axis=2))
    consts = np.zeros((128, 6, 128), np.float32)
    consts[:, 0] = np.eye(128)
    jj, ii = np.meshgrid(np.arange(128), np.arange(128), indexing="ij")
    consts[:, 1] = (jj <= ii)
    consts[:, 2] = (jj >= ii)
    consts[:, 3] = np.arange(1, 129)[None, :]
    consts[:, 4] = 1.0
    consts[:, 4, 0] = 0.0
    consts[:, 5] = 1.0
    consts[:, 5, 127] = 0.0
    sh["consts"] = consts
    km = np.ones((128, NT + 1), np.float32)
    km[:, 0::128] = 0.0
    sh["kmask"] = km
    return sh


_PROG_CACHE = {}


def _get_prog(Ls, L, debug=False, NB=None, SBK=None):
    key = (Ls, L, debug, NB, SBK)
    if key not in _PROG_CACHE:
        NT = 2 * (Ls + 128)
        if NB is None:
            NB = NT // 4
            SBK = NB // 4
        _PROG_CACHE[key] = Prog(Ls, L, NB, SBK, debug)
    return _PROG_CACHE[key]


def run(inp, L=None, debug=False, NB=None, SBK=None):
    x_prompt = np.asarray(inp["x_prompt"], np.float32)
    x_sample = np.asarray(inp["x_sample"], np.float32)
    meta = np.asarray(inp["meta_tokens"], np.float32)
    if L is None:
        L = int(np.asarray(inp["w_in"]).shape[0])
    slots, carry, real, where, Ls, NT = _slot_layout(x_prompt, x_sample, meta)
    sh = _prep_shared(inp, L, NT)
    prog = _get_prog(Ls, L, debug, NB, SBK)
    in_maps = []
    for c in range(8):
        m = dict(sh)
        m["xT"] = np.ascontiguousarray(slots[c].T)
        m["carry"] = np.full((128, 1), carry[c], np.float32)
        m["rmask"] = np.ascontiguousarray(np.broadcast_to(real[c][None, :], (128, NT)))
        in_maps.append(m)
    res = run_bass_kernel_spmd(prog.nc, in_maps, core_ids=list(range(8)))
    yp = np.zeros_like(x_prompt)
    ys = np.zeros_like(x_sample)
    for kind, i, c, start in where:
        yT = res.results[c]["yT"]
        if kind == "p":
            yp[i] = yT[:, start:start + x_prompt.shape[1]].T
        else:
            ys[i] = yT[:, start:start + Ls].T
    return (yp, ys), res


def kernel(**inputs):
    (yp, ys), _ = run(inputs)
    return yp, ys
```
